# Optimizing a Trainium2 kernel written in Bass

```python
import jax, jax.numpy as jnp
from jax import lax
import numpy as np

D_MODEL = 1024
BATCH = 4
SEQ = 8192
DEPTH = 1

N_ATTN_HEADS = 8
HEAD_DIM = 64
ATTN_WIDTH = N_ATTN_HEADS * HEAD_DIM
CONV_GROUPS = 8
CONV_WIDTH = 512
CONV_K = 3
D_FF = 4 * D_MODEL
PLE_DIM = 256
Q_BLOCK = 128
EPS = 1e-6
SPLIT_SIZES = (ATTN_WIDTH, ATTN_WIDTH, ATTN_WIDTH, CONV_WIDTH, CONV_WIDTH, CONV_WIDTH, D_MODEL, D_MODEL)
SPLIT_POINTS = tuple(int(v) for v in np.cumsum(SPLIT_SIZES)[:-1])
D_IN = sum(SPLIT_SIZES)

kernel_name = 'hybrid_stickbreak_shortconv_block'


def rms_norm(x, g):
    xf = x.astype(jnp.float32)
    var = jnp.mean(xf * xf, axis=-1, keepdims=True)
    return (xf * lax.rsqrt(var + EPS) * g.astype(jnp.float32)).astype(x.dtype)


def stick_breaking_attention(q, k, v):
    b, h, s, dh = q.shape
    nblk = s // Q_BLOCK
    scale = dh ** -0.5
    kf = k.astype(jnp.float32)
    vf = v.astype(jnp.float32)
    q_blocks = q.reshape(b, h, nblk, Q_BLOCK, dh).transpose(2, 0, 1, 3, 4)
    key_pos = jnp.arange(s, dtype=jnp.int32)
    starts = jnp.arange(nblk, dtype=jnp.int32) * Q_BLOCK

    def block(args):
        qb, start = args
        z = jnp.einsum('bhqd,bhkd->bhqk', qb.astype(jnp.float32), kf) * scale
        q_pos = start + jnp.arange(Q_BLOCK, dtype=jnp.int32)
        causal = key_pos[None, :] < q_pos[:, None]
        log_beta = jax.nn.log_sigmoid(z)
        log_keep = jnp.where(causal, log_beta - z, 0.0)
        between = lax.cumsum(log_keep, axis=3, reverse=True) - log_keep
        w = jnp.where(causal, jnp.exp(log_beta + between), 0.0)
        return jnp.einsum('bhqk,bhkd->bhqd', w, vf)

    out = lax.map(block, (q_blocks, starts))
    return out.transpose(1, 2, 0, 3, 4).reshape(b, h, s, dh).astype(q.dtype)


def causal_depthwise_conv(u, w):
    c = u.shape[-1]
    return lax.conv_general_dilated(
        u, w[:, None, :].astype(u.dtype), window_strides=(1,),
        padding=((CONV_K - 1, 0),), dimension_numbers=('NWC', 'WIO', 'NWC'),
        feature_group_count=c)


def setup_inputs(seed: int = 0) -> dict:
    key = jax.random.key(seed)
    ks = jax.random.split(key, 20)
    f32 = jnp.float32

    def nrm(k, shape, fan_in):
        return jax.random.normal(k, shape, f32) * (fan_in ** -0.5)

    def gain(k, shape):
        return jnp.ones(shape, f32) + 0.05 * jax.random.normal(k, shape, f32)

    return {
        'x': jax.random.normal(ks[0], (BATCH, SEQ, D_MODEL), f32),
        'p': jax.random.normal(ks[1], (DEPTH, BATCH, SEQ, PLE_DIM), f32),
        'g_pre_mix': gain(ks[2], (DEPTH, D_MODEL)),
        'w_in': nrm(ks[3], (DEPTH, D_MODEL, D_IN), D_MODEL),
        'b_gate': 0.1 * jax.random.normal(ks[4], (DEPTH, 2 * D_MODEL), f32),
        'w_conv': nrm(ks[5], (DEPTH, CONV_K, CONV_WIDTH), CONV_K),
        'w_attn_out': nrm(ks[6], (DEPTH, ATTN_WIDTH, D_MODEL), ATTN_WIDTH),
        'w_conv_out': nrm(ks[7], (DEPTH, CONV_WIDTH, D_MODEL), CONV_WIDTH),
        'w_o': nrm(ks[8], (DEPTH, D_MODEL, D_MODEL), D_MODEL),
        'g_post_mix': gain(ks[9], (DEPTH, D_MODEL)),
        'g_pre_mlp': gain(ks[10], (DEPTH, D_MODEL)),
        'w_up': nrm(ks[11], (DEPTH, D_MODEL, D_FF), D_MODEL),
        'w_down': nrm(ks[12], (DEPTH, D_FF, D_MODEL), D_FF),
        'g_post_mlp': gain(ks[13], (DEPTH, D_MODEL)),
        'g_ple': gain(ks[14], (DEPTH, D_MODEL)),
        'w_ple_gate': nrm(ks[15], (DEPTH, D_MODEL, D_MODEL), D_MODEL),
        'w_ple_proj': nrm(ks[16], (DEPTH, PLE_DIM, D_MODEL), PLE_DIM),
    }


def reference(x, p, g_pre_mix, w_in, b_gate, w_conv, w_attn_out, w_conv_out, w_o,
              g_post_mix, g_pre_mlp, w_up, w_down, g_post_mlp, g_ple, w_ple_gate, w_ple_proj):
    bsz, seq, _ = x.shape
    for i in range(DEPTH):
        h = rms_norm(x, g_pre_mix[i])
        proj = h @ w_in[i]
        q, k, v, cb, cc, cu, ga, gc = jnp.split(proj, SPLIT_POINTS, axis=-1)

        def heads(t):
            return t.reshape(bsz, seq, N_ATTN_HEADS, HEAD_DIM).transpose(0, 2, 1, 3)

        o = stick_breaking_attention(heads(q), heads(k), heads(v))
        o = o.transpose(0, 2, 1, 3).reshape(bsz, seq, ATTN_WIDTH)
        y_attn = o @ w_attn_out[i]

        y_conv = (cb * causal_depthwise_conv(cc * cu, w_conv[i])) @ w_conv_out[i]

        gates = jax.nn.sigmoid(jnp.concatenate([ga, gc], axis=-1) + b_gate[i])
        gate_attn, gate_conv = jnp.split(gates, 2, axis=-1)
        mixed = (gate_attn * y_attn + gate_conv * y_conv) @ w_o[i]
        x = x + rms_norm(mixed, g_post_mix[i])

        h = rms_norm(x, g_pre_mlp[i])
        f = jnp.square(jax.nn.relu(h @ w_up[i])) @ w_down[i]
        x = x + rms_norm(f, g_post_mlp[i])

        ple_gate = jax.nn.sigmoid(rms_norm(x, g_ple[i]) @ w_ple_gate[i])
        x = x + ple_gate * (p[i] @ w_ple_proj[i])
    return x
```

```python
import contextlib
import numpy as np
import concourse.bass as bass
import concourse.mybir as mybir
from concourse.bass_utils import run_bass_kernel_spmd

F32 = mybir.dt.float32
BF16 = mybir.dt.bfloat16
AF = mybir.ActivationFunctionType
ALU = mybir.AluOpType
AX = mybir.AxisListType

DT_SIZE = {F32: 4, BF16: 2}


class Op:
    __slots__ = ("eng", "fn", "dma_key", "deps", "signal", "idx", "sigval", "name")

    def __init__(self, eng, fn, dma_key, name):
        self.eng = eng
        self.fn = fn
        self.dma_key = dma_key
        self.deps = []
        self.signal = False
        self.idx = -1
        self.sigval = 0
        self.name = name


def _footprint(ap):
    t = ap.tensor
    esz = DT_SIZE[t.dtype] if t.dtype in DT_SIZE else mybir.dt.size(t.dtype)
    shape = [int(s) for s in t.shape]
    rowlen = 1
    for s in shape[1:]:
        rowlen *= s
    pat = [[int(a), int(b)] for a, b in ap.ap]
    off = int(ap.offset)
    is_psum = "PSum" in type(t).__name__
    base = 0
    if not is_psum:
        base = int(t.manual_sbuf_range[0])
    pstride, pcount = pat[0]
    assert pstride == rowlen or pcount == 1, (pat, rowlen, t.name)
    p0 = off // rowlen + int(t.base_partition)
    p1 = p0 + pcount
    foff = off % rowlen
    dims = [(s, c) for s, c in pat[1:] if c > 1]
    if not dims:
        runs = [(foff, foff + 1)]
    else:
        dims_sorted = dims
        inner_s, inner_c = dims_sorted[-1]
        if inner_s == 1:
            run = inner_c
            outer = dims_sorted[:-1]
        else:
            run = 1
            outer = dims_sorted
        starts = [foff]
        nruns = 1
        for s, c in outer:
            nruns *= c
        if nruns > 64:
            ext = foff + sum((c - 1) * abs(s) for s, c in dims) + 1
            runs = [(foff, ext)]
        else:
            for s, c in outer:
                starts = [st + k * s for st in starts for k in range(c)]
            runs = sorted((st, st + run) for st in starts)
            merged = []
            for a, b in runs:
                if merged and a <= merged[-1][1]:
                    merged[-1] = (merged[-1][0], max(merged[-1][1], b))
                else:
                    merged.append((a, b))
            runs = merged
    runs = [(base + a * esz, base + b * esz) for a, b in runs]
    if is_psum:
        banks = sorted({b // 2048 for a, e in runs for b in range(a, e, 512)} | {(e - 1) // 2048 for a, e in runs})
        runs = [(bk * 2048, bk * 2048 + 2048) for bk in banks]
        return ("P", 0, 128, runs)
    return ("S", p0, p1, runs)


class Sched:
    COMPUTE = ("pe", "act", "dve", "pool")
    PAGE = 2048

    def __init__(self):
        self.ops = {e: [] for e in ("pe", "act", "dve", "pool", "sp")}
        self.recs = {}
        self.allrecs = []
        self.keyw = {}
        self.keyr = {}
        self.dma_count = {}

    def _pages(self, space, runs):
        pg = set()
        for a, b in runs:
            for p in range(a // self.PAGE, (b - 1) // self.PAGE + 1):
                pg.add((space, p))
        return pg

    @staticmethod
    def _overlap(r, space, p0, p1, runs):
        if r[1] != space or r[3] <= p0 or p1 <= r[2]:
            return False
        for a, b in runs:
            for c, d in r[4]:
                if a < d and c < b:
                    return True
        return False

    @staticmethod
    def _covers(runs_outer, runs_inner):
        for c, d in runs_inner:
            ok = False
            for a, b in runs_outer:
                if a <= c and d <= b:
                    ok = True
                    break
            if not ok:
                return False
        return True

    def _access(self, op, item, is_write):
        if isinstance(item, tuple) and not hasattr(item, "tensor"):
            key = item
            w = self.keyw.get(key)
            if w is not None:
                op.deps.append(w)
            if is_write:
                for r in self.keyr.get(key, ()):
                    op.deps.append(r)
                self.keyw[key] = op
                self.keyr[key] = []
            else:
                self.keyr.setdefault(key, []).append(op)
            return
        space, p0, p1, runs = _footprint(item)
        excl = is_write or space == "P"
        pages = self._pages(space, runs)
        seen = set()
        for pg in pages:
            lst = self.recs.get(pg)
            if not lst:
                continue
            keep = []
            for r in lst:
                if not r[6]:
                    continue
                keep.append(r)
                if id(r) in seen:
                    continue
                seen.add(id(r))
                if not self._overlap(r, space, p0, p1, runs):
                    continue
                if r[0] == "w" or excl:
                    op.deps.append(r[5])
                if excl and r[2] >= p0 and r[3] <= p1 and self._covers(runs, r[4]):
                    r[6] = False
                elif (not excl) and r[0] == "r" and r[5].eng == op.eng and r[5].dma_key is None \
                        and r[2] == p0 and r[3] == p1 and r[4] == runs:
                    r[6] = False
            self.recs[pg] = [r for r in keep if r[6]]
        rec = ["w" if excl else "r", space, p0, p1, runs, op, True]
        for pg in pages:
            self.recs.setdefault(pg, []).append(rec)

    def add(self, eng, fn, reads=(), writes=(), dma_key=None, name=""):
        op = Op(eng, fn, dma_key, name)
        for it in reads:
            self._access(op, it, False)
        for it in writes:
            self._access(op, it, True)
        op.idx = len(self.ops[eng])
        self.ops[eng].append(op)
        if dma_key is not None:
            c = self.dma_count.get(dma_key, 0) + 16
            self.dma_count[dma_key] = c
            op.sigval = c
        return op

    def barrier(self):
        lasts = []
        for e, lst in self.ops.items():
            if e == "sp":
                continue
            real = [o for o in lst if o.fn is not None and o.dma_key is None]
            if real:
                lasts.append(real[-1])
        dm = {}
        for e, lst in self.ops.items():
            for o in lst:
                if o.dma_key is not None:
                    dm[o.dma_key] = o
        for e in self.ops:
            op = Op(e, None, None, "barrier")
            op.deps = list(lasts) + list(dm.values())
            op.idx = len(self.ops[e])
            self.ops[e].append(op)
        self.recs = {}
        self.keyw = {}
        self.keyr = {}

    def finalize(self):
        for e, lst in self.ops.items():
            for op in lst:
                nd = []
                for d in op.deps:
                    if d is op:
                        continue
                    if d.dma_key is None and d.eng == op.eng:
                        if e == "pe" or e == "sp":
                            continue
                    nd.append(d)
                op.deps = nd
                for d in nd:
                    d.signal = True
        self.dma_sems = {}
        for e, lst in self.ops.items():
            cnt = 0
            for op in lst:
                if op.fn is None:
                    continue
                if op.dma_key is not None:
                    pass
                elif op.signal:
                    cnt += 1
                    op.sigval = cnt

    def emit(self, nc, stack):
        self.finalize()
        sems = {}
        for e in self.COMPUTE:
            sems[e] = stack.enter_context(nc.semaphore("s_" + e))
        for n_, k in enumerate(self.dma_count):
            sems[("dma", k)] = stack.enter_context(nc.semaphore("d_%d" % n_))
        block = stack.enter_context(nc.Block())
        sched = self

        def run(engname, eng):
            seen = {}
            for op in sched.ops[engname]:
                need = {}
                for d in op.deps:
                    s = sems[("dma", d.dma_key)] if d.dma_key is not None else sems[d.eng]
                    sk = id(s)
                    v = d.sigval
                    if seen.get(sk, 0) >= v:
                        continue
                    if need.get(sk, (None, 0))[1] < v:
                        need[sk] = (s, v)
                if op.fn is not None and op.dma_key is not None and op.sigval > 16:
                    s = sems[("dma", op.dma_key)]
                    if seen.get(id(s), 0) < op.sigval - 16 and need.get(id(s), (None, 0))[1] < op.sigval - 16:
                        need[id(s)] = (s, op.sigval - 16)
                for sk, (s, v) in need.items():
                    eng.wait_ge(s, v)
                    seen[sk] = v
                if op.fn is None:
                    continue
                ins = op.fn(eng)
                if op.dma_key is not None:
                    ins.then_inc(sems[("dma", op.dma_key)], 16)
                elif op.signal:
                    ins.then_inc(sems[engname], 1)

        @block.tensor
        def _(eng):
            run("pe", eng)

        @block.scalar
        def _(eng):
            run("act", eng)

        @block.vector
        def _(eng):
            run("dve", eng)

        @block.gpsimd
        def _(eng):
            run("pool", eng)

        @block.sync
        def _(eng):
            run("sp", eng)


class Alloc:
    def __init__(self, nc, base=17408, limit=228352):
        self.nc = nc
        self.off = base
        self.limit = limit
        self.n = 0

    def __call__(self, name, shape, dtype):
        size = DT_SIZE[dtype]
        for s in shape[1:]:
            size *= s
        size = (size + 63) // 64 * 64
        assert self.off + size <= self.limit, (name, self.off, size)
        t = self.nc.alloc_sbuf_tensor_at(f"{name}_{self.n}", list(shape), dtype, offset=self.off)
        self.n += 1
        self.off += size
        return t

    def alias(self, name, shape, dtype, base):
        size = DT_SIZE[dtype]
        for v in shape[1:]:
            size *= v
        lo, hi = base.manual_sbuf_range
        assert size <= hi - lo, (name, size, hi - lo)
        t = self.nc.alloc_sbuf_tensor_at(f"{name}_{self.n}", list(shape), dtype, offset=int(lo))
        self.n += 1
        return t

    def mark(self):
        return self.off

    def release(self, m):
        self.off = m


D = 1024
DIN = 5120
DFF = 4096
PLE = 256
EPS = 1e-6
NEG = -30000.0


class Builder:
    def __init__(self, S, debug=False):
        self.S = S
        self.NCH = S // 512
        self.NSLOT = self.NCH // 2
        self.NKB = S // 128
        self.TOWN = self.NSLOT * 512
        self.debug = debug
        self.nc = bass.Bass("TRN2", target_bir_lowering=False)
        self.s = Sched()
        self.A = Alloc(self.nc)
        self.rr = 0

    def mm(self, out, lhsT, rhs, start, stop, skip=False):
        self.s.add("pe", lambda e, o=out, l=lhsT, r=rhs, a=start, b=stop, k=skip:
                   e.matmul(o, l, r, start=a, stop=b, skip_group_check=k),
                   reads=[lhsT, rhs], writes=[out], name="mm")

    def tr(self, out, in_, ident):
        self.s.add("pe", lambda e, o=out, i=in_, d=ident: e.transpose(o, i, d),
                   reads=[in_, ident], writes=[out], name="tr")

    def act(self, out, in_, func, bias=None, scale=None, accum=None, extra_reads=()):
        kw = {}
        rd = [in_] + list(extra_reads)
        wr = [out]
        if bias is not None:
            kw["bias"] = bias
            if not isinstance(bias, float):
                rd.append(bias)
        if scale is not None:
            kw["scale"] = scale
            if not isinstance(scale, float):
                rd.append(scale)
        if accum is not None:
            kw["accum_out"] = accum
            wr.append(accum)
        self.s.add("act", lambda e, o=out, i=in_, f=func, k=kw: e.activation(o, i, f, **k),
                   reads=rd, writes=wr, name="act")

    def vop(self, eng, method, out, ins, *args, extra_writes=(), **kw):
        rd = [a for a in ins if hasattr(a, "tensor")]
        rd += [a for a in args if hasattr(a, "tensor")]
        self.s.add(eng, lambda e, m=method, o=out, i=tuple(ins), a=args, k=kw: getattr(e, m)(o, *i, *a, **k),
                   reads=rd, writes=[out] + list(extra_writes), name=method)

    def dma(self, q, out, in_, key, reads=(), writes=()):
        rd = list(reads)
        wr = list(writes)
        if "DRam" not in type(in_.tensor).__name__:
            rd.append(in_)
        if "DRam" not in type(out.tensor).__name__:
            wr.append(out)
        self.s.add(q, lambda e, o=out, i=in_: e.dma_start(out=o, in_=i), reads=rd, writes=wr,
                   dma_key=(q, key), name="dma")

    def declare(self):
        nc, S, TOWN = self.nc, self.S, self.TOWN
        I = lambda n, sh: nc.dram_tensor(n, sh, F32, kind="ExternalInput")
        self.x_seq = I("x_seq", [S, D])
        self.x_own = I("x_own", [TOWN, D])
        self.x_halo = I("x_halo", [16, D])
        self.p_own = I("p_own", [TOWN, PLE])
        self.w = {
            "in": I("w_in", [D, DIN]), "ao": I("w_attn_out", [512, D]), "co": I("w_conv_out", [512, D]),
            "o": I("w_o", [D, D]), "up": I("w_up", [D, DFF]), "down": I("w_down", [DFF, D]),
            "pg": I("w_ple_gate", [D, D]), "pp": I("w_ple_proj", [PLE, D]),
        }
        self.gvec = I("gvec", [128, 24])
        self.gpost = I("gpost", [2, D])
        self.bgate = I("bgate", [128, 16])
        self.wconv = I("wconv", [128, 12])
        self.cmat = I("cmat", [128, 3 * 128 + 4])
        self.masks = I("masks", [128, 8 * 512])
        self.out = nc.dram_tensor("out", [TOWN, D], F32, kind="ExternalOutput")
        self.ws = {}
        for k, t in self.w.items():
            K, N = int(t.shape[0]), int(t.shape[1])
            self.ws[k] = nc.dram_tensor("ws_" + k, [N // 512, 128, K // 128, 512], BF16, kind="Internal")
        self.kt_s = nc.dram_tensor("kt_s", [4, 128, S], BF16, kind="Internal")
        self.v_s = nc.dram_tensor("v_s", [4, 128, self.NKB, 128], BF16, kind="Internal")
        self.ot_s = nc.dram_tensor("ot_s", [128, 4, TOWN], BF16, kind="Internal")
        self.P = nc.alloc_psum_tensor("P", [128, 4096], F32)
        self.Pb = self.P.bitcast(BF16)
        if self.debug:
            self.dbg_ot = nc.dram_tensor("dbg_ot", [128, 4, TOWN], BF16, kind="ExternalOutput")
            self.dbg_qt = nc.dram_tensor("dbg_qt", [128, 4, TOWN], BF16, kind="ExternalOutput")
            self.dbg_kt = nc.dram_tensor("dbg_kt", [4, 128, S], BF16, kind="ExternalOutput")
            self.dbg_v = nc.dram_tensor("dbg_v", [4, 128, self.NKB, 128], BF16, kind="ExternalOutput")
            self.dbg_x1 = nc.dram_tensor("dbg_x1", [512, D], F32, kind="ExternalOutput")
            self.dbg_x2 = nc.dram_tensor("dbg_x2", [512, D], F32, kind="ExternalOutput")
            self.dbg_mix = nc.dram_tensor("dbg_mix", [128, 8, 512], BF16, kind="ExternalOutput")
            self.dbg_vT = nc.dram_tensor("dbg_vT", [128, 4, 512], BF16, kind="ExternalOutput")

    def bank(self, b, n=1):
        return self.P[:, 512 * b:512 * (b + n)]

    def phase_const(self):
        A = self.A
        TOWN = self.TOWN
        self.cm = A("cm", [128, 3 * 128 + 4], BF16)
        self.ident = self.cm[:, 0:128]
        self.negL = self.cm[:, 128:256]
        self.posones = self.cm[:, 256:384]
        self.onescol = self.cm[:, 384:386]
        self.selc = self.cm[:, 386:387]
        self.gv = A("gv", [128, 24], F32)
        self.bg = A("bg", [128, 16], F32)
        self.wc = A("wc", [128, 12], F32)
        self.gpm = A("gpm", [128, D], F32)
        self.gpl = A("gpl", [128, D], F32)
        self.halfc = A("halfc", [128, 8], F32)
        self.hT_halo = A("hT_halo", [128, 8, 16], BF16)
        self.mAB = A.mark()
        self.OT = A("OT", [128, 4, TOWN], BF16)
        self.QT = A("QT", [128, 4, TOWN], BF16)
        self.maskf = A("maskf", [128, 8, 512], F32)
        m = A.mark()
        st = A("cst", [128, 4096], F32)
        self.dma("sp", st[:, 0:388], self.cmat.ap(), "c0")
        self.vop("dve", "tensor_copy", self.cm[:, :], [st[:, 0:388]])
        self.dma("sp", self.maskf[:, :, :], self.masks.ap().rearrange("p (a b) -> p a b", a=8), "c1")
        self.dma("sp", self.gv[:, :], self.gvec.ap(), "c2")
        self.dma("sp", self.bg[:, :], self.bgate.ap(), "c2")
        self.dma("sp", self.wc[:, :], self.wconv.ap(), "c2")
        gp = self.gpost.ap()
        self.dma("sp", self.gpm[:, :], gp[0:1, :].broadcast_to([128, D]), "c3")
        self.dma("sp", self.gpl[:, :], gp[1:2, :].broadcast_to([128, D]), "c3")
        self.vop("dve", "tensor_scalar", self.gpm[:, :], [self.gpm[:, :]], 32.0, None, ALU.mult)
        self.vop("dve", "tensor_scalar", self.gpl[:, :], [self.gpl[:, :]], 32.0, None, ALU.mult)
        self.vop("pool", "memset", self.halfc[:, :], [], -0.5)
        A.release(m)

    def weight_pieces(self):
        gain_col = {"in": 0, "up": 8, "pg": 16}
        first, rest = [], []
        for k in ("in", "ao", "co", "o", "up", "down", "pg", "pp"):
            w = self.w[k]
            K, N = int(w.shape[0]), int(w.shape[1])
            for rc in range(K // 128):
                for c0 in range(0, N, 2048):
                    cw = min(2048, N - c0)
                    g = gain_col.get(k)
                    (first if (k == "in" and c0 == 0) else rest).append((k, rc, c0, cw, g))
        return first, rest

    def phase_W1(self, pieces):
        A = self.A
        m = A.mark()
        NB = 2
        ot_lo, ot_hi = (int(v) for v in self.OT.manual_sbuf_range)
        stf, stb = [], []
        for i in range(NB):
            if ot_hi - ot_lo >= NB * 12288:
                stf.append(self.nc.alloc_sbuf_tensor_at(f"wst{i}", [128, 2048], F32, offset=ot_lo + i * 12288))
                stb.append(self.nc.alloc_sbuf_tensor_at(f"wsb{i}", [128, 2048], BF16, offset=ot_lo + i * 12288 + 8192))
            else:
                stf.append(A(f"wst{i}", [128, 2048], F32))
                stb.append(A(f"wsb{i}", [128, 2048], BF16))
        for n, (k, rc, c0, cw, g) in enumerate(pieces):
            w = self.w[k]
            T = self.ws[k].ap().rearrange("n p k c -> p n k c")
            b = n % NB
            self.dma("pool", stf[b][:, 0:cw], w[rc * 128:(rc + 1) * 128, c0:c0 + cw], ("wf", b))
            gap = self.gv[:, g + rc:g + rc + 1]
            if n % 2 == 0:
                self.vop("dve", "tensor_scalar", stb[b][:, 0:cw], [stf[b][:, 0:cw]], gap, None, ALU.mult)
            else:
                self.act(stb[b][:, 0:cw], stf[b][:, 0:cw], AF.Identity, scale=gap)
            self.dma("sp", T[:, c0 // 512:(c0 + cw) // 512, rc, :],
                     stb[b][:, 0:cw].rearrange("p (a c) -> p a c", c=512), ("wb", b), writes=[("w1", n)])
        self.w1_keys = [("w1", n) for n in range(len(pieces))]
        A.release(m)

    def start_W2(self, pieces):
        A = self.A
        self.w2 = dict(p=pieces, n=0, stf=[A(f"w2f{i}", [128, 2048], F32) for i in range(2)],
                       stb=[A(f"w2b{i}", [128, 2048], BF16) for i in range(2)])

    def _w2_load(self, n):
        w2 = self.w2
        if n >= len(w2["p"]):
            return
        k, rc, c0, cw, g = w2["p"][n]
        self.dma("pool", w2["stf"][n % 2][:, 0:cw], self.w[k][rc * 128:(rc + 1) * 128, c0:c0 + cw], ("w2f", n % 2))

    def step_W2(self, count=1):
        w2 = self.w2
        for _ in range(count):
            n = w2["n"]
            if n >= len(w2["p"]):
                return
            if n == 0:
                self._w2_load(0)
            self._w2_load(n + 1)
            k, rc, c0, cw, g = w2["p"][n]
            b = n % 2
            src, dst = w2["stf"][b][:, 0:cw], w2["stb"][b][:, 0:cw]
            if g is not None:
                self.vop("pool", "tensor_scalar", dst, [src], self.gv[:, g + rc:g + rc + 1], None, ALU.mult)
            else:
                self.vop("pool", "tensor_copy", dst, [src])
            T = self.ws[k].ap().rearrange("n p k c -> p n k c")
            self.dma("pool", T[:, c0 // 512:(c0 + cw) // 512, rc, :], dst.rearrange("p (a c) -> p a c", c=512),
                     ("w2b", b))
            w2["n"] += 1

    def load_panel(self, q, dst, k, n0, n1, k0, k1, key):
        T = self.ws[k].ap().rearrange("n p k c -> p n k c")
        self.dma(q, dst, T[:, n0:n1, k0:k1, :], key)

    def norm_pre(self, xblk, ntok, ss, rstd, xn, junk, xn_eng="pool"):
        self.act(junk[0:ntok, :], xblk, AF.Square, accum=ss[0:ntok, :])
        self.vop("dve", "tensor_scalar", rstd[0:ntok, :], [ss[0:ntok, :]], 1024.0 * EPS, None, ALU.add)
        self.vop("pool", "tensor_tensor", rstd[0:ntok, :], [rstd[0:ntok, :], self.halfc[0:ntok, 0:1]], ALU.pow)
        self.vop(xn_eng, "tensor_scalar", xn[0:ntok, :], [xblk], rstd[0:ntok, :], 32.0, ALU.mult, ALU.mult)

    def norm_post(self, ntok, hT_dst_fn, xn, pbank, evac_eng):
        pb = self.Pb[:, 1024 * pbank:1024 * (pbank + 1)]
        for d in range(8):
            self.tr(pb[:, d * 128:d * 128 + ntok], xn[0:ntok, d * 128:(d + 1) * 128], self.ident[0:ntok, 0:ntok])
        src = pb.rearrange("p (d t) -> p d t", d=8)[:, :, 0:ntok]
        dst = hT_dst_fn()
        if evac_eng == "act":
            self.act(dst, src, AF.Identity)
        else:
            self.vop(evac_eng, "tensor_copy", dst, [src])

    def norm_T(self, xblk, ntok, hT_dst_fn, ss, rstd, xn, junk, pbank, evac_eng, xn_eng="pool"):
        self.norm_pre(xblk, ntok, ss, rstd, xn, junk, xn_eng)
        self.norm_post(ntok, hT_dst_fn, xn, pbank, evac_eng)

    def phase_A(self, mid):
        A, S, NCH, NSLOT = self.A, self.S, self.NCH, self.NSLOT
        m = A.mark()
        wqkv = A("wqkv", [128, 3, 8, 512], BF16)
        xs = [A(f"xsA{i}", [128, 4, D], F32) for i in range(2)]
        xn = [A(f"xnA{i}", [128, D], BF16) for i in range(4)]
        hT = [A(f"hTA{i}", [128, 8, 512], BF16) for i in range(2)]
        kst = [A(f"kst{i}", [128, 4, 512], BF16) for i in range(2)]
        vst = [A(f"vst{i}", [128, 4, 4, 128], BF16) for i in range(2)]
        junk = A("junkA", [128, D], BF16)
        ss = A("ssA", [128, 8], F32)
        rstd = A("rstdA", [128, 8], F32)
        KT = self.kt_s.ap().rearrange("r p s -> p r s")
        VS = self.v_s.ap().rearrange("r p k d -> p r k d")
        jobs = [("kv", c) for c in range(NCH)] + [("q", i) for i in range(NSLOT)]
        NJ = len(jobs)

        def load(ji):
            if ji >= NJ:
                return
            kind, c = jobs[ji]
            src = self.x_seq if kind == "kv" else self.x_own
            self.dma("pool" if ji % 2 == 0 else "sp", xs[ji % 2][:, :, :],
                     src[c * 512:(c + 1) * 512, :].rearrange("(t p) d -> p t d", p=128), ("xsA", ji % 2))

        def pre(ji, tb):
            if ji >= NJ:
                return
            k = (ji * 4 + tb)
            self.norm_pre(xs[ji % 2][:, tb, :], 128, ss[:, k % 8:k % 8 + 1], rstd[:, k % 8:k % 8 + 1], xn[k % 4], junk)

        def post(ji, tb):
            if ji >= NJ:
                return
            k = (ji * 4 + tb)
            self.norm_post(128, lambda: hT[ji % 2][:, :, tb * 128:(tb + 1) * 128], xn[k % 4], 6 + k % 2, "act")

        load(0)
        load(1)
        for tb in range(4):
            pre(0, tb)
            post(0, tb)
        mid()
        T_in = self.ws["in"].ap().rearrange("n p k c -> p n k c")
        self.dma("sp", wqkv[:, :, :, :], T_in[:, 0:3, 0:8, :], "wqkv", reads=self.w1_keys)
        for ji, (kind, c) in enumerate(jobs):
            b = ji % 2
            h = hT[b]

            def weave(g):
                if g < 4:
                    pre(ji + 1, g)
                if 1 <= g <= 4:
                    post(ji + 1, g - 1)
                if g == 4:
                    load(ji + 2)

            if kind == "kv":
                for ft in range(4):
                    pbk = self.bank(ft % 4)
                    for kc in range(8):
                        self.mm(pbk, wqkv[:, 1, kc, ft * 128:(ft + 1) * 128], h[:, kc, :], kc == 0, kc == 7)
                    self.vop("dve", "tensor_copy", kst[b][:, ft, :], [pbk])
                    weave(ft)
                self.dma("sp", KT[:, :, c * 512:(c + 1) * 512], kst[b][:, :, :], ("kst", b))
                for tb in range(4):
                    pbk = self.bank(4 + tb % 2)
                    for kc in range(8):
                        self.mm(pbk, h[:, kc, tb * 128:(tb + 1) * 128], wqkv[:, 2, kc, :], kc == 0, kc == 7)
                    dst = vst[b][:, :, tb, :]
                    srcp = pbk.rearrange("p (r d) -> p r d", r=4)
                    self.vop("dve", "tensor_copy", dst, [srcp])
                    weave(4 + tb)
                self.dma("sp", VS[:, :, 4 * c:4 * c + 4, :], vst[b][:, :, :, :], ("vst", b))
            else:
                for pr in range(4):
                    pbk = self.bank(pr % 4)
                    for kc in range(8):
                        self.mm(pbk, wqkv[:, 0, kc, pr * 128:(pr + 1) * 128], h[:, kc, :], kc == 0, kc == 7)
                    self.vop("dve", "tensor_scalar", self.QT[:, pr, c * 512:(c + 1) * 512], [pbk], 0.125, None,
                             ALU.mult)
                    weave(pr)
                weave(4)
        xh = A("xh", [16, D], F32)
        self.dma("pool", xh[:, :], self.x_halo.ap(), "xh")
        self.norm_T(xh[:, :], 16, lambda: self.hT_halo[:, :, :], ss[:, 0:1], rstd[:, 0:1], xn[0], junk, 6, "act")
        A.release(m)

    def phase_B(self):
        A, S, NSLOT, NKB = self.A, self.S, self.NSLOT, self.NKB
        m = A.mark()
        KTb = [A(f"KTb{i}", [128, S], BF16) for i in range(2)]
        Vb = [A(f"Vb{i}", [128, NKB, 128], BF16) for i in range(2)]
        U = [A(f"U{i}", [128, 1024], F32) for i in range(2)]
        SP = [A(f"SP{i}", [128, 1024], BF16) for i in range(2)]
        Wb = [A(f"Wb{i}", [128, 1024], BF16) for i in range(2)]
        HI = [A(f"HI{i}", [64, 512], BF16) for i in range(2)]
        LO = [A(f"LO{i}", [64, 512], BF16) for i in range(2)]
        CARRY = self.bank(6)
        self.vop("dve", "memset", CARRY, [], 0.0)
        tiles = []
        chain = 0
        for pr in range(4):
            for i in range(NSLOT):
                top = 8 * i + 7
                for kb in range(top, -1, -1):
                    tiles.append(dict(pr=pr, i=i, kb=kb, first=(kb == top), last=(kb == 0),
                                      mask=(kb - 8 * i if kb >= 8 * i else None), chain=chain))
                chain += 1
        N = len(tiles)
        loaded = set()

        def load_kv(pr):
            if pr in loaded or pr >= 4:
                return
            loaded.add(pr)
            self.dma("sp", KTb[pr % 2][:, :], self.kt_s[pr, :, :], ("ktb", pr % 2))
            self.dma("sp", Vb[pr % 2][:, :, :], self.v_s[pr, :, :, :], ("vb", pr % 2))

        load_kv(0)
        load_kv(1)

        def zb(n):
            return (n % 3) * 2

        OB = 7

        def PE1(n):
            t = tiles[n]
            pr, i, kb, mk = t["pr"], t["i"], t["kb"], t["mask"]
            KT = KTb[pr % 2]
            for hh in range(2):
                z = self.bank(zb(n) + hh)
                rows = slice(64 * hh, 64 * hh + 64)
                self.mm(z, KT[rows, kb * 128:(kb + 1) * 128], self.QT[rows, pr, i * 512:(i + 1) * 512],
                        True, True)
            if mk is not None:
                z2 = self.bank(zb(n), 2).rearrange("p (h q) -> p h q", h=2)
                mb = self.maskf[:, mk, :]
                self.vop("dve", "tensor_tensor", z2, [z2, mb.unsqueeze(1).to_broadcast([128, 2, 512])], ALU.add)

        def X1(n):
            self.act(U[n % 2][:, :], self.bank(zb(n), 2), AF.Exp)

        def X2(n):
            self.act(SP[n % 2][:, :], U[n % 2][:, :], AF.Ln, bias=1.0)

        def PE2(n):
            t = tiles[n]
            sp = SP[n % 2]
            last_acc = t["first"]
            for hh in range(2):
                self.mm(self.bank(zb(n) + hh), self.negL, sp[:, 512 * hh:512 * hh + 512], False, last_acc, skip=True)
            if not t["last"]:
                for hh in range(2):
                    r = slice(32 * hh, 32 * hh + 2)
                    self.mm(CARRY[r, :], self.onescol[:, 0:2], sp[:, 512 * hh:512 * hh + 512], t["first"], True,
                            skip=True)
                self.vop("dve", "tensor_copy", HI[n % 2][0:34, :], [CARRY[0:34, :]])
                self.vop("dve", "scalar_tensor_tensor", LO[n % 2][0:34, :],
                         [HI[n % 2][0:34, :], self.selc[0:34, :], CARRY[0:34, :]], ALU.mult, ALU.subtract)
            if not t["first"]:
                hb = (n - 1) % 2
                for hh in range(2):
                    r = slice(32 * hh, 32 * hh + 2)
                    self.mm(self.bank(zb(n) + hh), self.posones[r, :], LO[hb][r, :], False, True, skip=True)

        def X3(n):
            self.act(Wb[n % 2][:, :], self.bank(zb(n), 2), AF.Exp)

        def PE3(n):
            t = tiles[n]
            pr, i, kb = t["pr"], t["i"], t["kb"]
            V = Vb[pr % 2]
            for hh in range(2):
                rows = slice(64 * hh, 64 * hh + 64)
                self.mm(self.bank(OB)[rows, :], V[:, kb, 64 * hh:64 * hh + 64], Wb[n % 2][:, 512 * hh:512 * hh + 512],
                        t["first"], t["last"])
            if t["last"]:
                self.vop("dve", "tensor_copy", self.OT[:, pr, i * 512:(i + 1) * 512], [self.bank(OB)])
                if i == NSLOT - 1:
                    load_kv(pr + 2)

        for w in range(24):
            self.mm(self.bank(OB), self.ident, self.QT[:, 0, 0:512], True, True)
        for j in range(-3, N):
            if j % 8 == 0:
                self.step_W2(1)
            if 0 <= j < N:
                X3(j)
            if 0 <= j + 1 < N:
                PE2(j + 1)
            if 0 <= j + 3 < N:
                PE1(j + 3)
            if 0 <= j < N:
                PE3(j)
            if 0 <= j + 2 < N:
                X1(j + 2)
                X2(j + 2)
        A.release(m)

    def phase_C(self):
        A, NSLOT = self.A, self.NSLOT
        m = A.mark()
        R = 5
        ring = [A(f"ring{i}", [128, 8, 512], BF16) for i in range(R)]
        xs = [A(f"xsC{i}", [128, 4, D], F32) for i in range(2)]
        xn = [A(f"xnC{i}", [128, D], BF16) for i in range(4)]
        hA = [A(f"hA{i}", [128, 8, 512], BF16) for i in range(2)]
        hB = A("hB", [128, 8, 512], BF16)
        otb = [A(f"otb{i}", [128, 4, 512], BF16) for i in range(2)]
        ss = A("ssC", [128, 32], F32)
        ssb = A("ssbC", [128, 8], F32)
        rstd = A("rstdC", [128, 32], F32)
        cus = A("cus", [128, 514], F32)
        u = A("u", [128, 514], F32)
        tcv = A("tcv", [128, 512], F32)
        vT = A("vT", [128, 4, 512], BF16)
        G = [A(f"G{i}", [128, 512], F32) for i in range(4)]
        gas, gcs, rr, gsb = G[0:2], G[2:4], G[0:2], G[2:4]
        t1 = A("t1", [128, 512], F32)
        t2 = A("t2", [128, 512], F32)
        junk = A.alias("junkC", [128, D], BF16, t1)
        mixT = A("mixT", [128, 8, 512], BF16)
        tmp = [A(f"tmpC{i}", [128, D], F32) for i in range(2)]
        fT = A("fT", [128, 32, 512], BF16)
        fs = A("fs", [128, 4, 512], F32)
        pst = A("pst", [128, 4, PLE], F32)
        pbf = A.alias("pbf", [128, 4, PLE], BF16, cus)
        pT = A.alias("pT", [128, 2, 512], BF16, u)

        seq = []
        for i in range(NSLOT):
            seq += [("in", 3, 4, 0, 8), ("in", 4, 5, 0, 8), ("in", 5, 6, 0, 8)]
            seq += [("in", 6, 7, 0, 8), ("in", 8, 9, 0, 8), ("ao", 0, 2, 0, 4), ("co", 0, 2, 0, 4),
                    ("in", 7, 8, 0, 8), ("in", 9, 10, 0, 8)]
            seq += [("o", 0, 1, 0, 8), ("o", 1, 2, 0, 8)]
            seq += [("up", g, g + 1, 0, 8) for g in range(8)]
            seq += [("down", hf, hf + 1, g * 8, g * 8 + 8) for hf in range(2) for g in range(4)]
            seq += [("pg", 0, 1, 0, 8), ("pg", 1, 2, 0, 8), ("pp", 0, 2, 0, 2)]
        st = dict(lp=0, up=0, rel=0)

        def view(n):
            k, n0, n1, k0, k1 = seq[n]
            nn, kk = n1 - n0, k1 - k0
            return ring[n % R][:, :, :].rearrange("p a c -> p (a c)")[:, 0:nn * kk * 512].rearrange(
                "p (n k c) -> p n k c", n=nn, k=kk)

        def pump():
            while st["lp"] < st["rel"] + R and st["lp"] < len(seq):
                k, n0, n1, k0, k1 = seq[st["lp"]]
                self.load_panel("sp", view(st["lp"]), k, n0, n1, k0, k1, ("ring", st["lp"] % R))
                st["lp"] += 1

        def get_panel():
            n = st["up"]
            pump()
            assert n < st["lp"], "too many live panels"
            st["up"] += 1
            return view(n)

        def done(k=1):
            st["rel"] += k
            pump()

        cnt = dict(k=0, x=0)

        def col():
            cnt["k"] += 1
            return cnt["k"] % 32

        def pre(xblk):
            c = col()
            cnt["x"] += 1
            xb = xn[cnt["x"] % 4]
            self.norm_pre(xblk, 128, ss[:, c:c + 1], rstd[:, c:c + 1], xb, junk)
            return xb

        def post(xb, dst_fn):
            cnt["x"] += 0
            self.norm_post(128, dst_fn, xb, 6 + (id(xb) // 64) % 2, "act")

        def chain(blocks):
            cols = [col() for _ in blocks]
            for (xblk, parts, ssum), c in zip(blocks, cols):
                self.vop("dve", "tensor_scalar", rstd[:, c:c + 1], [ssum], 1024.0 * EPS, None, ALU.add)
            for (xblk, parts, ssum), c in zip(blocks, cols):
                self.vop("pool", "tensor_tensor", rstd[:, c:c + 1], [rstd[:, c:c + 1], self.halfc[:, 0:1]], ALU.pow)
            for (xblk, parts, ssum), c in zip(blocks, cols):
                for ap, c0, w in parts:
                    self.vop("dve", "scalar_tensor_tensor", xblk[:, c0:c0 + w], [ap, rstd[:, c:c + 1], xblk[:, c0:c0 + w]],
                             ALU.mult, ALU.add)
            cols2 = [col() for _ in blocks]
            for (xblk, parts, ssum), c in zip(blocks, cols2):
                self.act(junk[:, :], xblk, AF.Square, accum=ss[:, c:c + 1])
            for (xblk, parts, ssum), c in zip(blocks, cols2):
                self.vop("dve", "tensor_scalar", rstd[:, c:c + 1], [ss[:, c:c + 1]], 1024.0 * EPS, None, ALU.add)
            for (xblk, parts, ssum), c in zip(blocks, cols2):
                self.vop("pool", "tensor_tensor", rstd[:, c:c + 1], [rstd[:, c:c + 1], self.halfc[:, 0:1]], ALU.pow)
            outs = []
            for (xblk, parts, ssum), c in zip(blocks, cols2):
                cnt["x"] += 1
                xb = xn[cnt["x"] % 4]
                self.vop("dve", "tensor_scalar", xb[:, :], [xblk], rstd[:, c:c + 1], 32.0, ALU.mult, ALU.mult)
                outs.append(xb)
            return outs

        def c1_load(i):
            if i >= NSLOT:
                return
            tok = slice(i * 512, (i + 1) * 512)
            self.dma("pool", xs[i % 2][:, :, :], self.x_own[tok, :].rearrange("(t p) d -> p t d", p=128),
                     ("xsC", i % 2))
            self.dma("sp", otb[i % 2][:, :, :], self.ot_s[:, :, tok], ("otb", i % 2))

        c1x = {}

        def c1_pre(i, tb):
            if i < NSLOT:
                c1x[(i, tb)] = pre(xs[i % 2][:, tb, :])

        def c1_post(i, tb):
            if i < NSLOT:
                post(c1x.pop((i, tb)), lambda: hA[i % 2][:, :, tb * 128:(tb + 1) * 128])

        c1_load(0)
        for tb in range(4):
            c1_pre(0, tb)
            c1_post(0, tb)

        for i in range(NSLOT):
            tok = slice(i * 512, (i + 1) * 512)
            X = xs[i % 2]
            hT = hA[i % 2]
            OTi = otb[i % 2]
            c1_load(i + 1)
            self.dma("pool", pst[:, :, :], self.p_own[tok, :].rearrange("(t p) d -> p t d", p=128), ("pst",))
            Wcb = get_panel()[:, 0]
            Wcc = get_panel()[:, 0]
            Wcu = get_panel()[:, 0]
            hb = self.bank(6)
            for ct in range(4):
                base = (ct % 2) * 3
                cs = slice(ct * 128, (ct + 1) * 128)
                for j, Wp in enumerate((Wcb, Wcc, Wcu)):
                    for kc in range(8):
                        self.mm(self.bank(base + j), Wp[:, kc, cs], hT[:, kc, :], kc == 0, kc == 7)
                for j, Wp in enumerate((Wcc, Wcu)):
                    for kc in range(8):
                        self.mm(hb[:, ct * 4 + j * 2:ct * 4 + j * 2 + 2], Wp[:, kc, cs],
                                self.hT_halo[:, kc, 2 * i:2 * i + 2], kc == 0, kc == 7)
                self.act(cus[:, 2:514], self.bank(base + 2), AF.Identity)
                self.act(cus[:, 0:2], hb[:, ct * 4 + 2:ct * 4 + 4], AF.Identity)
                self.vop("dve", "tensor_tensor", u[:, 2:514], [self.bank(base + 1), cus[:, 2:514]], ALU.mult)
                self.vop("dve", "tensor_tensor", u[:, 0:2], [hb[:, ct * 4:ct * 4 + 2], cus[:, 0:2]], ALU.mult)
                w0, w1, w2 = (self.wc[:, ct * 3 + k:ct * 3 + k + 1] for k in range(3))
                self.vop("dve", "tensor_scalar", tcv[:, :], [u[:, 0:512]], w0, None, ALU.mult)
                self.vop("dve", "scalar_tensor_tensor", tcv[:, :], [u[:, 1:513], w1, tcv[:, :]], ALU.mult, ALU.add)
                self.vop("dve", "scalar_tensor_tensor", tcv[:, :], [u[:, 2:514], w2, tcv[:, :]], ALU.mult, ALU.add)
                self.vop("dve", "tensor_tensor", vT[:, ct, :], [self.bank(base), tcv[:, :]], ALU.mult)
            done(3)
            gaP = [None, None]
            gcP = [None, None]
            gaP[0] = get_panel()[:, 0]
            gcP[0] = get_panel()[:, 0]
            aoP = get_panel()
            coP = get_panel()
            for dt in range(8):
                hf = dt // 4
                cs = slice((dt % 4) * 128, (dt % 4) * 128 + 128)
                if dt == 4:
                    done(2)
                    gaP[1] = get_panel()[:, 0]
                    gcP[1] = get_panel()[:, 0]
                base = (dt % 2) * 4
                for kc in range(8):
                    self.mm(self.bank(base + 2), gaP[hf][:, kc, cs], hT[:, kc, :], kc == 0, kc == 7)
                for kc in range(8):
                    self.mm(self.bank(base + 3), gcP[hf][:, kc, cs], hT[:, kc, :], kc == 0, kc == 7)
                for pr in range(4):
                    self.mm(self.bank(base), aoP[:, hf, pr, cs], OTi[:, pr, :], pr == 0, pr == 3)
                for ct in range(4):
                    self.mm(self.bank(base + 1), coP[:, hf, ct, cs], vT[:, ct, :], ct == 0, ct == 3)
                self.act(gas[dt % 2][:, :], self.bank(base + 2), AF.Sigmoid, bias=self.bg[:, dt:dt + 1])
                self.act(gcs[dt % 2][:, :], self.bank(base + 3), AF.Sigmoid, bias=self.bg[:, 8 + dt:9 + dt])
                self.vop("dve", "tensor_tensor", t1[:, :], [self.bank(base), gas[dt % 2][:, :]], ALU.mult)
                self.vop("dve", "tensor_tensor", t2[:, :], [self.bank(base + 1), gcs[dt % 2][:, :]], ALU.mult)
                self.vop("pool", "tensor_tensor", mixT[:, dt, :], [t1[:, :], t2[:, :]], ALU.add)
            done(4)
            oP = [get_panel()[:, 0], get_panel()[:, 0]]
            tA = [tmp[0][:, :], tmp[1][:, :], fs[:, 0:2, :].rearrange("p a c -> p (a c)"), fs[:, 2:4, :].rearrange("p a c -> p (a c)")]
            blocks = []
            for tb in range(4):
                pb = (tb % 3) * 2
                ts_ = slice(tb * 128, (tb + 1) * 128)
                for hf in range(2):
                    for kc in range(8):
                        self.mm(self.bank(pb + hf), mixT[:, kc, ts_], oP[hf][:, kc, :], kc == 0, kc == 7)
                c = col()
                self.act(junk[:, :], self.bank(pb, 2), AF.Square, accum=ss[:, c:c + 1])
                self.vop("dve", "tensor_tensor", tA[tb], [self.bank(pb, 2), self.gpm[:, :]], ALU.mult)
                blocks.append((X[:, tb, :], [(tA[tb], 0, D)], ss[:, c:c + 1]))
            xb5 = chain(blocks)
            done(2)
            if self.debug and i == 0:
                self.dma("sp", self.dbg_x1.ap().rearrange("(t p) d -> p t d", p=128), X[:, :, :], "dbgx1")
                self.dma("sp", self.dbg_mix.ap(), mixT[:, :, :], "dbgmix")
                self.dma("sp", self.dbg_vT.ap(), vT[:, :, :], "dbgvt")
            self.vop("pool", "tensor_copy", pbf[:, :, :], [pst[:, :, :]])
            for tb in range(4):
                pbk = self.Pb[:, 1024 * 7:1024 * 8]
                for kc in range(2):
                    self.tr(pbk[:, kc * 128:(kc + 1) * 128], pbf[:, tb, kc * 128:(kc + 1) * 128], self.ident)
                self.vop("dve", "tensor_copy", pT[:, :, tb * 128:(tb + 1) * 128],
                         [pbk[:, 0:256].rearrange("p (k t) -> p k t", k=2)])
            for tb in range(4):
                post(xb5[tb], lambda tb=tb: hB[:, :, tb * 128:(tb + 1) * 128])
            for g in range(8):
                Wp = get_panel()[:, 0]
                for q in range(4):
                    ft = g * 4 + q
                    bk = self.bank(ft % 6)
                    for kc in range(8):
                        self.mm(bk, Wp[:, kc, q * 128:(q + 1) * 128], hB[:, kc, :], kc == 0, kc == 7)
                    self.act(rr[ft % 2][:, :], bk, AF.Relu)
                    self.vop("dve", "tensor_tensor", fT[:, ft, :], [bk, rr[ft % 2][:, :]], ALU.mult)
                done(1)
                if g < 4:
                    c1_pre(i + 1, g)
                if 1 <= g <= 4:
                    c1_post(i + 1, g - 1)
            blocks6 = []
            for hf in range(2):
                for g in range(4):
                    Wp = get_panel()[:, 0]
                    for tb in range(4):
                        for kc in range(8):
                            self.mm(self.bank(hf * 4 + tb), fT[:, g * 8 + kc, tb * 128:(tb + 1) * 128], Wp[:, kc, :],
                                    g == 0 and kc == 0, g == 3 and kc == 7)
                    done(1)
                for tb in range(4):
                    bk = self.bank(hf * 4 + tb)
                    if hf == 0:
                        self.act(junk[:, 0:512], bk, AF.Square, accum=ssb[:, tb:tb + 1])
                        self.vop("dve", "tensor_tensor", fs[:, tb, :], [bk, self.gpl[:, 0:512]], ALU.mult)
                    else:
                        c = col()
                        self.act(junk[:, 0:512], bk, AF.Square, accum=ss[:, c:c + 1])
                        self.vop("dve", "tensor_tensor", ss[:, c:c + 1], [ss[:, c:c + 1], ssb[:, tb:tb + 1]], ALU.add)
                        self.vop("dve", "tensor_tensor", G[tb][:, :], [bk, self.gpl[:, 512:1024]], ALU.mult)
                        blocks6.append((X[:, tb, :], [(fs[:, tb, :], 0, 512), (G[tb][:, :], 512, 512)], ss[:, c:c + 1]))
            xb6 = chain(blocks6)
            if self.debug and i == 0:
                self.dma("sp", self.dbg_x2.ap().rearrange("(t p) d -> p t d", p=128), X[:, :, :], "dbgx2")
            pgP = [get_panel()[:, 0], get_panel()[:, 0]]
            ppP = get_panel()
            k = 0
            for tb in range(4):
                ts_ = slice(tb * 128, (tb + 1) * 128)
                post(xb6[tb], lambda tb=tb: hB[:, :, tb * 128:(tb + 1) * 128])
                tp = tmp[tb % 2]
                for hf in range(2):
                    gb = self.bank(k % 4)
                    pb_ = self.bank(4 + k % 2)
                    for kc in range(8):
                        self.mm(gb, hB[:, kc, ts_], pgP[hf][:, kc, :], kc == 0, kc == 7)
                    for kc in range(2):
                        self.mm(pb_, pT[:, kc, ts_], ppP[:, hf, kc, :], kc == 0, kc == 1)
                    self.act(gsb[k % 2][:, :], gb, AF.Sigmoid)
                    self.vop("dve", "tensor_tensor", tp[:, hf * 512:(hf + 1) * 512], [pb_, gsb[k % 2][:, :]], ALU.mult)
                    k += 1
                self.vop("pool", "tensor_tensor", X[:, tb, :], [X[:, tb, :], tp[:, :]], ALU.add)
            done(3)
            self.dma("pool", self.out[tok, :].rearrange("(t p) d -> p t d", p=128), X[:, :, :], ("outC", i % 2))
        A.release(m)

    def build(self):
        self.declare()
        with contextlib.ExitStack() as stack:
            self.phase_const()
            first, rest = self.weight_pieces()
            self.phase_A(lambda: self.phase_W1(first))
            self.s.barrier()
            self.start_W2(rest)
            self.phase_B()
            self.step_W2(len(rest))
            self.dma("sp", self.ot_s.ap(), self.OT[:, :, :], "otspill")
            if self.debug:
                self.dma("sp", self.dbg_ot.ap(), self.OT[:, :, :], "dbg0")
                self.dma("sp", self.dbg_qt.ap(), self.QT[:, :, :], "dbg1")
            self.s.barrier()
            self.A.release(self.mAB)
            self.phase_C()
            self.s.barrier()
            self.s.emit(self.nc, stack)
        return self.nc


_CACHE = {}


def _get_nc(S, debug=False):
    key = (S, debug)
    if key not in _CACHE:
        _CACHE[key] = Builder(S, debug).build()
    return _CACHE[key]


def _host_consts():
    cm = np.zeros((128, 3 * 128 + 4), np.float32)
    cm[:, 0:128] = np.eye(128, dtype=np.float32)
    j = np.arange(128)[:, None]
    s = np.arange(128)[None, :]
    cm[:, 128:256] = np.where(j >= s, -1.0, 0.0)
    cm[:, 256:384] = 1.0
    cm[:, 384:386] = 1.0
    cm[1, 386] = 1.0
    cm[33, 386] = 1.0
    return cm


def _host_masks(h):
    m = np.arange(8)[None, :, None]
    s = np.arange(128)[:, None, None]
    q = np.arange(512)[None, None, :]
    mk = np.where(m * 128 + s >= h * 512 + q, NEG, 0.0).astype(np.float32)
    return np.ascontiguousarray(mk.reshape(128, 8 * 512))


def make_in_maps(inputs):
    x = np.asarray(inputs["x"], np.float32)
    p = np.asarray(inputs["p"], np.float32)[0]
    B, S, _ = x.shape
    NCH = S // 512
    NSLOT = NCH // 2
    f = lambda k: np.ascontiguousarray(np.asarray(inputs[k], np.float32)[0])
    gvec = np.concatenate([f(k).reshape(8, 128).T for k in ("g_pre_mix", "g_pre_mlp", "g_ple")], axis=1)
    gpost = np.stack([f("g_post_mix"), f("g_post_mlp")])
    bgate = f("b_gate").reshape(16, 128).T
    wconv = f("w_conv").reshape(3, 4, 128).transpose(2, 1, 0).reshape(128, 12)
    common = {
        "w_in": f("w_in"), "w_attn_out": f("w_attn_out"), "w_conv_out": f("w_conv_out"), "w_o": f("w_o"),
        "w_up": f("w_up"), "w_down": f("w_down"), "w_ple_gate": f("w_ple_gate"), "w_ple_proj": f("w_ple_proj"),
        "gvec": np.ascontiguousarray(gvec), "gpost": np.ascontiguousarray(gpost),
        "bgate": np.ascontiguousarray(bgate), "wconv": np.ascontiguousarray(wconv), "cmat": _host_consts(),
    }
    maps = []
    for core in range(2 * B):
        b, h = core // 2, core % 2
        xc = x[b].reshape(NCH, 512, D)
        halo = np.zeros((16, D), np.float32)
        for i in range(NSLOT):
            c = 2 * i + h
            if c > 0:
                halo[2 * i:2 * i + 2] = xc[c - 1, 510:512]
        mp = dict(common)
        mp["x_seq"] = np.ascontiguousarray(x[b])
        mp["x_own"] = np.ascontiguousarray(xc[h::2].reshape(NSLOT * 512, D))
        mp["x_halo"] = halo
        mp["p_own"] = np.ascontiguousarray(p[b].reshape(NCH, 512, PLE)[h::2].reshape(NSLOT * 512, PLE))
        mp["masks"] = _host_masks(h)
        maps.append(mp)
    return maps


def kernel(**inputs):
    x = np.asarray(inputs["x"])
    B, S, _ = x.shape
    nc = _get_nc(S)
    maps = make_in_maps(inputs)
    res = run_bass_kernel_spmd(nc, maps, core_ids=list(range(2 * B)))
    NCH = S // 512
    out = np.empty((B, NCH, 512, D), np.float32)
    for core in range(2 * B):
        b, h = core // 2, core % 2
        out[b, h::2] = np.asarray(res.results[core]["out"]).reshape(NCH // 2, 512, D)
    return out.reshape(B, S, D)
```

```python
import contextlib
import numpy as np
import concourse.bass as bass
import concourse.mybir as mybir
from concourse.bass_utils import run_bass_kernel_spmd

F32 = mybir.dt.float32
BF16 = mybir.dt.bfloat16
AF = mybir.ActivationFunctionType
ALU = mybir.AluOpType
AX = mybir.AxisListType

DT_SIZE = {F32: 4, BF16: 2}


class Op:
    __slots__ = ("eng", "fn", "dma_key", "deps", "signal", "idx", "sigval", "name")

    def __init__(self, eng, fn, dma_key, name):
        self.eng = eng
        self.fn = fn
        self.dma_key = dma_key
        self.deps = []
        self.signal = False
        self.idx = -1
        self.sigval = 0
        self.name = name


def _footprint(ap):
    t = ap.tensor
    esz = DT_SIZE[t.dtype] if t.dtype in DT_SIZE else mybir.dt.size(t.dtype)
    shape = [int(s) for s in t.shape]
    rowlen = 1
    for s in shape[1:]:
        rowlen *= s
    pat = [[int(a), int(b)] for a, b in ap.ap]
    off = int(ap.offset)
    is_psum = "PSum" in type(t).__name__
    base = 0
    if not is_psum:
        base = int(t.manual_sbuf_range[0])
    pstride, pcount = pat[0]
    assert pstride == rowlen or pcount == 1, (pat, rowlen, t.name)
    p0 = off // rowlen + int(t.base_partition)
    p1 = p0 + pcount
    foff = off % rowlen
    dims = [(s, c) for s, c in pat[1:] if c > 1]
    if not dims:
        runs = [(foff, foff + 1)]
    else:
        dims_sorted = dims
        inner_s, inner_c = dims_sorted[-1]
        if inner_s == 1:
            run = inner_c
            outer = dims_sorted[:-1]
        else:
            run = 1
            outer = dims_sorted
        starts = [foff]
        nruns = 1
        for s, c in outer:
            nruns *= c
        if nruns > 64:
            ext = foff + sum((c - 1) * abs(s) for s, c in dims) + 1
            runs = [(foff, ext)]
        else:
            for s, c in outer:
                starts = [st + k * s for st in starts for k in range(c)]
            runs = sorted((st, st + run) for st in starts)
            merged = []
            for a, b in runs:
                if merged and a <= merged[-1][1]:
                    merged[-1] = (merged[-1][0], max(merged[-1][1], b))
                else:
                    merged.append((a, b))
            runs = merged
    runs = [(base + a * esz, base + b * esz) for a, b in runs]
    if is_psum:
        banks = sorted({b // 2048 for a, e in runs for b in range(a, e, 512)} | {(e - 1) // 2048 for a, e in runs})
        runs = [(bk * 2048, bk * 2048 + 2048) for bk in banks]
        return ("P", 0, 128, runs)
    return ("S", p0, p1, runs)


class Sched:
    COMPUTE = ("pe", "act", "dve", "pool")
    PAGE = 2048

    def __init__(self):
        self.ops = {e: [] for e in ("pe", "act", "dve", "pool", "sp")}
        self.recs = {}
        self.allrecs = []
        self.keyw = {}
        self.keyr = {}
        self.dma_count = {}

    def _pages(self, space, runs):
        pg = set()
        for a, b in runs:
            for p in range(a // self.PAGE, (b - 1) // self.PAGE + 1):
                pg.add((space, p))
        return pg

    @staticmethod
    def _overlap(r, space, p0, p1, runs):
        if r[1] != space or r[3] <= p0 or p1 <= r[2]:
            return False
        for a, b in runs:
            for c, d in r[4]:
                if a < d and c < b:
                    return True
        return False

    @staticmethod
    def _covers(runs_outer, runs_inner):
        for c, d in runs_inner:
            ok = False
            for a, b in runs_outer:
                if a <= c and d <= b:
                    ok = True
                    break
            if not ok:
                return False
        return True

    def _access(self, op, item, is_write):
        if isinstance(item, tuple) and not hasattr(item, "tensor"):
            key = item
            w = self.keyw.get(key)
            if w is not None:
                op.deps.append(w)
            if is_write:
                for r in self.keyr.get(key, ()):
                    op.deps.append(r)
                self.keyw[key] = op
                self.keyr[key] = []
            else:
                self.keyr.setdefault(key, []).append(op)
            return
        space, p0, p1, runs = _footprint(item)
        excl = is_write or space == "P"
        pages = self._pages(space, runs)
        seen = set()
        for pg in pages:
            lst = self.recs.get(pg)
            if not lst:
                continue
            keep = []
            for r in lst:
                if not r[6]:
                    continue
                keep.append(r)
                if id(r) in seen:
                    continue
                seen.add(id(r))
                if not self._overlap(r, space, p0, p1, runs):
                    continue
                if r[0] == "w" or excl:
                    op.deps.append(r[5])
                if excl and r[2] >= p0 and r[3] <= p1 and self._covers(runs, r[4]):
                    r[6] = False
                elif (not excl) and r[0] == "r" and r[5].eng == op.eng and r[5].dma_key is None \
                        and r[2] == p0 and r[3] == p1 and r[4] == runs:
                    r[6] = False
            self.recs[pg] = [r for r in keep if r[6]]
        rec = ["w" if excl else "r", space, p0, p1, runs, op, True]
        for pg in pages:
            self.recs.setdefault(pg, []).append(rec)

    def add(self, eng, fn, reads=(), writes=(), dma_key=None, name=""):
        op = Op(eng, fn, dma_key, name)
        for it in reads:
            self._access(op, it, False)
        for it in writes:
            self._access(op, it, True)
        op.idx = len(self.ops[eng])
        self.ops[eng].append(op)
        if dma_key is not None:
            c = self.dma_count.get(dma_key, 0) + 16
            self.dma_count[dma_key] = c
            op.sigval = c
        return op

    def barrier(self):
        lasts = []
        for e, lst in self.ops.items():
            if e == "sp":
                continue
            real = [o for o in lst if o.fn is not None and o.dma_key is None]
            if real:
                lasts.append(real[-1])
        dm = {}
        for e, lst in self.ops.items():
            for o in lst:
                if o.dma_key is not None:
                    dm[o.dma_key] = o
        for e in self.ops:
            op = Op(e, None, None, "barrier")
            op.deps = list(lasts) + list(dm.values())
            op.idx = len(self.ops[e])
            self.ops[e].append(op)
        self.recs = {}
        self.keyw = {}
        self.keyr = {}

    def finalize(self):
        for e, lst in self.ops.items():
            for op in lst:
                nd = []
                for d in op.deps:
                    if d is op:
                        continue
                    if d.dma_key is None and d.eng == op.eng:
                        if e == "pe" or e == "sp":
                            continue
                    nd.append(d)
                op.deps = nd
                for d in nd:
                    d.signal = True
        self.dma_sems = {}
        for e, lst in self.ops.items():
            cnt = 0
            for op in lst:
                if op.fn is None:
                    continue
                if op.dma_key is not None:
                    pass
                elif op.signal:
                    cnt += 1
                    op.sigval = cnt

    def emit(self, nc, stack):
        self.finalize()
        sems = {}
        for e in self.COMPUTE:
            sems[e] = stack.enter_context(nc.semaphore("s_" + e))
        for n_, k in enumerate(self.dma_count):
            sems[("dma", k)] = stack.enter_context(nc.semaphore("d_%d" % n_))
        block = stack.enter_context(nc.Block())
        sched = self

        def run(engname, eng):
            seen = {}
            for op in sched.ops[engname]:
                need = {}
                for d in op.deps:
                    s = sems[("dma", d.dma_key)] if d.dma_key is not None else sems[d.eng]
                    sk = id(s)
                    v = d.sigval
                    if seen.get(sk, 0) >= v:
                        continue
                    if need.get(sk, (None, 0))[1] < v:
                        need[sk] = (s, v)
                if op.fn is not None and op.dma_key is not None and op.sigval > 16:
                    s = sems[("dma", op.dma_key)]
                    if seen.get(id(s), 0) < op.sigval - 16 and need.get(id(s), (None, 0))[1] < op.sigval - 16:
                        need[id(s)] = (s, op.sigval - 16)
                for sk, (s, v) in need.items():
                    eng.wait_ge(s, v)
                    seen[sk] = v
                if op.fn is None:
                    continue
                ins = op.fn(eng)
                if op.dma_key is not None:
                    ins.then_inc(sems[("dma", op.dma_key)], 16)
                elif op.signal:
                    ins.then_inc(sems[engname], 1)

        @block.tensor
        def _(eng):
            run("pe", eng)

        @block.scalar
        def _(eng):
            run("act", eng)

        @block.vector
        def _(eng):
            run("dve", eng)

        @block.gpsimd
        def _(eng):
            run("pool", eng)

        @block.sync
        def _(eng):
            run("sp", eng)


class Alloc:
    def __init__(self, nc, base=17408, limit=228352):
        self.nc = nc
        self.off = base
        self.limit = limit
        self.n = 0

    def __call__(self, name, shape, dtype):
        size = DT_SIZE[dtype]
        for s in shape[1:]:
            size *= s
        size = (size + 63) // 64 * 64
        assert self.off + size <= self.limit, (name, self.off, size)
        t = self.nc.alloc_sbuf_tensor_at(f"{name}_{self.n}", list(shape), dtype, offset=self.off)
        self.n += 1
        self.off += size
        return t

    def alias(self, name, shape, dtype, base):
        size = DT_SIZE[dtype]
        for v in shape[1:]:
            size *= v
        lo, hi = base.manual_sbuf_range
        assert size <= hi - lo, (name, size, hi - lo)
        t = self.nc.alloc_sbuf_tensor_at(f"{name}_{self.n}", list(shape), dtype, offset=int(lo))
        self.n += 1
        return t

    def mark(self):
        return self.off

    def release(self, m):
        self.off = m


D = 1024
DIN = 5120
DFF = 4096
PLE = 256
EPS = 1e-6
NEG = -30000.0


class Builder:
    def __init__(self, S, debug=False):
        self.S = S
        self.NCH = S // 512
        self.NSLOT = self.NCH // 2
        self.NKB = S // 128
        self.TOWN = self.NSLOT * 512
        self.debug = debug
        self.nc = bass.Bass("TRN2", target_bir_lowering=False)
        self.s = Sched()
        self.A = Alloc(self.nc)
        self.rr = 0

    def mm(self, out, lhsT, rhs, start, stop, skip=False):
        self.s.add("pe", lambda e, o=out, l=lhsT, r=rhs, a=start, b=stop, k=skip:
                   e.matmul(o, l, r, start=a, stop=b, skip_group_check=k),
                   reads=[lhsT, rhs], writes=[out], name="mm")

    def tr(self, out, in_, ident):
        self.s.add("pe", lambda e, o=out, i=in_, d=ident: e.transpose(o, i, d),
                   reads=[in_, ident], writes=[out], name="tr")

    def act(self, out, in_, func, bias=None, scale=None, accum=None, extra_reads=()):
        kw = {}
        rd = [in_] + list(extra_reads)
        wr = [out]
        if bias is not None:
            kw["bias"] = bias
            if not isinstance(bias, float):
                rd.append(bias)
        if scale is not None:
            kw["scale"] = scale
            if not isinstance(scale, float):
                rd.append(scale)
        if accum is not None:
            kw["accum_out"] = accum
            wr.append(accum)
        self.s.add("act", lambda e, o=out, i=in_, f=func, k=kw: e.activation(o, i, f, **k),
                   reads=rd, writes=wr, name="act")

    def vop(self, eng, method, out, ins, *args, extra_writes=(), **kw):
        rd = [a for a in ins if hasattr(a, "tensor")]
        rd += [a for a in args if hasattr(a, "tensor")]
        self.s.add(eng, lambda e, m=method, o=out, i=tuple(ins), a=args, k=kw: getattr(e, m)(o, *i, *a, **k),
                   reads=rd, writes=[out] + list(extra_writes), name=method)

    def dma(self, q, out, in_, key, reads=(), writes=()):
        rd = list(reads)
        wr = list(writes)
        if "DRam" not in type(in_.tensor).__name__:
            rd.append(in_)
        if "DRam" not in type(out.tensor).__name__:
            wr.append(out)
        self.s.add(q, lambda e, o=out, i=in_: e.dma_start(out=o, in_=i), reads=rd, writes=wr,
                   dma_key=(q, key), name="dma")

    def declare(self):
        nc, S, TOWN = self.nc, self.S, self.TOWN
        I = lambda n, sh: nc.dram_tensor(n, sh, F32, kind="ExternalInput")
        self.x_seq = I("x_seq", [S, D])
        self.x_own = I("x_own", [TOWN, D])
        self.x_halo = I("x_halo", [16, D])
        self.p_own = I("p_own", [TOWN, PLE])
        self.w = {
            "in": I("w_in", [D, DIN]), "ao": I("w_attn_out", [512, D]), "co": I("w_conv_out", [512, D]),
            "o": I("w_o", [D, D]), "up": I("w_up", [D, DFF]), "down": I("w_down", [DFF, D]),
            "pg": I("w_ple_gate", [D, D]), "pp": I("w_ple_proj", [PLE, D]),
        }
        self.gvec = I("gvec", [128, 24])
        self.gpost = I("gpost", [2, D])
        self.bgate = I("bgate", [128, 16])
        self.wconv = I("wconv", [128, 12])
        self.cmat = I("cmat", [128, 3 * 128 + 4])
        self.masks = I("masks", [128, 8 * 512])
        self.out = nc.dram_tensor("out", [TOWN, D], F32, kind="ExternalOutput")
        self.ws = {}
        for k, t in self.w.items():
            K, N = int(t.shape[0]), int(t.shape[1])
            self.ws[k] = nc.dram_tensor("ws_" + k, [N // 512, 128, K // 128, 512], BF16, kind="Internal")
        self.kt_s = nc.dram_tensor("kt_s", [4, 128, S], BF16, kind="Internal")
        self.v_s = nc.dram_tensor("v_s", [4, 128, self.NKB, 128], BF16, kind="Internal")
        self.ot_s = nc.dram_tensor("ot_s", [128, 4, TOWN], BF16, kind="Internal")
        self.P = nc.alloc_psum_tensor("P", [128, 4096], F32)
        self.Pb = self.P.bitcast(BF16)
        if self.debug:
            self.dbg_ot = nc.dram_tensor("dbg_ot", [128, 4, TOWN], BF16, kind="ExternalOutput")
            self.dbg_qt = nc.dram_tensor("dbg_qt", [128, 4, TOWN], BF16, kind="ExternalOutput")
            self.dbg_kt = nc.dram_tensor("dbg_kt", [4, 128, S], BF16, kind="ExternalOutput")
            self.dbg_v = nc.dram_tensor("dbg_v", [4, 128, self.NKB, 128], BF16, kind="ExternalOutput")
            self.dbg_x1 = nc.dram_tensor("dbg_x1", [512, D], F32, kind="ExternalOutput")
            self.dbg_x2 = nc.dram_tensor("dbg_x2", [512, D], F32, kind="ExternalOutput")
            self.dbg_mix = nc.dram_tensor("dbg_mix", [128, 8, 512], BF16, kind="ExternalOutput")
            self.dbg_vT = nc.dram_tensor("dbg_vT", [128, 4, 512], BF16, kind="ExternalOutput")

    def bank(self, b, n=1):
        return self.P[:, 512 * b:512 * (b + n)]

    def phase_const(self):
        A = self.A
        TOWN = self.TOWN
        self.cm = A("cm", [128, 3 * 128 + 4], BF16)
        self.ident = self.cm[:, 0:128]
        self.negL = self.cm[:, 128:256]
        self.posones = self.cm[:, 256:384]
        self.onescol = self.cm[:, 384:386]
        self.selc = self.cm[:, 386:387]
        self.gv = A("gv", [128, 24], F32)
        self.bg = A("bg", [128, 16], F32)
        self.wc = A("wc", [128, 12], F32)
        self.gpm = A("gpm", [128, D], F32)
        self.gpl = A("gpl", [128, D], F32)
        self.halfc = A("halfc", [128, 8], F32)
        self.hT_halo = A("hT_halo", [128, 8, 16], BF16)
        self.mAB = A.mark()
        self.OT = A("OT", [128, 4, TOWN], BF16)
        self.QT = A("QT", [128, 4, TOWN], BF16)
        self.maskf = A("maskf", [128, 8, 512], F32)
        m = A.mark()
        st = A("cst", [128, 4096], F32)
        self.dma("sp", st[:, 0:388], self.cmat.ap(), "c0")
        self.vop("dve", "tensor_copy", self.cm[:, :], [st[:, 0:388]])
        self.dma("sp", self.maskf[:, :, :], self.masks.ap().rearrange("p (a b) -> p a b", a=8), "c1")
        self.dma("sp", self.gv[:, :], self.gvec.ap(), "c2")
        self.dma("sp", self.bg[:, :], self.bgate.ap(), "c2")
        self.dma("sp", self.wc[:, :], self.wconv.ap(), "c2")
        gp = self.gpost.ap()
        self.dma("sp", self.gpm[:, :], gp[0:1, :].broadcast_to([128, D]), "c3")
        self.dma("sp", self.gpl[:, :], gp[1:2, :].broadcast_to([128, D]), "c3")
        self.vop("dve", "tensor_scalar", self.gpm[:, :], [self.gpm[:, :]], 32.0, None, ALU.mult)
        self.vop("dve", "tensor_scalar", self.gpl[:, :], [self.gpl[:, :]], 32.0, None, ALU.mult)
        self.vop("pool", "memset", self.halfc[:, :], [], -0.5)
        A.release(m)

    def weight_pieces(self):
        gain_col = {"in": 0, "up": 8, "pg": 16}
        first, rest = [], []
        for k in ("in", "ao", "co", "o", "up", "down", "pg", "pp"):
            w = self.w[k]
            K, N = int(w.shape[0]), int(w.shape[1])
            for rc in range(K // 128):
                for c0 in range(0, N, 2048):
                    cw = min(2048, N - c0)
                    g = gain_col.get(k)
                    (first if (k == "in" and c0 == 0) else rest).append((k, rc, c0, cw, g))
        return first, rest

    def phase_W1(self, pieces):
        A = self.A
        m = A.mark()
        NB = 2
        ot_lo, ot_hi = (int(v) for v in self.OT.manual_sbuf_range)
        stf, stb = [], []
        for i in range(NB):
            if ot_hi - ot_lo >= NB * 12288:
                stf.append(self.nc.alloc_sbuf_tensor_at(f"wst{i}", [128, 2048], F32, offset=ot_lo + i * 12288))
                stb.append(self.nc.alloc_sbuf_tensor_at(f"wsb{i}", [128, 2048], BF16, offset=ot_lo + i * 12288 + 8192))
            else:
                stf.append(A(f"wst{i}", [128, 2048], F32))
                stb.append(A(f"wsb{i}", [128, 2048], BF16))
        for n, (k, rc, c0, cw, g) in enumerate(pieces):
            w = self.w[k]
            T = self.ws[k].ap().rearrange("n p k c -> p n k c")
            b = n % NB
            self.dma("sp", stf[b][:, 0:cw], w[rc * 128:(rc + 1) * 128, c0:c0 + cw], ("wf", b))
            gap = self.gv[:, g + rc:g + rc + 1]
            if n % 2 == 0:
                self.vop("dve", "tensor_scalar", stb[b][:, 0:cw], [stf[b][:, 0:cw]], gap, None, ALU.mult)
            else:
                self.act(stb[b][:, 0:cw], stf[b][:, 0:cw], AF.Identity, scale=gap)
            self.dma("sp", T[:, c0 // 512:(c0 + cw) // 512, rc, :],
                     stb[b][:, 0:cw].rearrange("p (a c) -> p a c", c=512), ("wb", b), writes=[("w1", n)])
        self.w1_keys = [("w1", n) for n in range(len(pieces))]
        A.release(m)

    def start_W2(self, pieces):
        A = self.A
        self.w2 = dict(p=pieces, n=0, stf=[A(f"w2f{i}", [128, 2048], F32) for i in range(2)],
                       stb=[A(f"w2b{i}", [128, 2048], BF16) for i in range(2)])

    def _w2_load(self, n):
        w2 = self.w2
        if n >= len(w2["p"]):
            return
        k, rc, c0, cw, g = w2["p"][n]
        self.dma("pool", w2["stf"][n % 2][:, 0:cw], self.w[k][rc * 128:(rc + 1) * 128, c0:c0 + cw], ("w2f", n % 2))

    def step_W2(self, count=1):
        w2 = self.w2
        for _ in range(count):
            n = w2["n"]
            if n >= len(w2["p"]):
                return
            if n == 0:
                self._w2_load(0)
            self._w2_load(n + 1)
            k, rc, c0, cw, g = w2["p"][n]
            b = n % 2
            src, dst = w2["stf"][b][:, 0:cw], w2["stb"][b][:, 0:cw]
            if g is not None:
                self.vop("pool", "tensor_scalar", dst, [src], self.gv[:, g + rc:g + rc + 1], None, ALU.mult)
            else:
                self.vop("pool", "tensor_copy", dst, [src])
            T = self.ws[k].ap().rearrange("n p k c -> p n k c")
            self.dma("pool", T[:, c0 // 512:(c0 + cw) // 512, rc, :], dst.rearrange("p (a c) -> p a c", c=512),
                     ("w2b", b))
            w2["n"] += 1

    def load_panel(self, q, dst, k, n0, n1, k0, k1, key):
        T = self.ws[k].ap().rearrange("n p k c -> p n k c")
        self.dma(q, dst, T[:, n0:n1, k0:k1, :], key)

    def norm_pre(self, xblk, ntok, ss, rstd, xn, junk, xn_eng="pool"):
        self.act(junk[0:ntok, :], xblk, AF.Square, accum=ss[0:ntok, :])
        self.vop("dve", "tensor_scalar", rstd[0:ntok, :], [ss[0:ntok, :]], 1024.0 * EPS, None, ALU.add)
        self.vop("pool", "tensor_tensor", rstd[0:ntok, :], [rstd[0:ntok, :], self.halfc[0:ntok, 0:1]], ALU.pow)
        self.vop(xn_eng, "tensor_scalar", xn[0:ntok, :], [xblk], rstd[0:ntok, :], 32.0, ALU.mult, ALU.mult)

    def norm_post(self, ntok, hT_dst_fn, xn, pbank, evac_eng):
        pb = self.Pb[:, 1024 * pbank:1024 * (pbank + 1)]
        for d in range(8):
            self.tr(pb[:, d * 128:d * 128 + ntok], xn[0:ntok, d * 128:(d + 1) * 128], self.ident[0:ntok, 0:ntok])
        src = pb.rearrange("p (d t) -> p d t", d=8)[:, :, 0:ntok]
        dst = hT_dst_fn()
        if evac_eng == "act":
            self.act(dst, src, AF.Identity)
        else:
            self.vop(evac_eng, "tensor_copy", dst, [src])

    def norm_T(self, xblk, ntok, hT_dst_fn, ss, rstd, xn, junk, pbank, evac_eng, xn_eng="pool"):
        self.norm_pre(xblk, ntok, ss, rstd, xn, junk, xn_eng)
        self.norm_post(ntok, hT_dst_fn, xn, pbank, evac_eng)

    def phase_A(self, mid):
        A, S, NCH, NSLOT = self.A, self.S, self.NCH, self.NSLOT
        m = A.mark()
        wqkv = A("wqkv", [128, 3, 8, 512], BF16)
        xs = [A(f"xsA{i}", [128, 4, D], F32) for i in range(2)]
        xn = [A(f"xnA{i}", [128, D], BF16) for i in range(4)]
        hT = [A(f"hTA{i}", [128, 8, 512], BF16) for i in range(2)]
        kst = [A(f"kst{i}", [128, 4, 512], BF16) for i in range(2)]
        vst = [A(f"vst{i}", [128, 4, 4, 128], BF16) for i in range(2)]
        junk = A("junkA", [128, D], BF16)
        ss = A("ssA", [128, 8], F32)
        rstd = A("rstdA", [128, 8], F32)
        KT = self.kt_s.ap().rearrange("r p s -> p r s")
        VS = self.v_s.ap().rearrange("r p k d -> p r k d")
        jobs = [("kv", c) for c in range(NCH)] + [("q", i) for i in range(NSLOT)]
        NJ = len(jobs)

        def load(ji):
            if ji >= NJ:
                return
            kind, c = jobs[ji]
            src = self.x_seq if kind == "kv" else self.x_own
            self.dma("sp", xs[ji % 2][:, :, :],
                     src[c * 512:(c + 1) * 512, :].rearrange("(t p) d -> p t d", p=128), ("xsA", ji % 2))

        def pre(ji, tb):
            if ji >= NJ:
                return
            k = (ji * 4 + tb)
            self.norm_pre(xs[ji % 2][:, tb, :], 128, ss[:, k % 8:k % 8 + 1], rstd[:, k % 8:k % 8 + 1], xn[k % 4], junk)

        def post(ji, tb):
            if ji >= NJ:
                return
            k = (ji * 4 + tb)
            self.norm_post(128, lambda: hT[ji % 2][:, :, tb * 128:(tb + 1) * 128], xn[k % 4], 6 + k % 2, "act")

        load(0)
        load(1)
        for tb in range(4):
            pre(0, tb)
            post(0, tb)
        mid()
        T_in = self.ws["in"].ap().rearrange("n p k c -> p n k c")
        self.dma("sp", wqkv[:, :, :, :], T_in[:, 0:3, 0:8, :], "wqkv", reads=self.w1_keys)
        for ji, (kind, c) in enumerate(jobs):
            b = ji % 2
            h = hT[b]

            def weave(g):
                if g < 4:
                    pre(ji + 1, g)
                if 1 <= g <= 4:
                    post(ji + 1, g - 1)
                if g == 4:
                    load(ji + 2)

            if kind == "kv":
                for ft in range(4):
                    pbk = self.bank(ft % 4)
                    for kc in range(8):
                        self.mm(pbk, wqkv[:, 1, kc, ft * 128:(ft + 1) * 128], h[:, kc, :], kc == 0, kc == 7)
                    self.vop("dve", "tensor_copy", kst[b][:, ft, :], [pbk])
                    weave(ft)
                self.dma("sp", KT[:, :, c * 512:(c + 1) * 512], kst[b][:, :, :], ("kst", b))
                for tb in range(4):
                    pbk = self.bank(4 + tb % 2)
                    for kc in range(8):
                        self.mm(pbk, h[:, kc, tb * 128:(tb + 1) * 128], wqkv[:, 2, kc, :], kc == 0, kc == 7)
                    dst = vst[b][:, :, tb, :]
                    srcp = pbk.rearrange("p (r d) -> p r d", r=4)
                    self.vop("dve", "tensor_copy", dst, [srcp])
                    weave(4 + tb)
                self.dma("sp", VS[:, :, 4 * c:4 * c + 4, :], vst[b][:, :, :, :], ("vst", b))
            else:
                for pr in range(4):
                    pbk = self.bank(pr % 4)
                    for kc in range(8):
                        self.mm(pbk, wqkv[:, 0, kc, pr * 128:(pr + 1) * 128], h[:, kc, :], kc == 0, kc == 7)
                    self.vop("dve", "tensor_scalar", self.QT[:, pr, c * 512:(c + 1) * 512], [pbk], 0.125, None,
                             ALU.mult)
                    weave(pr)
                weave(4)
        xh = A("xh", [16, D], F32)
        self.dma("sp", xh[:, :], self.x_halo.ap(), "xh")
        self.norm_T(xh[:, :], 16, lambda: self.hT_halo[:, :, :], ss[:, 0:1], rstd[:, 0:1], xn[0], junk, 6, "act")
        A.release(m)

    def phase_B(self):
        A, S, NSLOT, NKB = self.A, self.S, self.NSLOT, self.NKB
        m = A.mark()
        KTb = [A(f"KTb{i}", [128, S], BF16) for i in range(2)]
        Vb = [A(f"Vb{i}", [128, NKB, 128], BF16) for i in range(2)]
        U = [A(f"U{i}", [128, 1024], F32) for i in range(2)]
        SP = [A(f"SP{i}", [128, 1024], BF16) for i in range(2)]
        Wb = [A(f"Wb{i}", [128, 1024], BF16) for i in range(2)]
        HI = [A(f"HI{i}", [64, 512], BF16) for i in range(2)]
        LO = [A(f"LO{i}", [64, 512], BF16) for i in range(2)]
        CARRY = self.bank(6)
        self.vop("dve", "memset", CARRY, [], 0.0)
        tiles = []
        chain = 0
        for pr in range(4):
            for i in range(NSLOT):
                top = 8 * i + 7
                for kb in range(top, -1, -1):
                    tiles.append(dict(pr=pr, i=i, kb=kb, first=(kb == top), last=(kb == 0),
                                      mask=(kb - 8 * i if kb >= 8 * i else None), chain=chain))
                chain += 1
        N = len(tiles)
        loaded = set()

        def load_kv(pr):
            if pr in loaded or pr >= 4:
                return
            loaded.add(pr)
            self.dma("sp", KTb[pr % 2][:, :], self.kt_s[pr, :, :], ("ktb", pr % 2))
            self.dma("sp", Vb[pr % 2][:, :, :], self.v_s[pr, :, :, :], ("vb", pr % 2))

        load_kv(0)
        load_kv(1)

        def zb(n):
            return (n % 3) * 2

        OB = 7

        def PE1(n):
            t = tiles[n]
            pr, i, kb, mk = t["pr"], t["i"], t["kb"], t["mask"]
            KT = KTb[pr % 2]
            for hh in range(2):
                z = self.bank(zb(n) + hh)
                rows = slice(64 * hh, 64 * hh + 64)
                self.mm(z, KT[rows, kb * 128:(kb + 1) * 128], self.QT[rows, pr, i * 512:(i + 1) * 512],
                        True, True)
            if mk is not None:
                z2 = self.bank(zb(n), 2).rearrange("p (h q) -> p h q", h=2)
                mb = self.maskf[:, mk, :]
                self.vop("dve", "tensor_tensor", z2, [z2, mb.unsqueeze(1).to_broadcast([128, 2, 512])], ALU.add)

        def X1(n):
            self.act(U[n % 2][:, :], self.bank(zb(n), 2), AF.Exp)

        def X2(n):
            self.act(SP[n % 2][:, :], U[n % 2][:, :], AF.Ln, bias=1.0)

        def PE2(n):
            t = tiles[n]
            sp = SP[n % 2]
            last_acc = t["first"]
            for hh in range(2):
                self.mm(self.bank(zb(n) + hh), self.negL, sp[:, 512 * hh:512 * hh + 512], False, last_acc, skip=True)
            if not t["last"]:
                for hh in range(2):
                    r = slice(32 * hh, 32 * hh + 2)
                    self.mm(CARRY[r, :], self.onescol[:, 0:2], sp[:, 512 * hh:512 * hh + 512], t["first"], True,
                            skip=True)
                self.vop("dve", "tensor_copy", HI[n % 2][0:34, :], [CARRY[0:34, :]])
                self.vop("dve", "scalar_tensor_tensor", LO[n % 2][0:34, :],
                         [HI[n % 2][0:34, :], self.selc[0:34, :], CARRY[0:34, :]], ALU.mult, ALU.subtract)
            if not t["first"]:
                hb = (n - 1) % 2
                for hh in range(2):
                    r = slice(32 * hh, 32 * hh + 2)
                    self.mm(self.bank(zb(n) + hh), self.posones[r, :], LO[hb][r, :], False, True, skip=True)

        def X3(n):
            self.act(Wb[n % 2][:, :], self.bank(zb(n), 2), AF.Exp)

        def PE3(n):
            t = tiles[n]
            pr, i, kb = t["pr"], t["i"], t["kb"]
            V = Vb[pr % 2]
            for hh in range(2):
                rows = slice(64 * hh, 64 * hh + 64)
                self.mm(self.bank(OB)[rows, :], V[:, kb, 64 * hh:64 * hh + 64], Wb[n % 2][:, 512 * hh:512 * hh + 512],
                        t["first"], t["last"])
            if t["last"]:
                self.vop("dve", "tensor_copy", self.OT[:, pr, i * 512:(i + 1) * 512], [self.bank(OB)])
                if i == NSLOT - 1:
                    load_kv(pr + 2)

        for w in range(24):
            self.mm(self.bank(OB), self.ident, self.QT[:, 0, 0:512], True, True)
        for j in range(-3, N):
            if j % 8 == 0:
                self.step_W2(1)
            if 0 <= j < N:
                X3(j)
            if 0 <= j + 1 < N:
                PE2(j + 1)
            if 0 <= j + 3 < N:
                PE1(j + 3)
            if 0 <= j < N:
                PE3(j)
            if 0 <= j + 2 < N:
                X1(j + 2)
                X2(j + 2)
        A.release(m)

    def phase_C(self):
        A, NSLOT = self.A, self.NSLOT
        m = A.mark()
        R = 5
        ring = [A(f"ring{i}", [128, 8, 512], BF16) for i in range(R)]
        xs = [A(f"xsC{i}", [128, 4, D], F32) for i in range(2)]
        xn = [A(f"xnC{i}", [128, D], BF16) for i in range(4)]
        hA = [A(f"hA{i}", [128, 8, 512], BF16) for i in range(2)]
        hB = A("hB", [128, 8, 512], BF16)
        otb = [A(f"otb{i}", [128, 4, 512], BF16) for i in range(2)]
        ss = A("ssC", [128, 32], F32)
        ssb = A("ssbC", [128, 8], F32)
        rstd = A("rstdC", [128, 32], F32)
        cus = A("cus", [128, 514], F32)
        u = A("u", [128, 514], F32)
        tcv = A("tcv", [128, 512], F32)
        vT = A("vT", [128, 4, 512], BF16)
        G = [A(f"G{i}", [128, 512], F32) for i in range(4)]
        gas, gcs, rr, gsb = G[0:2], G[2:4], G[0:2], G[2:4]
        t1 = A("t1", [128, 512], F32)
        t2 = A("t2", [128, 512], F32)
        junk = A.alias("junkC", [128, D], BF16, t1)
        mixT = A("mixT", [128, 8, 512], BF16)
        tmp = [A(f"tmpC{i}", [128, D], F32) for i in range(2)]
        fT = A("fT", [128, 32, 512], BF16)
        fs = A("fs", [128, 4, 512], F32)
        pst = A("pst", [128, 4, PLE], F32)
        pbf = A.alias("pbf", [128, 4, PLE], BF16, cus)
        pT = A.alias("pT", [128, 2, 512], BF16, u)

        seq = []
        for i in range(NSLOT):
            seq += [("in", 3, 4, 0, 8), ("in", 4, 5, 0, 8), ("in", 5, 6, 0, 8)]
            seq += [("in", 6, 7, 0, 8), ("in", 8, 9, 0, 8), ("ao", 0, 2, 0, 4), ("co", 0, 2, 0, 4),
                    ("in", 7, 8, 0, 8), ("in", 9, 10, 0, 8)]
            seq += [("o", 0, 1, 0, 8), ("o", 1, 2, 0, 8)]
            seq += [("up", g, g + 1, 0, 8) for g in range(8)]
            seq += [("down", hf, hf + 1, g * 8, g * 8 + 8) for hf in range(2) for g in range(4)]
            seq += [("pg", 0, 1, 0, 8), ("pg", 1, 2, 0, 8), ("pp", 0, 2, 0, 2)]
        st = dict(lp=0, up=0, rel=0)

        def view(n):
            k, n0, n1, k0, k1 = seq[n]
            nn, kk = n1 - n0, k1 - k0
            return ring[n % R][:, :, :].rearrange("p a c -> p (a c)")[:, 0:nn * kk * 512].rearrange(
                "p (n k c) -> p n k c", n=nn, k=kk)

        def pump():
            while st["lp"] < st["rel"] + R and st["lp"] < len(seq):
                k, n0, n1, k0, k1 = seq[st["lp"]]
                self.load_panel("sp", view(st["lp"]), k, n0, n1, k0, k1, ("ring", st["lp"] % R))
                st["lp"] += 1

        def get_panel():
            n = st["up"]
            pump()
            assert n < st["lp"], "too many live panels"
            st["up"] += 1
            return view(n)

        def done(k=1):
            st["rel"] += k
            pump()

        cnt = dict(k=0, x=0)

        def col():
            cnt["k"] += 1
            return cnt["k"] % 32

        def pre(xblk):
            c = col()
            cnt["x"] += 1
            xb = xn[cnt["x"] % 4]
            self.norm_pre(xblk, 128, ss[:, c:c + 1], rstd[:, c:c + 1], xb, junk)
            return xb

        def post(xb, dst_fn):
            cnt["x"] += 0
            self.norm_post(128, dst_fn, xb, 6 + (id(xb) // 64) % 2, "act")

        def chain(blocks):
            cols = [col() for _ in blocks]
            for (xblk, parts, ssum), c in zip(blocks, cols):
                self.vop("dve", "tensor_scalar", rstd[:, c:c + 1], [ssum], 1024.0 * EPS, None, ALU.add)
            for (xblk, parts, ssum), c in zip(blocks, cols):
                self.vop("pool", "tensor_tensor", rstd[:, c:c + 1], [rstd[:, c:c + 1], self.halfc[:, 0:1]], ALU.pow)
            for (xblk, parts, ssum), c in zip(blocks, cols):
                for ap, c0, w in parts:
                    self.vop("dve", "scalar_tensor_tensor", xblk[:, c0:c0 + w], [ap, rstd[:, c:c + 1], xblk[:, c0:c0 + w]],
                             ALU.mult, ALU.add)
            cols2 = [col() for _ in blocks]
            for (xblk, parts, ssum), c in zip(blocks, cols2):
                self.act(junk[:, :], xblk, AF.Square, accum=ss[:, c:c + 1])
            for (xblk, parts, ssum), c in zip(blocks, cols2):
                self.vop("dve", "tensor_scalar", rstd[:, c:c + 1], [ss[:, c:c + 1]], 1024.0 * EPS, None, ALU.add)
            for (xblk, parts, ssum), c in zip(blocks, cols2):
                self.vop("pool", "tensor_tensor", rstd[:, c:c + 1], [rstd[:, c:c + 1], self.halfc[:, 0:1]], ALU.pow)
            outs = []
            for (xblk, parts, ssum), c in zip(blocks, cols2):
                cnt["x"] += 1
                xb = xn[cnt["x"] % 4]
                self.vop("dve", "tensor_scalar", xb[:, :], [xblk], rstd[:, c:c + 1], 32.0, ALU.mult, ALU.mult)
                outs.append(xb)
            return outs

        def c1_load(i):
            if i >= NSLOT:
                return
            tok = slice(i * 512, (i + 1) * 512)
            self.dma("sp", xs[i % 2][:, :, :], self.x_own[tok, :].rearrange("(t p) d -> p t d", p=128),
                     ("xsC", i % 2))
            self.dma("sp", otb[i % 2][:, :, :], self.ot_s[:, :, tok], ("otb", i % 2))

        c1x = {}

        def c1_pre(i, tb):
            if i < NSLOT:
                c1x[(i, tb)] = pre(xs[i % 2][:, tb, :])

        def c1_post(i, tb):
            if i < NSLOT:
                post(c1x.pop((i, tb)), lambda: hA[i % 2][:, :, tb * 128:(tb + 1) * 128])

        c1_load(0)
        for tb in range(4):
            c1_pre(0, tb)
            c1_post(0, tb)

        for i in range(NSLOT):
            tok = slice(i * 512, (i + 1) * 512)
            X = xs[i % 2]
            hT = hA[i % 2]
            OTi = otb[i % 2]
            c1_load(i + 1)
            self.dma("sp", pst[:, :, :], self.p_own[tok, :].rearrange("(t p) d -> p t d", p=128), ("pst",))
            Wcb = get_panel()[:, 0]
            Wcc = get_panel()[:, 0]
            Wcu = get_panel()[:, 0]
            hb = self.bank(6)
            for ct in range(4):
                base = (ct % 2) * 3
                cs = slice(ct * 128, (ct + 1) * 128)
                for j, Wp in enumerate((Wcb, Wcc, Wcu)):
                    for kc in range(8):
                        self.mm(self.bank(base + j), Wp[:, kc, cs], hT[:, kc, :], kc == 0, kc == 7)
                for j, Wp in enumerate((Wcc, Wcu)):
                    for kc in range(8):
                        self.mm(hb[:, ct * 4 + j * 2:ct * 4 + j * 2 + 2], Wp[:, kc, cs],
                                self.hT_halo[:, kc, 2 * i:2 * i + 2], kc == 0, kc == 7)
                self.act(cus[:, 2:514], self.bank(base + 2), AF.Identity)
                self.act(cus[:, 0:2], hb[:, ct * 4 + 2:ct * 4 + 4], AF.Identity)
                self.vop("dve", "tensor_tensor", u[:, 2:514], [self.bank(base + 1), cus[:, 2:514]], ALU.mult)
                self.vop("dve", "tensor_tensor", u[:, 0:2], [hb[:, ct * 4:ct * 4 + 2], cus[:, 0:2]], ALU.mult)
                w0, w1, w2 = (self.wc[:, ct * 3 + k:ct * 3 + k + 1] for k in range(3))
                self.vop("dve", "tensor_scalar", tcv[:, :], [u[:, 0:512]], w0, None, ALU.mult)
                self.vop("dve", "scalar_tensor_tensor", tcv[:, :], [u[:, 1:513], w1, tcv[:, :]], ALU.mult, ALU.add)
                self.vop("dve", "scalar_tensor_tensor", tcv[:, :], [u[:, 2:514], w2, tcv[:, :]], ALU.mult, ALU.add)
                self.vop("dve", "tensor_tensor", vT[:, ct, :], [self.bank(base), tcv[:, :]], ALU.mult)
            done(3)
            gaP = [None, None]
            gcP = [None, None]
            gaP[0] = get_panel()[:, 0]
            gcP[0] = get_panel()[:, 0]
            aoP = get_panel()
            coP = get_panel()
            for dt in range(8):
                hf = dt // 4
                cs = slice((dt % 4) * 128, (dt % 4) * 128 + 128)
                if dt == 4:
                    done(2)
                    gaP[1] = get_panel()[:, 0]
                    gcP[1] = get_panel()[:, 0]
                base = (dt % 2) * 4
                for kc in range(8):
                    self.mm(self.bank(base + 2), gaP[hf][:, kc, cs], hT[:, kc, :], kc == 0, kc == 7)
                for kc in range(8):
                    self.mm(self.bank(base + 3), gcP[hf][:, kc, cs], hT[:, kc, :], kc == 0, kc == 7)
                for pr in range(4):
                    self.mm(self.bank(base), aoP[:, hf, pr, cs], OTi[:, pr, :], pr == 0, pr == 3)
                for ct in range(4):
                    self.mm(self.bank(base + 1), coP[:, hf, ct, cs], vT[:, ct, :], ct == 0, ct == 3)
                self.act(gas[dt % 2][:, :], self.bank(base + 2), AF.Sigmoid, bias=self.bg[:, dt:dt + 1])
                self.act(gcs[dt % 2][:, :], self.bank(base + 3), AF.Sigmoid, bias=self.bg[:, 8 + dt:9 + dt])
                self.vop("dve", "tensor_tensor", t1[:, :], [self.bank(base), gas[dt % 2][:, :]], ALU.mult)
                self.vop("dve", "tensor_tensor", t2[:, :], [self.bank(base + 1), gcs[dt % 2][:, :]], ALU.mult)
                self.vop("pool", "tensor_tensor", mixT[:, dt, :], [t1[:, :], t2[:, :]], ALU.add)
            done(4)
            oP = [get_panel()[:, 0], get_panel()[:, 0]]
            tA = [tmp[0][:, :], tmp[1][:, :], fs[:, 0:2, :].rearrange("p a c -> p (a c)"), fs[:, 2:4, :].rearrange("p a c -> p (a c)")]
            blocks = []
            for tb in range(4):
                pb = (tb % 3) * 2
                ts_ = slice(tb * 128, (tb + 1) * 128)
                for hf in range(2):
                    for kc in range(8):
                        self.mm(self.bank(pb + hf), mixT[:, kc, ts_], oP[hf][:, kc, :], kc == 0, kc == 7)
                c = col()
                self.act(junk[:, :], self.bank(pb, 2), AF.Square, accum=ss[:, c:c + 1])
                self.vop("dve", "tensor_tensor", tA[tb], [self.bank(pb, 2), self.gpm[:, :]], ALU.mult)
                blocks.append((X[:, tb, :], [(tA[tb], 0, D)], ss[:, c:c + 1]))
            xb5 = chain(blocks)
            done(2)
            if self.debug and i == 0:
                self.dma("sp", self.dbg_x1.ap().rearrange("(t p) d -> p t d", p=128), X[:, :, :], "dbgx1")
                self.dma("sp", self.dbg_mix.ap(), mixT[:, :, :], "dbgmix")
                self.dma("sp", self.dbg_vT.ap(), vT[:, :, :], "dbgvt")
            self.vop("pool", "tensor_copy", pbf[:, :, :], [pst[:, :, :]])
            for tb in range(4):
                pbk = self.Pb[:, 1024 * 7:1024 * 8]
                for kc in range(2):
                    self.tr(pbk[:, kc * 128:(kc + 1) * 128], pbf[:, tb, kc * 128:(kc + 1) * 128], self.ident)
                self.vop("dve", "tensor_copy", pT[:, :, tb * 128:(tb + 1) * 128],
                         [pbk[:, 0:256].rearrange("p (k t) -> p k t", k=2)])
            for tb in range(4):
                post(xb5[tb], lambda tb=tb: hB[:, :, tb * 128:(tb + 1) * 128])
            for g in range(8):
                Wp = get_panel()[:, 0]
                for q in range(4):
                    ft = g * 4 + q
                    bk = self.bank(ft % 6)
                    for kc in range(8):
                        self.mm(bk, Wp[:, kc, q * 128:(q + 1) * 128], hB[:, kc, :], kc == 0, kc == 7)
                    self.act(rr[ft % 2][:, :], bk, AF.Relu)
                    self.vop("dve", "tensor_tensor", fT[:, ft, :], [bk, rr[ft % 2][:, :]], ALU.mult)
                done(1)
                if g < 4:
                    c1_pre(i + 1, g)
                if 1 <= g <= 4:
                    c1_post(i + 1, g - 1)
            blocks6 = []
            for hf in range(2):
                for g in range(4):
                    Wp = get_panel()[:, 0]
                    for tb in range(4):
                        for kc in range(8):
                            self.mm(self.bank(hf * 4 + tb), fT[:, g * 8 + kc, tb * 128:(tb + 1) * 128], Wp[:, kc, :],
                                    g == 0 and kc == 0, g == 3 and kc == 7)
                    done(1)
                for tb in range(4):
                    bk = self.bank(hf * 4 + tb)
                    if hf == 0:
                        self.act(junk[:, 0:512], bk, AF.Square, accum=ssb[:, tb:tb + 1])
                        self.vop("dve", "tensor_tensor", fs[:, tb, :], [bk, self.gpl[:, 0:512]], ALU.mult)
                    else:
                        c = col()
                        self.act(junk[:, 0:512], bk, AF.Square, accum=ss[:, c:c + 1])
                        self.vop("dve", "tensor_tensor", ss[:, c:c + 1], [ss[:, c:c + 1], ssb[:, tb:tb + 1]], ALU.add)
                        self.vop("dve", "tensor_tensor", G[tb][:, :], [bk, self.gpl[:, 512:1024]], ALU.mult)
                        blocks6.append((X[:, tb, :], [(fs[:, tb, :], 0, 512), (G[tb][:, :], 512, 512)], ss[:, c:c + 1]))
            xb6 = chain(blocks6)
            if self.debug and i == 0:
                self.dma("sp", self.dbg_x2.ap().rearrange("(t p) d -> p t d", p=128), X[:, :, :], "dbgx2")
            pgP = [get_panel()[:, 0], get_panel()[:, 0]]
            ppP = get_panel()
            k = 0
            for tb in range(4):
                ts_ = slice(tb * 128, (tb + 1) * 128)
                post(xb6[tb], lambda tb=tb: hB[:, :, tb * 128:(tb + 1) * 128])
                tp = tmp[tb % 2]
                for hf in range(2):
                    gb = self.bank(k % 4)
                    pb_ = self.bank(4 + k % 2)
                    for kc in range(8):
                        self.mm(gb, hB[:, kc, ts_], pgP[hf][:, kc, :], kc == 0, kc == 7)
                    for kc in range(2):
                        self.mm(pb_, pT[:, kc, ts_], ppP[:, hf, kc, :], kc == 0, kc == 1)
                    self.act(gsb[k % 2][:, :], gb, AF.Sigmoid)
                    self.vop("dve", "tensor_tensor", tp[:, hf * 512:(hf + 1) * 512], [pb_, gsb[k % 2][:, :]], ALU.mult)
                    k += 1
                self.vop("pool", "tensor_tensor", X[:, tb, :], [X[:, tb, :], tp[:, :]], ALU.add)
            done(3)
            self.dma("sp", self.out[tok, :].rearrange("(t p) d -> p t d", p=128), X[:, :, :], ("outC", i % 2))
        A.release(m)

    def build(self):
        self.declare()
        with contextlib.ExitStack() as stack:
            self.phase_const()
            first, rest = self.weight_pieces()
            self.phase_A(lambda: self.phase_W1(first))
            self.s.barrier()
            self.start_W2(rest)
            self.phase_B()
            self.step_W2(len(rest))
            self.dma("sp", self.ot_s.ap(), self.OT[:, :, :], "otspill")
            if self.debug:
                self.dma("sp", self.dbg_ot.ap(), self.OT[:, :, :], "dbg0")
                self.dma("sp", self.dbg_qt.ap(), self.QT[:, :, :], "dbg1")
            self.s.barrier()
            self.A.release(self.mAB)
            self.phase_C()
            self.s.barrier()
            self.s.emit(self.nc, stack)
        return self.nc


_CACHE = {}


def _get_nc(S, debug=False):
    key = (S, debug)
    if key not in _CACHE:
        _CACHE[key] = Builder(S, debug).build()
    return _CACHE[key]


def _host_consts():
    cm = np.zeros((128, 3 * 128 + 4), np.float32)
    cm[:, 0:128] = np.eye(128, dtype=np.float32)
    j = np.arange(128)[:, None]
    s = np.arange(128)[None, :]
    cm[:, 128:256] = np.where(j >= s, -1.0, 0.0)
    cm[:, 256:384] = 1.0
    cm[:, 384:386] = 1.0
    cm[1, 386] = 1.0
    cm[33, 386] = 1.0
    return cm


def _host_masks(h):
    m = np.arange(8)[None, :, None]
    s = np.arange(128)[:, None, None]
    q = np.arange(512)[None, None, :]
    mk = np.where(m * 128 + s >= h * 512 + q, NEG, 0.0).astype(np.float32)
    return np.ascontiguousarray(mk.reshape(128, 8 * 512))


def make_in_maps(inputs):
    x = np.asarray(inputs["x"], np.float32)
    p = np.asarray(inputs["p"], np.float32)[0]
    B, S, _ = x.shape
    NCH = S // 512
    NSLOT = NCH // 2
    f = lambda k: np.ascontiguousarray(np.asarray(inputs[k], np.float32)[0])
    gvec = np.concatenate([f(k).reshape(8, 128).T for k in ("g_pre_mix", "g_pre_mlp", "g_ple")], axis=1)
    gpost = np.stack([f("g_post_mix"), f("g_post_mlp")])
    bgate = f("b_gate").reshape(16, 128).T
    wconv = f("w_conv").reshape(3, 4, 128).transpose(2, 1, 0).reshape(128, 12)
    common = {
        "w_in": f("w_in"), "w_attn_out": f("w_attn_out"), "w_conv_out": f("w_conv_out"), "w_o": f("w_o"),
        "w_up": f("w_up"), "w_down": f("w_down"), "w_ple_gate": f("w_ple_gate"), "w_ple_proj": f("w_ple_proj"),
        "gvec": np.ascontiguousarray(gvec), "gpost": np.ascontiguousarray(gpost),
        "bgate": np.ascontiguousarray(bgate), "wconv": np.ascontiguousarray(wconv), "cmat": _host_consts(),
    }
    maps = []
    for core in range(2 * B):
        b, h = core // 2, core % 2
        xc = x[b].reshape(NCH, 512, D)
        halo = np.zeros((16, D), np.float32)
        for i in range(NSLOT):
            c = 2 * i + h
            if c > 0:
                halo[2 * i:2 * i + 2] = xc[c - 1, 510:512]
        mp = dict(common)
        mp["x_seq"] = np.ascontiguousarray(x[b])
        mp["x_own"] = np.ascontiguousarray(xc[h::2].reshape(NSLOT * 512, D))
        mp["x_halo"] = halo
        mp["p_own"] = np.ascontiguousarray(p[b].reshape(NCH, 512, PLE)[h::2].reshape(NSLOT * 512, PLE))
        mp["masks"] = _host_masks(h)
        maps.append(mp)
    return maps


def kernel(**inputs):
    x = np.asarray(inputs["x"])
    B, S, _ = x.shape
    nc = _get_nc(S)
    maps = make_in_maps(inputs)
    res = run_bass_kernel_spmd(nc, maps, core_ids=list(range(2 * B)))
    NCH = S // 512
    out = np.empty((B, NCH, 512, D), np.float32)
    for core in range(2 * B):
        b, h = core // 2, core % 2
        out[b, h::2] = np.asarray(res.results[core]["out"]).reshape(NCH // 2, 512, D)
    return out.reshape(B, S, D)
```

```python
import contextlib
import numpy as np
import concourse.bass as bass
import concourse.mybir as mybir
from concourse.bass_utils import run_bass_kernel_spmd

F32 = mybir.dt.float32
BF16 = mybir.dt.bfloat16
AF = mybir.ActivationFunctionType
ALU = mybir.AluOpType
AX = mybir.AxisListType

DT_SIZE = {F32: 4, BF16: 2}


class Op:
    __slots__ = ("eng", "fn", "dma_key", "deps", "signal", "idx", "sigval", "name")

    def __init__(self, eng, fn, dma_key, name):
        self.eng = eng
        self.fn = fn
        self.dma_key = dma_key
        self.deps = []
        self.signal = False
        self.idx = -1
        self.sigval = 0
        self.name = name


def _footprint(ap):
    t = ap.tensor
    esz = DT_SIZE[t.dtype] if t.dtype in DT_SIZE else mybir.dt.size(t.dtype)
    shape = [int(s) for s in t.shape]
    rowlen = 1
    for s in shape[1:]:
        rowlen *= s
    pat = [[int(a), int(b)] for a, b in ap.ap]
    off = int(ap.offset)
    is_psum = "PSum" in type(t).__name__
    base = 0
    if not is_psum:
        base = int(t.manual_sbuf_range[0])
    pstride, pcount = pat[0]
    assert pstride == rowlen or pcount == 1, (pat, rowlen, t.name)
    p0 = off // rowlen + int(t.base_partition)
    p1 = p0 + pcount
    foff = off % rowlen
    dims = [(s, c) for s, c in pat[1:] if c > 1]
    if not dims:
        runs = [(foff, foff + 1)]
    else:
        dims_sorted = dims
        inner_s, inner_c = dims_sorted[-1]
        if inner_s == 1:
            run = inner_c
            outer = dims_sorted[:-1]
        else:
            run = 1
            outer = dims_sorted
        starts = [foff]
        nruns = 1
        for s, c in outer:
            nruns *= c
        if nruns > 64:
            ext = foff + sum((c - 1) * abs(s) for s, c in dims) + 1
            runs = [(foff, ext)]
        else:
            for s, c in outer:
                starts = [st + k * s for st in starts for k in range(c)]
            runs = sorted((st, st + run) for st in starts)
            merged = []
            for a, b in runs:
                if merged and a <= merged[-1][1]:
                    merged[-1] = (merged[-1][0], max(merged[-1][1], b))
                else:
                    merged.append((a, b))
            runs = merged
    runs = [(base + a * esz, base + b * esz) for a, b in runs]
    if is_psum:
        banks = sorted({b // 2048 for a, e in runs for b in range(a, e, 512)} | {(e - 1) // 2048 for a, e in runs})
        runs = [(bk * 2048, bk * 2048 + 2048) for bk in banks]
        return ("P", 0, 128, runs)
    return ("S", p0, p1, runs)


class Sched:
    COMPUTE = ("pe", "act", "dve", "pool")
    PAGE = 2048

    def __init__(self):
        self.ops = {e: [] for e in ("pe", "act", "dve", "pool", "sp")}
        self.recs = {}
        self.allrecs = []
        self.keyw = {}
        self.keyr = {}
        self.dma_count = {}

    def _pages(self, space, runs):
        pg = set()
        for a, b in runs:
            for p in range(a // self.PAGE, (b - 1) // self.PAGE + 1):
                pg.add((space, p))
        return pg

    @staticmethod
    def _overlap(r, space, p0, p1, runs):
        if r[1] != space or r[3] <= p0 or p1 <= r[2]:
            return False
        for a, b in runs:
            for c, d in r[4]:
                if a < d and c < b:
                    return True
        return False

    @staticmethod
    def _covers(runs_outer, runs_inner):
        for c, d in runs_inner:
            ok = False
            for a, b in runs_outer:
                if a <= c and d <= b:
                    ok = True
                    break
            if not ok:
                return False
        return True

    def _access(self, op, item, is_write):
        if isinstance(item, tuple) and not hasattr(item, "tensor"):
            key = item
            w = self.keyw.get(key)
            if w is not None:
                op.deps.append(w)
            if is_write:
                for r in self.keyr.get(key, ()):
                    op.deps.append(r)
                self.keyw[key] = op
                self.keyr[key] = []
            else:
                self.keyr.setdefault(key, []).append(op)
            return
        space, p0, p1, runs = _footprint(item)
        excl = is_write or space == "P"
        pages = self._pages(space, runs)
        seen = set()
        for pg in pages:
            lst = self.recs.get(pg)
            if not lst:
                continue
            keep = []
            for r in lst:
                if not r[6]:
                    continue
                keep.append(r)
                if id(r) in seen:
                    continue
                seen.add(id(r))
                if not self._overlap(r, space, p0, p1, runs):
                    continue
                if r[0] == "w" or excl:
                    op.deps.append(r[5])
                if excl and r[2] >= p0 and r[3] <= p1 and self._covers(runs, r[4]):
                    r[6] = False
                elif (not excl) and r[0] == "r" and r[5].eng == op.eng and r[5].dma_key is None \
                        and r[2] == p0 and r[3] == p1 and r[4] == runs:
                    r[6] = False
            self.recs[pg] = [r for r in keep if r[6]]
        rec = ["w" if excl else "r", space, p0, p1, runs, op, True]
        for pg in pages:
            self.recs.setdefault(pg, []).append(rec)

    def add(self, eng, fn, reads=(), writes=(), dma_key=None, name=""):
        op = Op(eng, fn, dma_key, name)
        for it in reads:
            self._access(op, it, False)
        for it in writes:
            self._access(op, it, True)
        op.idx = len(self.ops[eng])
        self.ops[eng].append(op)
        if dma_key is not None:
            c = self.dma_count.get(dma_key, 0) + 16
            self.dma_count[dma_key] = c
            op.sigval = c
        return op

    def barrier(self):
        lasts = []
        for e, lst in self.ops.items():
            if e == "sp":
                continue
            real = [o for o in lst if o.fn is not None and o.dma_key is None]
            if real:
                lasts.append(real[-1])
        dm = {}
        for e, lst in self.ops.items():
            for o in lst:
                if o.dma_key is not None:
                    dm[o.dma_key] = o
        for e in self.ops:
            op = Op(e, None, None, "barrier")
            op.deps = list(lasts) + list(dm.values())
            op.idx = len(self.ops[e])
            self.ops[e].append(op)
        self.recs = {}
        self.keyw = {}
        self.keyr = {}

    def finalize(self):
        for e, lst in self.ops.items():
            for op in lst:
                nd = []
                for d in op.deps:
                    if d is op:
                        continue
                    if d.dma_key is None and d.eng == op.eng:
                        if e == "pe" or e == "sp":
                            continue
                    nd.append(d)
                op.deps = nd
                for d in nd:
                    d.signal = True
        self.dma_sems = {}
        for e, lst in self.ops.items():
            cnt = 0
            for op in lst:
                if op.fn is None:
                    continue
                if op.dma_key is not None:
                    pass
                elif op.signal:
                    cnt += 1
                    op.sigval = cnt

    def emit(self, nc, stack):
        self.finalize()
        sems = {}
        for e in self.COMPUTE:
            sems[e] = stack.enter_context(nc.semaphore("s_" + e))
        for n_, k in enumerate(self.dma_count):
            sems[("dma", k)] = stack.enter_context(nc.semaphore("d_%d" % n_))
        block = stack.enter_context(nc.Block())
        sched = self

        def run(engname, eng):
            seen = {}
            for op in sched.ops[engname]:
                need = {}
                for d in op.deps:
                    s = sems[("dma", d.dma_key)] if d.dma_key is not None else sems[d.eng]
                    sk = id(s)
                    v = d.sigval
                    if seen.get(sk, 0) >= v:
                        continue
                    if need.get(sk, (None, 0))[1] < v:
                        need[sk] = (s, v)
                if op.fn is not None and op.dma_key is not None and op.sigval > 16:
                    s = sems[("dma", op.dma_key)]
                    if seen.get(id(s), 0) < op.sigval - 16 and need.get(id(s), (None, 0))[1] < op.sigval - 16:
                        need[id(s)] = (s, op.sigval - 16)
                for sk, (s, v) in need.items():
                    eng.wait_ge(s, v)
                    seen[sk] = v
                if op.fn is None:
                    continue
                ins = op.fn(eng)
                if op.dma_key is not None:
                    ins.then_inc(sems[("dma", op.dma_key)], 16)
                elif op.signal:
                    ins.then_inc(sems[engname], 1)

        @block.tensor
        def _(eng):
            run("pe", eng)

        @block.scalar
        def _(eng):
            run("act", eng)

        @block.vector
        def _(eng):
            run("dve", eng)

        @block.gpsimd
        def _(eng):
            run("pool", eng)

        @block.sync
        def _(eng):
            run("sp", eng)


class Alloc:
    def __init__(self, nc, base=17408, limit=228352):
        self.nc = nc
        self.off = base
        self.limit = limit
        self.n = 0

    def __call__(self, name, shape, dtype):
        size = DT_SIZE[dtype]
        for s in shape[1:]:
            size *= s
        size = (size + 63) // 64 * 64
        assert self.off + size <= self.limit, (name, self.off, size)
        t = self.nc.alloc_sbuf_tensor_at(f"{name}_{self.n}", list(shape), dtype, offset=self.off)
        self.n += 1
        self.off += size
        return t

    def alias(self, name, shape, dtype, base):
        size = DT_SIZE[dtype]
        for v in shape[1:]:
            size *= v
        lo, hi = base.manual_sbuf_range
        assert size <= hi - lo, (name, size, hi - lo)
        t = self.nc.alloc_sbuf_tensor_at(f"{name}_{self.n}", list(shape), dtype, offset=int(lo))
        self.n += 1
        return t

    def mark(self):
        return self.off

    def release(self, m):
        self.off = m


D = 1024
DIN = 5120
DFF = 4096
PLE = 256
EPS = 1e-6
NEG = -30000.0


class Builder:
    def __init__(self, S, debug=False):
        self.S = S
        self.NCH = S // 512
        self.NSLOT = self.NCH // 2
        self.NKB = S // 128
        self.TOWN = self.NSLOT * 512
        self.debug = debug
        self.nc = bass.Bass("TRN2", target_bir_lowering=False)
        self.s = Sched()
        self.A = Alloc(self.nc)
        self.rr = 0

    def mm(self, out, lhsT, rhs, start, stop, skip=False):
        self.s.add("pe", lambda e, o=out, l=lhsT, r=rhs, a=start, b=stop, k=skip:
                   e.matmul(o, l, r, start=a, stop=b, skip_group_check=k),
                   reads=[lhsT, rhs], writes=[out], name="mm")

    def tr(self, out, in_, ident):
        self.s.add("pe", lambda e, o=out, i=in_, d=ident: e.transpose(o, i, d),
                   reads=[in_, ident], writes=[out], name="tr")

    def act(self, out, in_, func, bias=None, scale=None, accum=None, extra_reads=()):
        kw = {}
        rd = [in_] + list(extra_reads)
        wr = [out]
        if bias is not None:
            kw["bias"] = bias
            if not isinstance(bias, float):
                rd.append(bias)
        if scale is not None:
            kw["scale"] = scale
            if not isinstance(scale, float):
                rd.append(scale)
        if accum is not None:
            kw["accum_out"] = accum
            wr.append(accum)
        self.s.add("act", lambda e, o=out, i=in_, f=func, k=kw: e.activation(o, i, f, **k),
                   reads=rd, writes=wr, name="act")

    def vop(self, eng, method, out, ins, *args, extra_writes=(), **kw):
        rd = [a for a in ins if hasattr(a, "tensor")]
        rd += [a for a in args if hasattr(a, "tensor")]
        self.s.add(eng, lambda e, m=method, o=out, i=tuple(ins), a=args, k=kw: getattr(e, m)(o, *i, *a, **k),
                   reads=rd, writes=[out] + list(extra_writes), name=method)

    def dma(self, q, out, in_, key, reads=(), writes=()):
        rd = list(reads)
        wr = list(writes)
        if "DRam" not in type(in_.tensor).__name__:
            rd.append(in_)
        if "DRam" not in type(out.tensor).__name__:
            wr.append(out)
        self.s.add(q, lambda e, o=out, i=in_: e.dma_start(out=o, in_=i), reads=rd, writes=wr,
                   dma_key=(q, key), name="dma")

    def declare(self):
        nc, S, TOWN = self.nc, self.S, self.TOWN
        I = lambda n, sh: nc.dram_tensor(n, sh, F32, kind="ExternalInput")
        self.x_seq = I("x_seq", [S, D])
        self.x_own = I("x_own", [TOWN, D])
        self.x_halo = I("x_halo", [16, D])
        self.p_own = I("p_own", [TOWN, PLE])
        self.w = {
            "in": I("w_in", [D, DIN]), "ao": I("w_attn_out", [512, D]), "co": I("w_conv_out", [512, D]),
            "o": I("w_o", [D, D]), "up": I("w_up", [D, DFF]), "down": I("w_down", [DFF, D]),
            "pg": I("w_ple_gate", [D, D]), "pp": I("w_ple_proj", [PLE, D]),
        }
        self.gvec = I("gvec", [128, 24])
        self.gpost = I("gpost", [2, D])
        self.bgate = I("bgate", [128, 16])
        self.wconv = I("wconv", [128, 12])
        self.cmat = I("cmat", [128, 3 * 128 + 4])
        self.masks = I("masks", [128, 8 * 512])
        self.out = nc.dram_tensor("out", [TOWN, D], F32, kind="ExternalOutput")
        self.ws = {}
        for k, t in self.w.items():
            K, N = int(t.shape[0]), int(t.shape[1])
            self.ws[k] = nc.dram_tensor("ws_" + k, [N // 512, 128, K // 128, 512], BF16, kind="Internal")
        self.kt_s = nc.dram_tensor("kt_s", [4, 128, S], BF16, kind="Internal")
        self.v_s = nc.dram_tensor("v_s", [4, 128, self.NKB, 128], BF16, kind="Internal")
        self.ot_s = nc.dram_tensor("ot_s", [128, 4, TOWN], BF16, kind="Internal")
        self.P = nc.alloc_psum_tensor("P", [128, 4096], F32)
        self.Pb = self.P.bitcast(BF16)
        if self.debug:
            self.dbg_ot = nc.dram_tensor("dbg_ot", [128, 4, TOWN], BF16, kind="ExternalOutput")
            self.dbg_qt = nc.dram_tensor("dbg_qt", [128, 4, TOWN], BF16, kind="ExternalOutput")
            self.dbg_kt = nc.dram_tensor("dbg_kt", [4, 128, S], BF16, kind="ExternalOutput")
            self.dbg_v = nc.dram_tensor("dbg_v", [4, 128, self.NKB, 128], BF16, kind="ExternalOutput")
            self.dbg_x1 = nc.dram_tensor("dbg_x1", [512, D], F32, kind="ExternalOutput")
            self.dbg_x2 = nc.dram_tensor("dbg_x2", [512, D], F32, kind="ExternalOutput")
            self.dbg_mix = nc.dram_tensor("dbg_mix", [128, 8, 512], BF16, kind="ExternalOutput")
            self.dbg_vT = nc.dram_tensor("dbg_vT", [128, 4, 512], BF16, kind="ExternalOutput")

    def bank(self, b, n=1):
        return self.P[:, 512 * b:512 * (b + n)]

    def phase_const(self):
        A = self.A
        TOWN = self.TOWN
        self.cm = A("cm", [128, 3 * 128 + 4], BF16)
        self.ident = self.cm[:, 0:128]
        self.negL = self.cm[:, 128:256]
        self.posones = self.cm[:, 256:384]
        self.onescol = self.cm[:, 384:386]
        self.selc = self.cm[:, 386:387]
        self.gv = A("gv", [128, 24], F32)
        self.bg = A("bg", [128, 16], F32)
        self.wc = A("wc", [128, 12], F32)
        self.gpm = A("gpm", [128, D], F32)
        self.gpl = A("gpl", [128, D], F32)
        self.halfc = A("halfc", [128, 8], F32)
        self.hT_halo = A("hT_halo", [128, 8, 16], BF16)
        self.mAB = A.mark()
        self.OT = A("OT", [128, 4, TOWN], BF16)
        self.QT = A("QT", [128, 4, TOWN], BF16)
        self.maskf = A("maskf", [128, 8, 512], F32)
        m = A.mark()
        st = A("cst", [128, 4096], F32)
        self.dma("sp", st[:, 0:388], self.cmat.ap(), "c0")
        self.vop("dve", "tensor_copy", self.cm[:, :], [st[:, 0:388]])
        self.dma("sp", self.gv[:, :], self.gvec.ap(), "c2")
        self.dma("sp", self.bg[:, :], self.bgate.ap(), "c2")
        self.dma("sp", self.wc[:, :], self.wconv.ap(), "c2")
        gp = self.gpost.ap()
        self.dma("sp", self.gpm[:, :], gp[0:1, :].broadcast_to([128, D]), "c3")
        self.dma("sp", self.gpl[:, :], gp[1:2, :].broadcast_to([128, D]), "c3")
        self.vop("dve", "tensor_scalar", self.gpm[:, :], [self.gpm[:, :]], 32.0, None, ALU.mult)
        self.vop("dve", "tensor_scalar", self.gpl[:, :], [self.gpl[:, :]], 32.0, None, ALU.mult)
        self.vop("pool", "memset", self.halfc[:, :], [], -0.5)
        A.release(m)

    def weight_pieces(self):
        gain_col = {"in": 0, "up": 8, "pg": 16}
        first, rest = [], []
        for k in ("in", "ao", "co", "o", "up", "down", "pg", "pp"):
            w = self.w[k]
            K, N = int(w.shape[0]), int(w.shape[1])
            for rc in range(K // 128):
                for c0 in range(0, N, 2048):
                    cw = min(2048, N - c0)
                    g = gain_col.get(k)
                    (first if (k == "in" and c0 == 0) else rest).append((k, rc, c0, cw, g))
        return first, rest

    def phase_W1(self, pieces):
        A = self.A
        m = A.mark()
        NB = 2
        ot_lo, ot_hi = (int(v) for v in self.OT.manual_sbuf_range)
        stf, stb = [], []
        for i in range(NB):
            if ot_hi - ot_lo >= NB * 12288:
                stf.append(self.nc.alloc_sbuf_tensor_at(f"wst{i}", [128, 2048], F32, offset=ot_lo + i * 12288))
                stb.append(self.nc.alloc_sbuf_tensor_at(f"wsb{i}", [128, 2048], BF16, offset=ot_lo + i * 12288 + 8192))
            else:
                stf.append(A(f"wst{i}", [128, 2048], F32))
                stb.append(A(f"wsb{i}", [128, 2048], BF16))
        for n, (k, rc, c0, cw, g) in enumerate(pieces):
            w = self.w[k]
            T = self.ws[k].ap().rearrange("n p k c -> p n k c")
            b = n % NB
            self.dma("sp", stf[b][:, 0:cw], w[rc * 128:(rc + 1) * 128, c0:c0 + cw], ("wf", b))
            gap = self.gv[:, g + rc:g + rc + 1]
            if n % 2 == 0:
                self.vop("dve", "tensor_scalar", stb[b][:, 0:cw], [stf[b][:, 0:cw]], gap, None, ALU.mult)
            else:
                self.act(stb[b][:, 0:cw], stf[b][:, 0:cw], AF.Identity, scale=gap)
            self.dma("sp", T[:, c0 // 512:(c0 + cw) // 512, rc, :],
                     stb[b][:, 0:cw].rearrange("p (a c) -> p a c", c=512), ("wb", b), writes=[("w1", n)])
        self.w1_keys = [("w1", n) for n in range(len(pieces))]
        A.release(m)

    def start_W2(self, pieces):
        A = self.A
        self.w2 = dict(p=pieces, n=0, stf=[A(f"w2f{i}", [128, 2048], F32) for i in range(2)],
                       stb=[A(f"w2b{i}", [128, 2048], BF16) for i in range(2)])

    def _w2_load(self, n):
        w2 = self.w2
        if n >= len(w2["p"]):
            return
        k, rc, c0, cw, g = w2["p"][n]
        self.dma("pool", w2["stf"][n % 2][:, 0:cw], self.w[k][rc * 128:(rc + 1) * 128, c0:c0 + cw], ("w2f", n % 2))

    def step_W2(self, count=1):
        w2 = self.w2
        for _ in range(count):
            n = w2["n"]
            if n >= len(w2["p"]):
                return
            if n == 0:
                self._w2_load(0)
            self._w2_load(n + 1)
            k, rc, c0, cw, g = w2["p"][n]
            b = n % 2
            src, dst = w2["stf"][b][:, 0:cw], w2["stb"][b][:, 0:cw]
            if g is not None:
                self.vop("pool", "tensor_scalar", dst, [src], self.gv[:, g + rc:g + rc + 1], None, ALU.mult)
            else:
                self.vop("pool", "tensor_copy", dst, [src])
            T = self.ws[k].ap().rearrange("n p k c -> p n k c")
            self.dma("pool", T[:, c0 // 512:(c0 + cw) // 512, rc, :], dst.rearrange("p (a c) -> p a c", c=512),
                     ("w2b", b))
            w2["n"] += 1

    def load_panel(self, q, dst, k, n0, n1, k0, k1, key):
        T = self.ws[k].ap().rearrange("n p k c -> p n k c")
        self.dma(q, dst, T[:, n0:n1, k0:k1, :], key)

    def norm_pre(self, xblk, ntok, ss, rstd, xn, junk, xn_eng="pool"):
        self.act(junk[0:ntok, :], xblk, AF.Square, accum=ss[0:ntok, :])
        self.vop("dve", "tensor_scalar", rstd[0:ntok, :], [ss[0:ntok, :]], 1024.0 * EPS, None, ALU.add)
        self.vop("pool", "tensor_tensor", rstd[0:ntok, :], [rstd[0:ntok, :], self.halfc[0:ntok, 0:1]], ALU.pow)
        self.vop(xn_eng, "tensor_scalar", xn[0:ntok, :], [xblk], rstd[0:ntok, :], 32.0, ALU.mult, ALU.mult)

    def norm_post(self, ntok, hT_dst_fn, xn, pbank, evac_eng):
        pb = self.Pb[:, 1024 * pbank:1024 * (pbank + 1)]
        for d in range(8):
            self.tr(pb[:, d * 128:d * 128 + ntok], xn[0:ntok, d * 128:(d + 1) * 128], self.ident[0:ntok, 0:ntok])
        src = pb.rearrange("p (d t) -> p d t", d=8)[:, :, 0:ntok]
        dst = hT_dst_fn()
        if evac_eng == "act":
            self.act(dst, src, AF.Identity)
        else:
            self.vop(evac_eng, "tensor_copy", dst, [src])

    def norm_T(self, xblk, ntok, hT_dst_fn, ss, rstd, xn, junk, pbank, evac_eng, xn_eng="pool"):
        self.norm_pre(xblk, ntok, ss, rstd, xn, junk, xn_eng)
        self.norm_post(ntok, hT_dst_fn, xn, pbank, evac_eng)

    def phase_A(self, mid):
        A, S, NCH, NSLOT = self.A, self.S, self.NCH, self.NSLOT
        m = A.mark()
        wqkv = A("wqkv", [128, 3, 8, 512], BF16)
        xs = [A(f"xsA{i}", [128, 4, D], F32) for i in range(2)]
        xn = [A(f"xnA{i}", [128, D], BF16) for i in range(4)]
        hT = [A(f"hTA{i}", [128, 8, 512], BF16) for i in range(2)]
        kst = [A(f"kst{i}", [128, 4, 512], BF16) for i in range(2)]
        vst = [A(f"vst{i}", [128, 4, 4, 128], BF16) for i in range(2)]
        junk = A("junkA", [128, D], BF16)
        ss = A("ssA", [128, 8], F32)
        rstd = A("rstdA", [128, 8], F32)
        KT = self.kt_s.ap().rearrange("r p s -> p r s")
        VS = self.v_s.ap().rearrange("r p k d -> p r k d")
        jobs = [("kv", c) for c in range(NCH)] + [("q", i) for i in range(NSLOT)]
        NJ = len(jobs)

        def load(ji):
            if ji >= NJ:
                return
            kind, c = jobs[ji]
            src = self.x_seq if kind == "kv" else self.x_own
            self.dma("sp", xs[ji % 2][:, :, :],
                     src[c * 512:(c + 1) * 512, :].rearrange("(t p) d -> p t d", p=128), ("xsA", ji % 2))

        def pre(ji, tb):
            if ji >= NJ:
                return
            k = (ji * 4 + tb)
            self.norm_pre(xs[ji % 2][:, tb, :], 128, ss[:, k % 8:k % 8 + 1], rstd[:, k % 8:k % 8 + 1], xn[k % 4], junk)

        def post(ji, tb):
            if ji >= NJ:
                return
            k = (ji * 4 + tb)
            self.norm_post(128, lambda: hT[ji % 2][:, :, tb * 128:(tb + 1) * 128], xn[k % 4], 6 + k % 2, "act")

        load(0)
        for tb in range(4):
            pre(0, tb)
            post(0, tb)
        mid()
        load(1)
        self.dma("sp", self.maskf[:, :, :], self.masks.ap().rearrange("p (a b) -> p a b", a=8), "c1")
        T_in = self.ws["in"].ap().rearrange("n p k c -> p n k c")
        self.dma("sp", wqkv[:, :, :, :], T_in[:, 0:3, 0:8, :], "wqkv", reads=self.w1_keys)
        for ji, (kind, c) in enumerate(jobs):
            b = ji % 2
            h = hT[b]

            def weave(g):
                if g < 4:
                    pre(ji + 1, g)
                if 1 <= g <= 4:
                    post(ji + 1, g - 1)
                if g == 4:
                    load(ji + 2)

            if kind == "kv":
                for ft in range(4):
                    pbk = self.bank(ft % 4)
                    for kc in range(8):
                        self.mm(pbk, wqkv[:, 1, kc, ft * 128:(ft + 1) * 128], h[:, kc, :], kc == 0, kc == 7)
                    self.vop("dve", "tensor_copy", kst[b][:, ft, :], [pbk])
                    weave(ft)
                self.dma("sp", KT[:, :, c * 512:(c + 1) * 512], kst[b][:, :, :], ("kst", b))
                for tb in range(4):
                    pbk = self.bank(4 + tb % 2)
                    for kc in range(8):
                        self.mm(pbk, h[:, kc, tb * 128:(tb + 1) * 128], wqkv[:, 2, kc, :], kc == 0, kc == 7)
                    dst = vst[b][:, :, tb, :]
                    srcp = pbk.rearrange("p (r d) -> p r d", r=4)
                    self.vop("dve", "tensor_copy", dst, [srcp])
                    weave(4 + tb)
                self.dma("sp", VS[:, :, 4 * c:4 * c + 4, :], vst[b][:, :, :, :], ("vst", b))
            else:
                for pr in range(4):
                    pbk = self.bank(pr % 4)
                    for kc in range(8):
                        self.mm(pbk, wqkv[:, 0, kc, pr * 128:(pr + 1) * 128], h[:, kc, :], kc == 0, kc == 7)
                    self.vop("dve", "tensor_scalar", self.QT[:, pr, c * 512:(c + 1) * 512], [pbk], 0.125, None,
                             ALU.mult)
                    weave(pr)
                weave(4)
        xh = A("xh", [16, D], F32)
        self.dma("sp", xh[:, :], self.x_halo.ap(), "xh")
        self.norm_T(xh[:, :], 16, lambda: self.hT_halo[:, :, :], ss[:, 0:1], rstd[:, 0:1], xn[0], junk, 6, "act")
        A.release(m)

    def phase_B(self):
        A, S, NSLOT, NKB = self.A, self.S, self.NSLOT, self.NKB
        m = A.mark()
        KTb = [A(f"KTb{i}", [128, S], BF16) for i in range(2)]
        Vb = [A(f"Vb{i}", [128, NKB, 128], BF16) for i in range(2)]
        U = [A(f"U{i}", [128, 1024], F32) for i in range(2)]
        SP = [A(f"SP{i}", [128, 1024], BF16) for i in range(2)]
        Wb = [A(f"Wb{i}", [128, 1024], BF16) for i in range(2)]
        HI = [A(f"HI{i}", [64, 512], BF16) for i in range(2)]
        LO = [A(f"LO{i}", [64, 512], BF16) for i in range(2)]
        CARRY = self.bank(6)
        self.vop("dve", "memset", CARRY, [], 0.0)
        tiles = []
        chain = 0
        for pr in range(4):
            for i in range(NSLOT):
                top = 8 * i + 7
                for kb in range(top, -1, -1):
                    tiles.append(dict(pr=pr, i=i, kb=kb, first=(kb == top), last=(kb == 0),
                                      mask=(kb - 8 * i if kb >= 8 * i else None), chain=chain))
                chain += 1
        N = len(tiles)
        loaded = set()

        def load_kv(pr):
            if pr in loaded or pr >= 4:
                return
            loaded.add(pr)
            self.dma("sp", KTb[pr % 2][:, :], self.kt_s[pr, :, :], ("ktb", pr % 2))
            self.dma("sp", Vb[pr % 2][:, :, :], self.v_s[pr, :, :, :], ("vb", pr % 2))

        load_kv(0)
        load_kv(1)

        def zb(n):
            return (n % 3) * 2

        OB = 7

        def PE1(n):
            t = tiles[n]
            pr, i, kb, mk = t["pr"], t["i"], t["kb"], t["mask"]
            KT = KTb[pr % 2]
            for hh in range(2):
                z = self.bank(zb(n) + hh)
                rows = slice(64 * hh, 64 * hh + 64)
                self.mm(z, KT[rows, kb * 128:(kb + 1) * 128], self.QT[rows, pr, i * 512:(i + 1) * 512],
                        True, True)
            if mk is not None:
                z2 = self.bank(zb(n), 2).rearrange("p (h q) -> p h q", h=2)
                mb = self.maskf[:, mk, :]
                self.vop("dve", "tensor_tensor", z2, [z2, mb.unsqueeze(1).to_broadcast([128, 2, 512])], ALU.add)

        def X1(n):
            self.act(U[n % 2][:, :], self.bank(zb(n), 2), AF.Exp)

        def X2(n):
            self.act(SP[n % 2][:, :], U[n % 2][:, :], AF.Ln, bias=1.0)

        def PE2(n):
            t = tiles[n]
            sp = SP[n % 2]
            last_acc = t["first"]
            for hh in range(2):
                self.mm(self.bank(zb(n) + hh), self.negL, sp[:, 512 * hh:512 * hh + 512], False, last_acc, skip=True)
            if not t["last"]:
                for hh in range(2):
                    r = slice(32 * hh, 32 * hh + 2)
                    self.mm(CARRY[r, :], self.onescol[:, 0:2], sp[:, 512 * hh:512 * hh + 512], t["first"], True,
                            skip=True)
                self.vop("dve", "tensor_copy", HI[n % 2][0:34, :], [CARRY[0:34, :]])
                self.vop("dve", "scalar_tensor_tensor", LO[n % 2][0:34, :],
                         [HI[n % 2][0:34, :], self.selc[0:34, :], CARRY[0:34, :]], ALU.mult, ALU.subtract)
            if not t["first"]:
                hb = (n - 1) % 2
                for hh in range(2):
                    r = slice(32 * hh, 32 * hh + 2)
                    self.mm(self.bank(zb(n) + hh), self.posones[r, :], LO[hb][r, :], False, True, skip=True)

        def X3(n):
            self.act(Wb[n % 2][:, :], self.bank(zb(n), 2), AF.Exp)

        def PE3(n):
            t = tiles[n]
            pr, i, kb = t["pr"], t["i"], t["kb"]
            V = Vb[pr % 2]
            for hh in range(2):
                rows = slice(64 * hh, 64 * hh + 64)
                self.mm(self.bank(OB)[rows, :], V[:, kb, 64 * hh:64 * hh + 64], Wb[n % 2][:, 512 * hh:512 * hh + 512],
                        t["first"], t["last"])
            if t["last"]:
                self.vop("dve", "tensor_copy", self.OT[:, pr, i * 512:(i + 1) * 512], [self.bank(OB)])
                if i == NSLOT - 1:
                    load_kv(pr + 2)

        for w in range(24):
            self.mm(self.bank(OB), self.ident, self.QT[:, 0, 0:512], True, True)
        for j in range(-3, N):
            if j % 8 == 0:
                self.step_W2(1)
            if 0 <= j < N:
                X3(j)
            if 0 <= j + 1 < N:
                PE2(j + 1)
            if 0 <= j + 3 < N:
                PE1(j + 3)
            if 0 <= j < N:
                PE3(j)
            if 0 <= j + 2 < N:
                X1(j + 2)
                X2(j + 2)
        A.release(m)

    def phase_C(self):
        A, NSLOT = self.A, self.NSLOT
        m = A.mark()
        R = 5
        ring = [A(f"ring{i}", [128, 8, 512], BF16) for i in range(R)]
        xs = [A(f"xsC{i}", [128, 4, D], F32) for i in range(2)]
        xn = [A(f"xnC{i}", [128, D], BF16) for i in range(4)]
        hA = [A(f"hA{i}", [128, 8, 512], BF16) for i in range(2)]
        hB = A("hB", [128, 8, 512], BF16)
        otb = [A(f"otb{i}", [128, 4, 512], BF16) for i in range(2)]
        ss = A("ssC", [128, 32], F32)
        ssb = A("ssbC", [128, 8], F32)
        rstd = A("rstdC", [128, 32], F32)
        cus = A("cus", [128, 514], F32)
        u = A("u", [128, 514], F32)
        tcv = A("tcv", [128, 512], F32)
        vT = A("vT", [128, 4, 512], BF16)
        G = [A(f"G{i}", [128, 512], F32) for i in range(4)]
        gas, gcs, rr, gsb = G[0:2], G[2:4], G[0:2], G[2:4]
        t1 = A("t1", [128, 512], F32)
        t2 = A("t2", [128, 512], F32)
        junk = A.alias("junkC", [128, D], BF16, t1)
        mixT = A("mixT", [128, 8, 512], BF16)
        tmp = [A(f"tmpC{i}", [128, D], F32) for i in range(2)]
        fT = A("fT", [128, 32, 512], BF16)
        fs = A("fs", [128, 4, 512], F32)
        pst = A("pst", [128, 4, PLE], F32)
        pbf = A.alias("pbf", [128, 4, PLE], BF16, cus)
        pT = A.alias("pT", [128, 2, 512], BF16, u)

        seq = []
        for i in range(NSLOT):
            seq += [("in", 3, 4, 0, 8), ("in", 4, 5, 0, 8), ("in", 5, 6, 0, 8)]
            seq += [("in", 6, 7, 0, 8), ("in", 8, 9, 0, 8), ("ao", 0, 2, 0, 4), ("co", 0, 2, 0, 4),
                    ("in", 7, 8, 0, 8), ("in", 9, 10, 0, 8)]
            seq += [("o", 0, 1, 0, 8), ("o", 1, 2, 0, 8)]
            seq += [("up", g, g + 1, 0, 8) for g in range(8)]
            seq += [("down", hf, hf + 1, g * 8, g * 8 + 8) for hf in range(2) for g in range(4)]
            seq += [("pg", 0, 1, 0, 8), ("pg", 1, 2, 0, 8), ("pp", 0, 2, 0, 2)]
        st = dict(lp=0, up=0, rel=0)

        def view(n):
            k, n0, n1, k0, k1 = seq[n]
            nn, kk = n1 - n0, k1 - k0
            return ring[n % R][:, :, :].rearrange("p a c -> p (a c)")[:, 0:nn * kk * 512].rearrange(
                "p (n k c) -> p n k c", n=nn, k=kk)

        def pump():
            while st["lp"] < st["rel"] + R and st["lp"] < len(seq):
                k, n0, n1, k0, k1 = seq[st["lp"]]
                self.load_panel("sp", view(st["lp"]), k, n0, n1, k0, k1, ("ring", st["lp"] % R))
                st["lp"] += 1

        def get_panel():
            n = st["up"]
            pump()
            assert n < st["lp"], "too many live panels"
            st["up"] += 1
            return view(n)

        def done(k=1):
            st["rel"] += k
            pump()

        cnt = dict(k=0, x=0)

        def col():
            cnt["k"] += 1
            return cnt["k"] % 32

        def pre(xblk):
            c = col()
            cnt["x"] += 1
            xb = xn[cnt["x"] % 4]
            self.norm_pre(xblk, 128, ss[:, c:c + 1], rstd[:, c:c + 1], xb, junk)
            return xb

        def post(xb, dst_fn):
            cnt["p"] = cnt.get("p", 0) + 1
            self.norm_post(128, dst_fn, xb, 6 + cnt["p"] % 2, "act" if cnt["p"] % 2 == 0 else "dve")

        def chain(blocks):
            cols = [col() for _ in blocks]
            for (xblk, parts, ssum), c in zip(blocks, cols):
                self.vop("dve", "tensor_scalar", rstd[:, c:c + 1], [ssum], 1024.0 * EPS, None, ALU.add)
            for (xblk, parts, ssum), c in zip(blocks, cols):
                self.vop("pool", "tensor_tensor", rstd[:, c:c + 1], [rstd[:, c:c + 1], self.halfc[:, 0:1]], ALU.pow)
            for (xblk, parts, ssum), c in zip(blocks, cols):
                for ap, c0, w in parts:
                    self.vop("dve", "scalar_tensor_tensor", xblk[:, c0:c0 + w], [ap, rstd[:, c:c + 1], xblk[:, c0:c0 + w]],
                             ALU.mult, ALU.add)
            cols2 = [col() for _ in blocks]
            for (xblk, parts, ssum), c in zip(blocks, cols2):
                self.act(junk[:, :], xblk, AF.Square, accum=ss[:, c:c + 1])
            for (xblk, parts, ssum), c in zip(blocks, cols2):
                self.vop("dve", "tensor_scalar", rstd[:, c:c + 1], [ss[:, c:c + 1]], 1024.0 * EPS, None, ALU.add)
            for (xblk, parts, ssum), c in zip(blocks, cols2):
                self.vop("pool", "tensor_tensor", rstd[:, c:c + 1], [rstd[:, c:c + 1], self.halfc[:, 0:1]], ALU.pow)
            outs = []
            for (xblk, parts, ssum), c in zip(blocks, cols2):
                cnt["x"] += 1
                xb = xn[cnt["x"] % 4]
                self.vop("dve", "tensor_scalar", xb[:, :], [xblk], rstd[:, c:c + 1], 32.0, ALU.mult, ALU.mult)
                outs.append(xb)
            return outs

        def c1_load(i):
            if i >= NSLOT:
                return
            tok = slice(i * 512, (i + 1) * 512)
            self.dma("sp", xs[i % 2][:, :, :], self.x_own[tok, :].rearrange("(t p) d -> p t d", p=128),
                     ("xsC", i % 2))
            self.dma("sp", otb[i % 2][:, :, :], self.ot_s[:, :, tok], ("otb", i % 2))

        c1x = {}

        def c1_pre(i, tb):
            if i < NSLOT:
                c1x[(i, tb)] = pre(xs[i % 2][:, tb, :])

        def c1_post(i, tb):
            if i < NSLOT:
                post(c1x.pop((i, tb)), lambda: hA[i % 2][:, :, tb * 128:(tb + 1) * 128])

        c1_load(0)
        for tb in range(4):
            c1_pre(0, tb)
            c1_post(0, tb)

        for i in range(NSLOT):
            tok = slice(i * 512, (i + 1) * 512)
            X = xs[i % 2]
            hT = hA[i % 2]
            OTi = otb[i % 2]
            c1_load(i + 1)
            self.dma("sp", pst[:, :, :], self.p_own[tok, :].rearrange("(t p) d -> p t d", p=128), ("pst",))
            Wcb = get_panel()[:, 0]
            Wcc = get_panel()[:, 0]
            Wcu = get_panel()[:, 0]
            hb = self.bank(6)
            for ct in range(4):
                base = (ct % 2) * 3
                cs = slice(ct * 128, (ct + 1) * 128)
                for j, Wp in enumerate((Wcb, Wcc, Wcu)):
                    for kc in range(8):
                        self.mm(self.bank(base + j), Wp[:, kc, cs], hT[:, kc, :], kc == 0, kc == 7)
                for j, Wp in enumerate((Wcc, Wcu)):
                    for kc in range(8):
                        self.mm(hb[:, ct * 4 + j * 2:ct * 4 + j * 2 + 2], Wp[:, kc, cs],
                                self.hT_halo[:, kc, 2 * i:2 * i + 2], kc == 0, kc == 7)
                self.act(cus[:, 2:514], self.bank(base + 2), AF.Identity)
                self.act(cus[:, 0:2], hb[:, ct * 4 + 2:ct * 4 + 4], AF.Identity)
                self.vop("dve", "tensor_tensor", u[:, 2:514], [self.bank(base + 1), cus[:, 2:514]], ALU.mult)
                self.vop("dve", "tensor_tensor", u[:, 0:2], [hb[:, ct * 4:ct * 4 + 2], cus[:, 0:2]], ALU.mult)
                w0, w1, w2 = (self.wc[:, ct * 3 + k:ct * 3 + k + 1] for k in range(3))
                self.vop("dve", "tensor_scalar", tcv[:, :], [u[:, 0:512]], w0, None, ALU.mult)
                self.vop("dve", "scalar_tensor_tensor", tcv[:, :], [u[:, 1:513], w1, tcv[:, :]], ALU.mult, ALU.add)
                self.vop("dve", "scalar_tensor_tensor", tcv[:, :], [u[:, 2:514], w2, tcv[:, :]], ALU.mult, ALU.add)
                self.vop("dve", "tensor_tensor", vT[:, ct, :], [self.bank(base), tcv[:, :]], ALU.mult)
            done(3)
            gaP = [None, None]
            gcP = [None, None]
            gaP[0] = get_panel()[:, 0]
            gcP[0] = get_panel()[:, 0]
            aoP = get_panel()
            coP = get_panel()
            for dt in range(8):
                hf = dt // 4
                cs = slice((dt % 4) * 128, (dt % 4) * 128 + 128)
                if dt == 4:
                    done(2)
                    gaP[1] = get_panel()[:, 0]
                    gcP[1] = get_panel()[:, 0]
                base = (dt % 2) * 4
                for kc in range(8):
                    self.mm(self.bank(base + 2), gaP[hf][:, kc, cs], hT[:, kc, :], kc == 0, kc == 7)
                for kc in range(8):
                    self.mm(self.bank(base + 3), gcP[hf][:, kc, cs], hT[:, kc, :], kc == 0, kc == 7)
                for pr in range(4):
                    self.mm(self.bank(base), aoP[:, hf, pr, cs], OTi[:, pr, :], pr == 0, pr == 3)
                for ct in range(4):
                    self.mm(self.bank(base + 1), coP[:, hf, ct, cs], vT[:, ct, :], ct == 0, ct == 3)
                self.act(gas[dt % 2][:, :], self.bank(base + 2), AF.Sigmoid, bias=self.bg[:, dt:dt + 1])
                self.act(gcs[dt % 2][:, :], self.bank(base + 3), AF.Sigmoid, bias=self.bg[:, 8 + dt:9 + dt])
                self.vop("dve", "tensor_tensor", t1[:, :], [self.bank(base), gas[dt % 2][:, :]], ALU.mult)
                self.vop("dve", "tensor_tensor", t2[:, :], [self.bank(base + 1), gcs[dt % 2][:, :]], ALU.mult)
                self.vop("pool", "tensor_tensor", mixT[:, dt, :], [t1[:, :], t2[:, :]], ALU.add)
            done(4)
            oP = [get_panel()[:, 0], get_panel()[:, 0]]
            tA = [tmp[0][:, :], tmp[1][:, :], fs[:, 0:2, :].rearrange("p a c -> p (a c)"), fs[:, 2:4, :].rearrange("p a c -> p (a c)")]
            blocks = []
            for tb in range(4):
                pb = (tb % 3) * 2
                ts_ = slice(tb * 128, (tb + 1) * 128)
                for hf in range(2):
                    for kc in range(8):
                        self.mm(self.bank(pb + hf), mixT[:, kc, ts_], oP[hf][:, kc, :], kc == 0, kc == 7)
                c = col()
                self.act(junk[:, :], self.bank(pb, 2), AF.Square, accum=ss[:, c:c + 1])
                self.vop("dve", "tensor_tensor", tA[tb], [self.bank(pb, 2), self.gpm[:, :]], ALU.mult)
                blocks.append((X[:, tb, :], [(tA[tb], 0, D)], ss[:, c:c + 1]))
                if tb == 1:
                    xb5 = chain(blocks)
                    blocks = []
            xb5 = xb5 + chain(blocks)
            done(2)
            if self.debug and i == 0:
                self.dma("sp", self.dbg_x1.ap().rearrange("(t p) d -> p t d", p=128), X[:, :, :], "dbgx1")
                self.dma("sp", self.dbg_mix.ap(), mixT[:, :, :], "dbgmix")
                self.dma("sp", self.dbg_vT.ap(), vT[:, :, :], "dbgvt")
            self.act(pbf[:, :, :], pst[:, :, :], AF.Identity)
            for tb in range(4):
                pbk = self.Pb[:, 1024 * 7:1024 * 8]
                for kc in range(2):
                    self.tr(pbk[:, kc * 128:(kc + 1) * 128], pbf[:, tb, kc * 128:(kc + 1) * 128], self.ident)
                self.vop("dve", "tensor_copy", pT[:, :, tb * 128:(tb + 1) * 128],
                         [pbk[:, 0:256].rearrange("p (k t) -> p k t", k=2)])
            for tb in range(4):
                post(xb5[tb], lambda tb=tb: hB[:, :, tb * 128:(tb + 1) * 128])
            for g in range(8):
                Wp = get_panel()[:, 0]
                for q in range(4):
                    ft = g * 4 + q
                    bk = self.bank(ft % 6)
                    for kc in range(8):
                        self.mm(bk, Wp[:, kc, q * 128:(q + 1) * 128], hB[:, kc, :], kc == 0, kc == 7)
                    self.act(rr[ft % 2][:, :], bk, AF.Relu)
                    self.vop("dve", "tensor_tensor", fT[:, ft, :], [bk, rr[ft % 2][:, :]], ALU.mult)
                done(1)
                if g < 4:
                    c1_pre(i + 1, g)
                if 1 <= g <= 4:
                    c1_post(i + 1, g - 1)
            blocks6 = []
            for hf in range(2):
                for g in range(4):
                    Wp = get_panel()[:, 0]
                    for tb in range(4):
                        for kc in range(8):
                            self.mm(self.bank(hf * 4 + tb), fT[:, g * 8 + kc, tb * 128:(tb + 1) * 128], Wp[:, kc, :],
                                    g == 0 and kc == 0, g == 3 and kc == 7)
                    done(1)
                for tb in range(4):
                    bk = self.bank(hf * 4 + tb)
                    if hf == 0:
                        self.act(junk[:, 0:512], bk, AF.Square, accum=ssb[:, tb:tb + 1])
                        self.vop("dve", "tensor_tensor", fs[:, tb, :], [bk, self.gpl[:, 0:512]], ALU.mult)
                    else:
                        c = col()
                        self.act(junk[:, 0:512], bk, AF.Square, accum=ss[:, c:c + 1])
                        self.vop("dve", "tensor_tensor", ss[:, c:c + 1], [ss[:, c:c + 1], ssb[:, tb:tb + 1]], ALU.add)
                        self.vop("dve", "tensor_tensor", G[tb][:, :], [bk, self.gpl[:, 512:1024]], ALU.mult)
                        blocks6.append((X[:, tb, :], [(fs[:, tb, :], 0, 512), (G[tb][:, :], 512, 512)], ss[:, c:c + 1]))
            xb6 = chain(blocks6)
            if self.debug and i == 0:
                self.dma("sp", self.dbg_x2.ap().rearrange("(t p) d -> p t d", p=128), X[:, :, :], "dbgx2")
            pgP = [get_panel()[:, 0], get_panel()[:, 0]]
            ppP = get_panel()
            k = 0
            for tb in range(4):
                ts_ = slice(tb * 128, (tb + 1) * 128)
                post(xb6[tb], lambda tb=tb: hB[:, :, tb * 128:(tb + 1) * 128])
                tp = tmp[tb % 2]
                for hf in range(2):
                    gb = self.bank(k % 4)
                    pb_ = self.bank(4 + k % 2)
                    for kc in range(8):
                        self.mm(gb, hB[:, kc, ts_], pgP[hf][:, kc, :], kc == 0, kc == 7)
                    for kc in range(2):
                        self.mm(pb_, pT[:, kc, ts_], ppP[:, hf, kc, :], kc == 0, kc == 1)
                    self.act(gsb[k % 2][:, :], gb, AF.Sigmoid)
                    self.vop("dve", "tensor_tensor", tp[:, hf * 512:(hf + 1) * 512], [pb_, gsb[k % 2][:, :]], ALU.mult)
                    k += 1
                self.vop("pool", "tensor_tensor", X[:, tb, :], [X[:, tb, :], tp[:, :]], ALU.add)
            done(3)
            self.dma("sp", self.out[tok, :].rearrange("(t p) d -> p t d", p=128), X[:, :, :], ("outC", i % 2))
        A.release(m)

    def build(self):
        self.declare()
        with contextlib.ExitStack() as stack:
            self.phase_const()
            first, rest = self.weight_pieces()
            self.phase_A(lambda: self.phase_W1(first))
            self.s.barrier()
            self.start_W2(rest)
            self.phase_B()
            self.step_W2(len(rest))
            self.dma("sp", self.ot_s.ap(), self.OT[:, :, :], "otspill")
            if self.debug:
                self.dma("sp", self.dbg_ot.ap(), self.OT[:, :, :], "dbg0")
                self.dma("sp", self.dbg_qt.ap(), self.QT[:, :, :], "dbg1")
            self.s.barrier()
            self.A.release(self.mAB)
            self.phase_C()
            self.s.barrier()
            self.s.emit(self.nc, stack)
        return self.nc


_CACHE = {}


def _get_nc(S, debug=False):
    key = (S, debug)
    if key not in _CACHE:
        _CACHE[key] = Builder(S, debug).build()
    return _CACHE[key]


def _host_consts():
    cm = np.zeros((128, 3 * 128 + 4), np.float32)
    cm[:, 0:128] = np.eye(128, dtype=np.float32)
    j = np.arange(128)[:, None]
    s = np.arange(128)[None, :]
    cm[:, 128:256] = np.where(j >= s, -1.0, 0.0)
    cm[:, 256:384] = 1.0
    cm[:, 384:386] = 1.0
    cm[1, 386] = 1.0
    cm[33, 386] = 1.0
    return cm


def _host_masks(h):
    m = np.arange(8)[None, :, None]
    s = np.arange(128)[:, None, None]
    q = np.arange(512)[None, None, :]
    mk = np.where(m * 128 + s >= h * 512 + q, NEG, 0.0).astype(np.float32)
    return np.ascontiguousarray(mk.reshape(128, 8 * 512))


def make_in_maps(inputs):
    x = np.asarray(inputs["x"], np.float32)
    p = np.asarray(inputs["p"], np.float32)[0]
    B, S, _ = x.shape
    NCH = S // 512
    NSLOT = NCH // 2
    f = lambda k: np.ascontiguousarray(np.asarray(inputs[k], np.float32)[0])
    gvec = np.concatenate([f(k).reshape(8, 128).T for k in ("g_pre_mix", "g_pre_mlp", "g_ple")], axis=1)
    gpost = np.stack([f("g_post_mix"), f("g_post_mlp")])
    bgate = f("b_gate").reshape(16, 128).T
    wconv = f("w_conv").reshape(3, 4, 128).transpose(2, 1, 0).reshape(128, 12)
    common = {
        "w_in": f("w_in"), "w_attn_out": f("w_attn_out"), "w_conv_out": f("w_conv_out"), "w_o": f("w_o"),
        "w_up": f("w_up"), "w_down": f("w_down"), "w_ple_gate": f("w_ple_gate"), "w_ple_proj": f("w_ple_proj"),
        "gvec": np.ascontiguousarray(gvec), "gpost": np.ascontiguousarray(gpost),
        "bgate": np.ascontiguousarray(bgate), "wconv": np.ascontiguousarray(wconv), "cmat": _host_consts(),
    }
    maps = []
    for core in range(2 * B):
        b, h = core // 2, core % 2
        xc = x[b].reshape(NCH, 512, D)
        halo = np.zeros((16, D), np.float32)
        for i in range(NSLOT):
            c = 2 * i + h
            if c > 0:
                halo[2 * i:2 * i + 2] = xc[c - 1, 510:512]
        mp = dict(common)
        mp["x_seq"] = np.ascontiguousarray(x[b])
        mp["x_own"] = np.ascontiguousarray(xc[h::2].reshape(NSLOT * 512, D))
        mp["x_halo"] = halo
        mp["p_own"] = np.ascontiguousarray(p[b].reshape(NCH, 512, PLE)[h::2].reshape(NSLOT * 512, PLE))
        mp["masks"] = _host_masks(h)
        maps.append(mp)
    return maps


def kernel(**inputs):
    x = np.asarray(inputs["x"])
    B, S, _ = x.shape
    nc = _get_nc(S)
    maps = make_in_maps(inputs)
    res = run_bass_kernel_spmd(nc, maps, core_ids=list(range(2 * B)))
    NCH = S // 512
    out = np.empty((B, NCH, 512, D), np.float32)
    for core in range(2 * B):
        b, h = core // 2, core % 2
        out[b, h::2] = np.asarray(res.results[core]["out"]).reshape(NCH // 2, 512, D)
    return out.reshape(B, S, D)
```

```python
import contextlib
import numpy as np
import concourse.bass as bass
import concourse.mybir as mybir
from concourse.bass_utils import run_bass_kernel_spmd

F32 = mybir.dt.float32
BF16 = mybir.dt.bfloat16
AF = mybir.ActivationFunctionType
ALU = mybir.AluOpType
AX = mybir.AxisListType

DT_SIZE = {F32: 4, BF16: 2}


class Op:
    __slots__ = ("eng", "fn", "dma_key", "deps", "signal", "idx", "sigval", "name")

    def __init__(self, eng, fn, dma_key, name):
        self.eng = eng
        self.fn = fn
        self.dma_key = dma_key
        self.deps = []
        self.signal = False
        self.idx = -1
        self.sigval = 0
        self.name = name


def _footprint(ap):
    t = ap.tensor
    esz = DT_SIZE[t.dtype] if t.dtype in DT_SIZE else mybir.dt.size(t.dtype)
    shape = [int(s) for s in t.shape]
    rowlen = 1
    for s in shape[1:]:
        rowlen *= s
    pat = [[int(a), int(b)] for a, b in ap.ap]
    off = int(ap.offset)
    is_psum = "PSum" in type(t).__name__
    base = 0
    if not is_psum:
        base = int(t.manual_sbuf_range[0])
    pstride, pcount = pat[0]
    assert pstride == rowlen or pcount == 1, (pat, rowlen, t.name)
    p0 = off // rowlen + int(t.base_partition)
    p1 = p0 + pcount
    foff = off % rowlen
    dims = [(s, c) for s, c in pat[1:] if c > 1]
    if not dims:
        runs = [(foff, foff + 1)]
    else:
        dims_sorted = dims
        inner_s, inner_c = dims_sorted[-1]
        if inner_s == 1:
            run = inner_c
            outer = dims_sorted[:-1]
        else:
            run = 1
            outer = dims_sorted
        starts = [foff]
        nruns = 1
        for s, c in outer:
            nruns *= c
        if nruns > 64:
            ext = foff + sum((c - 1) * abs(s) for s, c in dims) + 1
            runs = [(foff, ext)]
        else:
            for s, c in outer:
                starts = [st + k * s for st in starts for k in range(c)]
            runs = sorted((st, st + run) for st in starts)
            merged = []
            for a, b in runs:
                if merged and a <= merged[-1][1]:
                    merged[-1] = (merged[-1][0], max(merged[-1][1], b))
                else:
                    merged.append((a, b))
            runs = merged
    runs = [(base + a * esz, base + b * esz) for a, b in runs]
    if is_psum:
        banks = sorted({b // 2048 for a, e in runs for b in range(a, e, 512)} | {(e - 1) // 2048 for a, e in runs})
        runs = [(bk * 2048, bk * 2048 + 2048) for bk in banks]
        return ("P", 0, 128, runs)
    return ("S", p0, p1, runs)


class Sched:
    COMPUTE = ("pe", "act", "dve", "pool")
    PAGE = 2048

    def __init__(self):
        self.ops = {e: [] for e in ("pe", "act", "dve", "pool", "sp")}
        self.recs = {}
        self.allrecs = []
        self.keyw = {}
        self.keyr = {}
        self.dma_count = {}

    def _pages(self, space, runs):
        pg = set()
        for a, b in runs:
            for p in range(a // self.PAGE, (b - 1) // self.PAGE + 1):
                pg.add((space, p))
        return pg

    @staticmethod
    def _overlap(r, space, p0, p1, runs):
        if r[1] != space or r[3] <= p0 or p1 <= r[2]:
            return False
        for a, b in runs:
            for c, d in r[4]:
                if a < d and c < b:
                    return True
        return False

    @staticmethod
    def _covers(runs_outer, runs_inner):
        for c, d in runs_inner:
            ok = False
            for a, b in runs_outer:
                if a <= c and d <= b:
                    ok = True
                    break
            if not ok:
                return False
        return True

    def _access(self, op, item, is_write):
        if isinstance(item, tuple) and not hasattr(item, "tensor"):
            key = item
            w = self.keyw.get(key)
            if w is not None:
                op.deps.append(w)
            if is_write:
                for r in self.keyr.get(key, ()):
                    op.deps.append(r)
                self.keyw[key] = op
                self.keyr[key] = []
            else:
                self.keyr.setdefault(key, []).append(op)
            return
        space, p0, p1, runs = _footprint(item)
        excl = is_write or space == "P"
        pages = self._pages(space, runs)
        seen = set()
        for pg in pages:
            lst = self.recs.get(pg)
            if not lst:
                continue
            keep = []
            for r in lst:
                if not r[6]:
                    continue
                keep.append(r)
                if id(r) in seen:
                    continue
                seen.add(id(r))
                if not self._overlap(r, space, p0, p1, runs):
                    continue
                if r[0] == "w" or excl:
                    op.deps.append(r[5])
                if excl and r[2] >= p0 and r[3] <= p1 and self._covers(runs, r[4]):
                    r[6] = False
                elif (not excl) and r[0] == "r" and r[5].eng == op.eng and r[5].dma_key is None \
                        and r[2] == p0 and r[3] == p1 and r[4] == runs:
                    r[6] = False
            self.recs[pg] = [r for r in keep if r[6]]
        rec = ["w" if excl else "r", space, p0, p1, runs, op, True]
        for pg in pages:
            self.recs.setdefault(pg, []).append(rec)

    def add(self, eng, fn, reads=(), writes=(), dma_key=None, name=""):
        op = Op(eng, fn, dma_key, name)
        for it in reads:
            self._access(op, it, False)
        for it in writes:
            self._access(op, it, True)
        op.idx = len(self.ops[eng])
        self.ops[eng].append(op)
        if dma_key is not None:
            c = self.dma_count.get(dma_key, 0) + 16
            self.dma_count[dma_key] = c
            op.sigval = c
        return op

    def barrier(self):
        lasts = []
        for e, lst in self.ops.items():
            if e == "sp":
                continue
            real = [o for o in lst if o.fn is not None and o.dma_key is None]
            if real:
                lasts.append(real[-1])
        dm = {}
        for e, lst in self.ops.items():
            for o in lst:
                if o.dma_key is not None:
                    dm[o.dma_key] = o
        for e in self.ops:
            op = Op(e, None, None, "barrier")
            op.deps = list(lasts) + list(dm.values())
            op.idx = len(self.ops[e])
            self.ops[e].append(op)
        self.recs = {}
        self.keyw = {}
        self.keyr = {}

    def finalize(self):
        for e, lst in self.ops.items():
            for op in lst:
                nd = []
                for d in op.deps:
                    if d is op:
                        continue
                    if d.dma_key is None and d.eng == op.eng:
                        if e == "pe" or e == "sp":
                            continue
                    nd.append(d)
                op.deps = nd
                for d in nd:
                    d.signal = True
        self.dma_sems = {}
        for e, lst in self.ops.items():
            cnt = 0
            for op in lst:
                if op.fn is None:
                    continue
                if op.dma_key is not None:
                    pass
                elif op.signal:
                    cnt += 1
                    op.sigval = cnt

    def emit(self, nc, stack):
        self.finalize()
        sems = {}
        for e in self.COMPUTE:
            sems[e] = stack.enter_context(nc.semaphore("s_" + e))
        for n_, k in enumerate(self.dma_count):
            sems[("dma", k)] = stack.enter_context(nc.semaphore("d_%d" % n_))
        block = stack.enter_context(nc.Block())
        sched = self

        def run(engname, eng):
            seen = {}
            for op in sched.ops[engname]:
                need = {}
                for d in op.deps:
                    s = sems[("dma", d.dma_key)] if d.dma_key is not None else sems[d.eng]
                    sk = id(s)
                    v = d.sigval
                    if seen.get(sk, 0) >= v:
                        continue
                    if need.get(sk, (None, 0))[1] < v:
                        need[sk] = (s, v)
                if op.fn is not None and op.dma_key is not None and op.sigval > 16:
                    s = sems[("dma", op.dma_key)]
                    if seen.get(id(s), 0) < op.sigval - 16 and need.get(id(s), (None, 0))[1] < op.sigval - 16:
                        need[id(s)] = (s, op.sigval - 16)
                for sk, (s, v) in need.items():
                    eng.wait_ge(s, v)
                    seen[sk] = v
                if op.fn is None:
                    continue
                ins = op.fn(eng)
                if op.dma_key is not None:
                    ins.then_inc(sems[("dma", op.dma_key)], 16)
                elif op.signal:
                    ins.then_inc(sems[engname], 1)

        @block.tensor
        def _(eng):
            run("pe", eng)

        @block.scalar
        def _(eng):
            run("act", eng)

        @block.vector
        def _(eng):
            run("dve", eng)

        @block.gpsimd
        def _(eng):
            run("pool", eng)

        @block.sync
        def _(eng):
            run("sp", eng)


class Alloc:
    def __init__(self, nc, base=17408, limit=228352):
        self.nc = nc
        self.off = base
        self.limit = limit
        self.n = 0

    def __call__(self, name, shape, dtype):
        size = DT_SIZE[dtype]
        for s in shape[1:]:
            size *= s
        size = (size + 63) // 64 * 64
        assert self.off + size <= self.limit, (name, self.off, size)
        t = self.nc.alloc_sbuf_tensor_at(f"{name}_{self.n}", list(shape), dtype, offset=self.off)
        self.n += 1
        self.off += size
        return t

    def alias(self, name, shape, dtype, base):
        size = DT_SIZE[dtype]
        for v in shape[1:]:
            size *= v
        lo, hi = base.manual_sbuf_range
        assert size <= hi - lo, (name, size, hi - lo)
        t = self.nc.alloc_sbuf_tensor_at(f"{name}_{self.n}", list(shape), dtype, offset=int(lo))
        self.n += 1
        return t

    def mark(self):
        return self.off

    def release(self, m):
        self.off = m


D = 1024
DIN = 5120
DFF = 4096
PLE = 256
EPS = 1e-6
NEG = -30000.0


class Builder:
    def __init__(self, S, debug=False):
        self.S = S
        self.NCH = S // 512
        self.NSLOT = self.NCH // 2
        self.NKB = S // 128
        self.TOWN = self.NSLOT * 512
        self.debug = debug
        self.nc = bass.Bass("TRN2", target_bir_lowering=False)
        self.s = Sched()
        self.A = Alloc(self.nc)
        self.rr = 0

    def mm(self, out, lhsT, rhs, start, stop, skip=False):
        self.s.add("pe", lambda e, o=out, l=lhsT, r=rhs, a=start, b=stop, k=skip:
                   e.matmul(o, l, r, start=a, stop=b, skip_group_check=k),
                   reads=[lhsT, rhs], writes=[out], name="mm")

    def tr(self, out, in_, ident):
        self.s.add("pe", lambda e, o=out, i=in_, d=ident: e.transpose(o, i, d),
                   reads=[in_, ident], writes=[out], name="tr")

    def act(self, out, in_, func, bias=None, scale=None, accum=None, extra_reads=()):
        kw = {}
        rd = [in_] + list(extra_reads)
        wr = [out]
        if bias is not None:
            kw["bias"] = bias
            if not isinstance(bias, float):
                rd.append(bias)
        if scale is not None:
            kw["scale"] = scale
            if not isinstance(scale, float):
                rd.append(scale)
        if accum is not None:
            kw["accum_out"] = accum
            wr.append(accum)
        self.s.add("act", lambda e, o=out, i=in_, f=func, k=kw: e.activation(o, i, f, **k),
                   reads=rd, writes=wr, name="act")

    def vop(self, eng, method, out, ins, *args, extra_writes=(), **kw):
        rd = [a for a in ins if hasattr(a, "tensor")]
        rd += [a for a in args if hasattr(a, "tensor")]
        self.s.add(eng, lambda e, m=method, o=out, i=tuple(ins), a=args, k=kw: getattr(e, m)(o, *i, *a, **k),
                   reads=rd, writes=[out] + list(extra_writes), name=method)

    def dma(self, q, out, in_, key, reads=(), writes=()):
        rd = list(reads)
        wr = list(writes)
        if "DRam" not in type(in_.tensor).__name__:
            rd.append(in_)
        if "DRam" not in type(out.tensor).__name__:
            wr.append(out)
        self.s.add(q, lambda e, o=out, i=in_: e.dma_start(out=o, in_=i), reads=rd, writes=wr,
                   dma_key=(q, key), name="dma")

    def declare(self):
        nc, S, TOWN = self.nc, self.S, self.TOWN
        I = lambda n, sh: nc.dram_tensor(n, sh, F32, kind="ExternalInput")
        self.x_seq = I("x_seq", [S, D])
        self.x_own = I("x_own", [TOWN, D])
        self.x_halo = I("x_halo", [16, D])
        self.p_own = I("p_own", [TOWN, PLE])
        self.w = {
            "in": I("w_in", [D, DIN]), "ao": I("w_attn_out", [512, D]), "co": I("w_conv_out", [512, D]),
            "o": I("w_o", [D, D]), "up": I("w_up", [D, DFF]), "down": I("w_down", [DFF, D]),
            "pg": I("w_ple_gate", [D, D]), "pp": I("w_ple_proj", [PLE, D]),
        }
        self.gvec = I("gvec", [128, 24])
        self.gpost = I("gpost", [2, D])
        self.bgate = I("bgate", [128, 16])
        self.wconv = I("wconv", [128, 12])
        self.cmat = I("cmat", [128, 3 * 128 + 4])
        self.masks = I("masks", [128, 8 * 512])
        self.out = nc.dram_tensor("out", [TOWN, D], F32, kind="ExternalOutput")
        self.ws = {}
        for k, t in self.w.items():
            K, N = int(t.shape[0]), int(t.shape[1])
            self.ws[k] = nc.dram_tensor("ws_" + k, [N // 512, 128, K // 128, 512], BF16, kind="Internal")
        self.kt_s = nc.dram_tensor("kt_s", [4, 128, S], BF16, kind="Internal")
        self.v_s = nc.dram_tensor("v_s", [4, 128, self.NKB, 128], BF16, kind="Internal")
        self.ot_s = nc.dram_tensor("ot_s", [128, 4, TOWN], BF16, kind="Internal")
        self.P = nc.alloc_psum_tensor("P", [128, 4096], F32)
        self.Pb = self.P.bitcast(BF16)
        if self.debug:
            self.dbg_ot = nc.dram_tensor("dbg_ot", [128, 4, TOWN], BF16, kind="ExternalOutput")
            self.dbg_qt = nc.dram_tensor("dbg_qt", [128, 4, TOWN], BF16, kind="ExternalOutput")
            self.dbg_kt = nc.dram_tensor("dbg_kt", [4, 128, S], BF16, kind="ExternalOutput")
            self.dbg_v = nc.dram_tensor("dbg_v", [4, 128, self.NKB, 128], BF16, kind="ExternalOutput")
            self.dbg_x1 = nc.dram_tensor("dbg_x1", [512, D], F32, kind="ExternalOutput")
            self.dbg_x2 = nc.dram_tensor("dbg_x2", [512, D], F32, kind="ExternalOutput")
            self.dbg_mix = nc.dram_tensor("dbg_mix", [128, 8, 512], BF16, kind="ExternalOutput")
            self.dbg_vT = nc.dram_tensor("dbg_vT", [128, 4, 512], BF16, kind="ExternalOutput")

    def bank(self, b, n=1):
        return self.P[:, 512 * b:512 * (b + n)]

    def phase_const(self):
        A = self.A
        TOWN = self.TOWN
        self.cm = A("cm", [128, 3 * 128 + 4], BF16)
        self.ident = self.cm[:, 0:128]
        self.negL = self.cm[:, 128:256]
        self.posones = self.cm[:, 256:384]
        self.onescol = self.cm[:, 384:386]
        self.selc = self.cm[:, 386:387]
        self.gv = A("gv", [128, 24], F32)
        self.bg = A("bg", [128, 16], F32)
        self.wc = A("wc", [128, 12], F32)
        self.gpm = A("gpm", [128, D], F32)
        self.gpl = A("gpl", [128, D], F32)
        self.halfc = A("halfc", [128, 8], F32)
        self.hT_halo = A("hT_halo", [128, 8, 16], BF16)
        self.mAB = A.mark()
        self.OT = A("OT", [128, 4, TOWN], BF16)
        self.QT = A("QT", [128, 4, TOWN], BF16)
        self.maskf = A("maskf", [128, 8, 512], F32)
        m = A.mark()
        st = A("cst", [128, 4096], F32)
        self.dma("sp", st[:, 0:388], self.cmat.ap(), "c0")
        self.vop("dve", "tensor_copy", self.cm[:, :], [st[:, 0:388]])
        self.dma("sp", self.gv[:, :], self.gvec.ap(), "c2")
        self.dma("sp", self.bg[:, :], self.bgate.ap(), "c2")
        self.dma("sp", self.wc[:, :], self.wconv.ap(), "c2")
        gp = self.gpost.ap()
        self.dma("sp", self.gpm[:, :], gp[0:1, :].broadcast_to([128, D]), "c3")
        self.dma("sp", self.gpl[:, :], gp[1:2, :].broadcast_to([128, D]), "c3")
        self.vop("dve", "tensor_scalar", self.gpm[:, :], [self.gpm[:, :]], 32.0, None, ALU.mult)
        self.vop("dve", "tensor_scalar", self.gpl[:, :], [self.gpl[:, :]], 32.0, None, ALU.mult)
        self.vop("pool", "memset", self.halfc[:, :], [], -0.5)
        A.release(m)

    def weight_pieces(self):
        gain_col = {"in": 0, "up": 8, "pg": 16}
        first, rest = [], []
        for k in ("in", "ao", "co", "o", "up", "down", "pg", "pp"):
            w = self.w[k]
            K, N = int(w.shape[0]), int(w.shape[1])
            for rc in range(K // 128):
                for c0 in range(0, N, 2048):
                    cw = min(2048, N - c0)
                    g = gain_col.get(k)
                    (first if (k == "in" and c0 == 0) else rest).append((k, rc, c0, cw, g))
        return first, rest

    def phase_W1(self, pieces):
        A = self.A
        m = A.mark()
        NB = 2
        ot_lo, ot_hi = (int(v) for v in self.OT.manual_sbuf_range)
        stf, stb = [], []
        for i in range(NB):
            if ot_hi - ot_lo >= NB * 12288:
                stf.append(self.nc.alloc_sbuf_tensor_at(f"wst{i}", [128, 2048], F32, offset=ot_lo + i * 12288))
                stb.append(self.nc.alloc_sbuf_tensor_at(f"wsb{i}", [128, 2048], BF16, offset=ot_lo + i * 12288 + 8192))
            else:
                stf.append(A(f"wst{i}", [128, 2048], F32))
                stb.append(A(f"wsb{i}", [128, 2048], BF16))
        for n, (k, rc, c0, cw, g) in enumerate(pieces):
            w = self.w[k]
            T = self.ws[k].ap().rearrange("n p k c -> p n k c")
            b = n % NB
            self.dma("sp", stf[b][:, 0:cw], w[rc * 128:(rc + 1) * 128, c0:c0 + cw], ("wf", b))
            gap = self.gv[:, g + rc:g + rc + 1]
            if n % 2 == 0:
                self.vop("dve", "tensor_scalar", stb[b][:, 0:cw], [stf[b][:, 0:cw]], gap, None, ALU.mult)
            else:
                self.act(stb[b][:, 0:cw], stf[b][:, 0:cw], AF.Identity, scale=gap)
            self.dma("sp", T[:, c0 // 512:(c0 + cw) // 512, rc, :],
                     stb[b][:, 0:cw].rearrange("p (a c) -> p a c", c=512), ("wb", b), writes=[("w1", n)])
        self.w1_keys = [("w1", n) for n in range(len(pieces))]
        A.release(m)

    def start_W2(self, pieces):
        A = self.A
        self.w2 = dict(p=pieces, n=0, stf=[A(f"w2f{i}", [128, 2048], F32) for i in range(2)],
                       stb=[A(f"w2b{i}", [128, 2048], BF16) for i in range(2)])

    def _w2_load(self, n):
        w2 = self.w2
        if n >= len(w2["p"]):
            return
        k, rc, c0, cw, g = w2["p"][n]
        self.dma("pool", w2["stf"][n % 2][:, 0:cw], self.w[k][rc * 128:(rc + 1) * 128, c0:c0 + cw], ("w2f", n % 2))

    def step_W2(self, count=1):
        w2 = self.w2
        for _ in range(count):
            n = w2["n"]
            if n >= len(w2["p"]):
                return
            if n == 0:
                self._w2_load(0)
            self._w2_load(n + 1)
            k, rc, c0, cw, g = w2["p"][n]
            b = n % 2
            src, dst = w2["stf"][b][:, 0:cw], w2["stb"][b][:, 0:cw]
            if g is not None:
                self.vop("pool", "tensor_scalar", dst, [src], self.gv[:, g + rc:g + rc + 1], None, ALU.mult)
            else:
                self.vop("pool", "tensor_copy", dst, [src])
            T = self.ws[k].ap().rearrange("n p k c -> p n k c")
            self.dma("pool", T[:, c0 // 512:(c0 + cw) // 512, rc, :], dst.rearrange("p (a c) -> p a c", c=512),
                     ("w2b", b))
            w2["n"] += 1

    def load_panel(self, q, dst, k, n0, n1, k0, k1, key):
        T = self.ws[k].ap().rearrange("n p k c -> p n k c")
        self.dma(q, dst, T[:, n0:n1, k0:k1, :], key)

    def norm_pre(self, xblk, ntok, ss, rstd, xn, junk, xn_eng="pool"):
        self.act(junk[0:ntok, :], xblk, AF.Square, accum=ss[0:ntok, :])
        self.vop("dve", "tensor_scalar", rstd[0:ntok, :], [ss[0:ntok, :]], 1024.0 * EPS, None, ALU.add)
        self.vop("pool", "tensor_tensor", rstd[0:ntok, :], [rstd[0:ntok, :], self.halfc[0:ntok, 0:1]], ALU.pow)
        self.vop(xn_eng, "tensor_scalar", xn[0:ntok, :], [xblk], rstd[0:ntok, :], 32.0, ALU.mult, ALU.mult)

    def norm_post(self, ntok, hT_dst_fn, xn, pbank, evac_eng):
        pb = self.Pb[:, 1024 * pbank:1024 * (pbank + 1)]
        for d in range(8):
            self.tr(pb[:, d * 128:d * 128 + ntok], xn[0:ntok, d * 128:(d + 1) * 128], self.ident[0:ntok, 0:ntok])
        src = pb.rearrange("p (d t) -> p d t", d=8)[:, :, 0:ntok]
        dst = hT_dst_fn()
        if evac_eng == "act":
            self.act(dst, src, AF.Identity)
        else:
            self.vop(evac_eng, "tensor_copy", dst, [src])

    def norm_T(self, xblk, ntok, hT_dst_fn, ss, rstd, xn, junk, pbank, evac_eng, xn_eng="pool"):
        self.norm_pre(xblk, ntok, ss, rstd, xn, junk, xn_eng)
        self.norm_post(ntok, hT_dst_fn, xn, pbank, evac_eng)

    def phase_A(self, mid):
        A, S, NCH, NSLOT = self.A, self.S, self.NCH, self.NSLOT
        m = A.mark()
        wqkv = A("wqkv", [128, 3, 8, 512], BF16)
        xs = [A(f"xsA{i}", [128, 4, D], F32) for i in range(2)]
        xn = [A(f"xnA{i}", [128, D], BF16) for i in range(4)]
        hT = [A(f"hTA{i}", [128, 8, 512], BF16) for i in range(2)]
        kst = [A(f"kst{i}", [128, 4, 512], BF16) for i in range(2)]
        vst = [A(f"vst{i}", [128, 4, 4, 128], BF16) for i in range(2)]
        junk = A("junkA", [128, D], BF16)
        ss = A("ssA", [128, 8], F32)
        rstd = A("rstdA", [128, 8], F32)
        KT = self.kt_s.ap().rearrange("r p s -> p r s")
        VS = self.v_s.ap().rearrange("r p k d -> p r k d")
        jobs = [("kv", c) for c in range(NCH)] + [("q", i) for i in range(NSLOT)]
        NJ = len(jobs)

        def load(ji):
            if ji >= NJ:
                return
            kind, c = jobs[ji]
            src = self.x_seq if kind == "kv" else self.x_own
            self.dma("sp", xs[ji % 2][:, :, :],
                     src[c * 512:(c + 1) * 512, :].rearrange("(t p) d -> p t d", p=128), ("xsA", ji % 2))

        def pre(ji, tb):
            if ji >= NJ:
                return
            k = (ji * 4 + tb)
            self.norm_pre(xs[ji % 2][:, tb, :], 128, ss[:, k % 8:k % 8 + 1], rstd[:, k % 8:k % 8 + 1], xn[k % 4], junk)

        def post(ji, tb):
            if ji >= NJ:
                return
            k = (ji * 4 + tb)
            self.norm_post(128, lambda: hT[ji % 2][:, :, tb * 128:(tb + 1) * 128], xn[k % 4], 6 + k % 2, "act")

        load(0)
        for tb in range(4):
            pre(0, tb)
            post(0, tb)
        mid()
        load(1)
        self.dma("sp", self.maskf[:, :, :], self.masks.ap().rearrange("p (a b) -> p a b", a=8), "c1")
        T_in = self.ws["in"].ap().rearrange("n p k c -> p n k c")
        self.dma("sp", wqkv[:, :, :, :], T_in[:, 0:3, 0:8, :], "wqkv", reads=self.w1_keys)
        for ji, (kind, c) in enumerate(jobs):
            b = ji % 2
            h = hT[b]

            def weave(g):
                if g < 4:
                    pre(ji + 1, g)
                if 1 <= g <= 4:
                    post(ji + 1, g - 1)
                if g == 4:
                    load(ji + 2)

            if kind == "kv":
                for ft in range(4):
                    pbk = self.bank(ft % 4)
                    for kc in range(8):
                        self.mm(pbk, wqkv[:, 1, kc, ft * 128:(ft + 1) * 128], h[:, kc, :], kc == 0, kc == 7)
                    self.vop("dve", "tensor_copy", kst[b][:, ft, :], [pbk])
                    weave(ft)
                self.dma("sp", KT[:, :, c * 512:(c + 1) * 512], kst[b][:, :, :], ("kst", b))
                for tb in range(4):
                    pbk = self.bank(4 + tb % 2)
                    for kc in range(8):
                        self.mm(pbk, h[:, kc, tb * 128:(tb + 1) * 128], wqkv[:, 2, kc, :], kc == 0, kc == 7)
                    dst = vst[b][:, :, tb, :]
                    srcp = pbk.rearrange("p (r d) -> p r d", r=4)
                    self.vop("dve", "tensor_copy", dst, [srcp])
                    weave(4 + tb)
                self.dma("sp", VS[:, :, 4 * c:4 * c + 4, :], vst[b][:, :, :, :], ("vst", b))
            else:
                for pr in range(4):
                    pbk = self.bank(pr % 4)
                    for kc in range(8):
                        self.mm(pbk, wqkv[:, 0, kc, pr * 128:(pr + 1) * 128], h[:, kc, :], kc == 0, kc == 7)
                    self.vop("dve", "tensor_scalar", self.QT[:, pr, c * 512:(c + 1) * 512], [pbk], 0.125, None,
                             ALU.mult)
                    weave(pr)
                weave(4)
        xh = A("xh", [16, D], F32)
        self.dma("sp", xh[:, :], self.x_halo.ap(), "xh")
        self.norm_T(xh[:, :], 16, lambda: self.hT_halo[:, :, :], ss[:, 0:1], rstd[:, 0:1], xn[0], junk, 6, "act")
        A.release(m)

    def phase_B(self):
        A, S, NSLOT, NKB = self.A, self.S, self.NSLOT, self.NKB
        m = A.mark()
        KTb = [A(f"KTb{i}", [128, S], BF16) for i in range(2)]
        Vb = [A(f"Vb{i}", [128, NKB, 128], BF16) for i in range(2)]
        U = [A(f"U{i}", [128, 1024], F32) for i in range(2)]
        SP = [A(f"SP{i}", [128, 1024], BF16) for i in range(2)]
        Wb = [A(f"Wb{i}", [128, 1024], BF16) for i in range(2)]
        HI = [A(f"HI{i}", [64, 512], BF16) for i in range(2)]
        LO = [A(f"LO{i}", [64, 512], BF16) for i in range(2)]
        CARRY = self.bank(6)
        self.vop("dve", "memset", CARRY, [], 0.0)
        tiles = []
        chain = 0
        for pr in range(4):
            for i in range(NSLOT):
                top = 8 * i + 7
                for kb in range(top, -1, -1):
                    tiles.append(dict(pr=pr, i=i, kb=kb, first=(kb == top), last=(kb == 0),
                                      mask=(kb - 8 * i if kb >= 8 * i else None), chain=chain))
                chain += 1
        N = len(tiles)
        loaded = set()

        def load_kv(pr):
            if pr in loaded or pr >= 4:
                return
            loaded.add(pr)
            self.dma("sp", KTb[pr % 2][:, :], self.kt_s[pr, :, :], ("ktb", pr % 2))
            self.dma("sp", Vb[pr % 2][:, :, :], self.v_s[pr, :, :, :], ("vb", pr % 2))

        load_kv(0)
        load_kv(1)

        def zb(n):
            return (n % 3) * 2

        OB = 7

        def PE1(n):
            t = tiles[n]
            pr, i, kb, mk = t["pr"], t["i"], t["kb"], t["mask"]
            KT = KTb[pr % 2]
            for hh in range(2):
                z = self.bank(zb(n) + hh)
                rows = slice(64 * hh, 64 * hh + 64)
                self.mm(z, KT[rows, kb * 128:(kb + 1) * 128], self.QT[rows, pr, i * 512:(i + 1) * 512],
                        True, True)
            if mk is not None:
                wq = 512 if mk >= 4 else (mk + 1) * 128
                z2 = self.bank(zb(n), 2).rearrange("p (h q) -> p h q", h=2)[:, :, 0:wq]
                mb = self.maskf[:, mk, 0:wq]
                self.vop("dve", "tensor_tensor", z2, [z2, mb.unsqueeze(1).to_broadcast([128, 2, wq])], ALU.add)

        def X1(n):
            self.act(U[n % 2][:, :], self.bank(zb(n), 2), AF.Exp)

        def X2(n):
            self.act(SP[n % 2][:, :], U[n % 2][:, :], AF.Ln, bias=1.0)

        def PE2(n):
            t = tiles[n]
            sp = SP[n % 2]
            last_acc = t["first"]
            for hh in range(2):
                self.mm(self.bank(zb(n) + hh), self.negL, sp[:, 512 * hh:512 * hh + 512], False, last_acc, skip=True)
            if not t["last"]:
                for hh in range(2):
                    r = slice(32 * hh, 32 * hh + 2)
                    self.mm(CARRY[r, :], self.onescol[:, 0:2], sp[:, 512 * hh:512 * hh + 512], t["first"], True,
                            skip=True)
                self.vop("dve", "tensor_copy", HI[n % 2][0:34, :], [CARRY[0:34, :]])
                self.vop("dve", "scalar_tensor_tensor", LO[n % 2][0:34, :],
                         [HI[n % 2][0:34, :], self.selc[0:34, :], CARRY[0:34, :]], ALU.mult, ALU.subtract)
            if not t["first"]:
                hb = (n - 1) % 2
                for hh in range(2):
                    r = slice(32 * hh, 32 * hh + 2)
                    self.mm(self.bank(zb(n) + hh), self.posones[r, :], LO[hb][r, :], False, True, skip=True)

        def X3(n):
            self.act(Wb[n % 2][:, :], self.bank(zb(n), 2), AF.Exp)

        def PE3(n):
            t = tiles[n]
            pr, i, kb = t["pr"], t["i"], t["kb"]
            V = Vb[pr % 2]
            for hh in range(2):
                rows = slice(64 * hh, 64 * hh + 64)
                self.mm(self.bank(OB)[rows, :], V[:, kb, 64 * hh:64 * hh + 64], Wb[n % 2][:, 512 * hh:512 * hh + 512],
                        t["first"], t["last"])
            if t["last"]:
                self.vop("dve", "tensor_copy", self.OT[:, pr, i * 512:(i + 1) * 512], [self.bank(OB)])
                if i == NSLOT - 1:
                    load_kv(pr + 2)

        for w in range(24):
            self.mm(self.bank(OB), self.ident, self.QT[:, 0, 0:512], True, True)
        for j in range(-3, N):
            if j % 8 == 0:
                self.step_W2(1)
            if 0 <= j < N:
                X3(j)
            if 0 <= j + 1 < N:
                PE2(j + 1)
            if 0 <= j + 3 < N:
                PE1(j + 3)
            if 0 <= j < N:
                PE3(j)
            if 0 <= j + 2 < N:
                X1(j + 2)
                X2(j + 2)
        A.release(m)

    def phase_C(self):
        A, NSLOT = self.A, self.NSLOT
        m = A.mark()
        R = 5
        ring = [A(f"ring{i}", [128, 8, 512], BF16) for i in range(R)]
        xs = [A(f"xsC{i}", [128, 4, D], F32) for i in range(2)]
        xn = [A(f"xnC{i}", [128, D], BF16) for i in range(4)]
        hA = [A(f"hA{i}", [128, 8, 512], BF16) for i in range(2)]
        hB = A("hB", [128, 8, 512], BF16)
        otb = [A(f"otb{i}", [128, 4, 512], BF16) for i in range(2)]
        ss = A("ssC", [128, 32], F32)
        ssb = A("ssbC", [128, 8], F32)
        rstd = A("rstdC", [128, 32], F32)
        cus = A("cus", [128, 514], F32)
        u = A("u", [128, 514], F32)
        tcv = A("tcv", [128, 512], F32)
        vT = A("vT", [128, 4, 512], BF16)
        G = [A(f"G{i}", [128, 512], F32) for i in range(4)]
        gas, gcs, rr, gsb = G[0:2], G[2:4], G[0:2], G[2:4]
        t1 = A("t1", [128, 512], F32)
        t2 = A("t2", [128, 512], F32)
        junk = A.alias("junkC", [128, D], BF16, t1)
        mixT = A("mixT", [128, 8, 512], BF16)
        tmp = [A(f"tmpC{i}", [128, D], F32) for i in range(2)]
        fT = A("fT", [128, 32, 512], BF16)
        fs = A("fs", [128, 4, 512], F32)
        pst = A("pst", [128, 4, PLE], F32)
        pbf = A.alias("pbf", [128, 4, PLE], BF16, cus)
        pT = A.alias("pT", [128, 2, 512], BF16, u)

        seq = []
        for i in range(NSLOT):
            seq += [("in", 3, 4, 0, 8), ("in", 4, 5, 0, 8), ("in", 5, 6, 0, 8)]
            seq += [("in", 6, 7, 0, 8), ("in", 8, 9, 0, 8), ("ao", 0, 2, 0, 4), ("co", 0, 2, 0, 4),
                    ("in", 7, 8, 0, 8), ("in", 9, 10, 0, 8)]
            seq += [("o", 0, 1, 0, 8), ("o", 1, 2, 0, 8)]
            seq += [("up", g, g + 1, 0, 8) for g in range(8)]
            seq += [("down", hf, hf + 1, g * 8, g * 8 + 8) for hf in range(2) for g in range(4)]
            seq += [("pg", 0, 1, 0, 8), ("pg", 1, 2, 0, 8), ("pp", 0, 2, 0, 2)]
        st = dict(lp=0, up=0, rel=0)

        def view(n):
            k, n0, n1, k0, k1 = seq[n]
            nn, kk = n1 - n0, k1 - k0
            return ring[n % R][:, :, :].rearrange("p a c -> p (a c)")[:, 0:nn * kk * 512].rearrange(
                "p (n k c) -> p n k c", n=nn, k=kk)

        def pump():
            while st["lp"] < st["rel"] + R and st["lp"] < len(seq):
                k, n0, n1, k0, k1 = seq[st["lp"]]
                self.load_panel("sp", view(st["lp"]), k, n0, n1, k0, k1, ("ring", st["lp"] % R))
                st["lp"] += 1

        def get_panel():
            n = st["up"]
            pump()
            assert n < st["lp"], "too many live panels"
            st["up"] += 1
            return view(n)

        def done(k=1):
            st["rel"] += k
            pump()

        cnt = dict(k=0, x=0)

        def col():
            cnt["k"] += 1
            return cnt["k"] % 32

        def pre(xblk):
            c = col()
            cnt["x"] += 1
            xb = xn[cnt["x"] % 4]
            self.norm_pre(xblk, 128, ss[:, c:c + 1], rstd[:, c:c + 1], xb, junk)
            return xb

        def post(xb, dst_fn):
            cnt["p"] = cnt.get("p", 0) + 1
            self.norm_post(128, dst_fn, xb, 6 + cnt["p"] % 2, "act" if cnt["p"] % 2 == 0 else "dve")

        def chain(blocks):
            cols = [col() for _ in blocks]
            for (xblk, parts, ssum), c in zip(blocks, cols):
                self.vop("dve", "tensor_scalar", rstd[:, c:c + 1], [ssum], 1024.0 * EPS, None, ALU.add)
            for (xblk, parts, ssum), c in zip(blocks, cols):
                self.vop("pool", "tensor_tensor", rstd[:, c:c + 1], [rstd[:, c:c + 1], self.halfc[:, 0:1]], ALU.pow)
            for (xblk, parts, ssum), c in zip(blocks, cols):
                for ap, c0, w in parts:
                    self.vop("dve", "scalar_tensor_tensor", xblk[:, c0:c0 + w], [ap, rstd[:, c:c + 1], xblk[:, c0:c0 + w]],
                             ALU.mult, ALU.add)
            cols2 = [col() for _ in blocks]
            for (xblk, parts, ssum), c in zip(blocks, cols2):
                self.act(junk[:, :], xblk, AF.Square, accum=ss[:, c:c + 1])
            for (xblk, parts, ssum), c in zip(blocks, cols2):
                self.vop("dve", "tensor_scalar", rstd[:, c:c + 1], [ss[:, c:c + 1]], 1024.0 * EPS, None, ALU.add)
            for (xblk, parts, ssum), c in zip(blocks, cols2):
                self.vop("pool", "tensor_tensor", rstd[:, c:c + 1], [rstd[:, c:c + 1], self.halfc[:, 0:1]], ALU.pow)
            outs = []
            for (xblk, parts, ssum), c in zip(blocks, cols2):
                cnt["x"] += 1
                xb = xn[cnt["x"] % 4]
                self.vop("dve", "tensor_scalar", xb[:, :], [xblk], rstd[:, c:c + 1], 32.0, ALU.mult, ALU.mult)
                outs.append(xb)
            return outs

        def c1_load(i):
            if i >= NSLOT:
                return
            tok = slice(i * 512, (i + 1) * 512)
            self.dma("sp", xs[i % 2][:, :, :], self.x_own[tok, :].rearrange("(t p) d -> p t d", p=128),
                     ("xsC", i % 2))
            self.dma("sp", otb[i % 2][:, :, :], self.ot_s[:, :, tok], ("otb", i % 2))

        c1x = {}

        def c1_pre(i, tb):
            if i < NSLOT:
                c1x[(i, tb)] = pre(xs[i % 2][:, tb, :])

        def c1_post(i, tb):
            if i < NSLOT:
                post(c1x.pop((i, tb)), lambda: hA[i % 2][:, :, tb * 128:(tb + 1) * 128])

        c1_load(0)
        for tb in range(4):
            c1_pre(0, tb)
            c1_post(0, tb)

        for i in range(NSLOT):
            tok = slice(i * 512, (i + 1) * 512)
            X = xs[i % 2]
            hT = hA[i % 2]
            OTi = otb[i % 2]
            c1_load(i + 1)
            self.dma("sp", pst[:, :, :], self.p_own[tok, :].rearrange("(t p) d -> p t d", p=128), ("pst",))
            Wcb = get_panel()[:, 0]
            Wcc = get_panel()[:, 0]
            Wcu = get_panel()[:, 0]
            hb = self.bank(6)
            for ct in range(4):
                base = (ct % 2) * 3
                cs = slice(ct * 128, (ct + 1) * 128)
                for j, Wp in enumerate((Wcb, Wcc, Wcu)):
                    for kc in range(8):
                        self.mm(self.bank(base + j), Wp[:, kc, cs], hT[:, kc, :], kc == 0, kc == 7)
                for j, Wp in enumerate((Wcc, Wcu)):
                    for kc in range(8):
                        self.mm(hb[:, ct * 4 + j * 2:ct * 4 + j * 2 + 2], Wp[:, kc, cs],
                                self.hT_halo[:, kc, 2 * i:2 * i + 2], kc == 0, kc == 7)
                self.act(cus[:, 2:514], self.bank(base + 2), AF.Identity)
                self.act(cus[:, 0:2], hb[:, ct * 4 + 2:ct * 4 + 4], AF.Identity)
                self.vop("dve", "tensor_tensor", u[:, 2:514], [self.bank(base + 1), cus[:, 2:514]], ALU.mult)
                self.vop("dve", "tensor_tensor", u[:, 0:2], [hb[:, ct * 4:ct * 4 + 2], cus[:, 0:2]], ALU.mult)
                w0, w1, w2 = (self.wc[:, ct * 3 + k:ct * 3 + k + 1] for k in range(3))
                self.vop("dve", "tensor_scalar", tcv[:, :], [u[:, 0:512]], w0, None, ALU.mult)
                self.vop("dve", "scalar_tensor_tensor", tcv[:, :], [u[:, 1:513], w1, tcv[:, :]], ALU.mult, ALU.add)
                self.vop("dve", "scalar_tensor_tensor", tcv[:, :], [u[:, 2:514], w2, tcv[:, :]], ALU.mult, ALU.add)
                self.vop("dve", "tensor_tensor", vT[:, ct, :], [self.bank(base), tcv[:, :]], ALU.mult)
            done(3)
            gaP = [None, None]
            gcP = [None, None]
            gaP[0] = get_panel()[:, 0]
            gcP[0] = get_panel()[:, 0]
            aoP = get_panel()
            coP = get_panel()
            for dt in range(8):
                hf = dt // 4
                cs = slice((dt % 4) * 128, (dt % 4) * 128 + 128)
                if dt == 4:
                    done(2)
                    gaP[1] = get_panel()[:, 0]
                    gcP[1] = get_panel()[:, 0]
                base = (dt % 2) * 4
                for kc in range(8):
                    self.mm(self.bank(base + 2), gaP[hf][:, kc, cs], hT[:, kc, :], kc == 0, kc == 7)
                for kc in range(8):
                    self.mm(self.bank(base + 3), gcP[hf][:, kc, cs], hT[:, kc, :], kc == 0, kc == 7)
                for pr in range(4):
                    self.mm(self.bank(base), aoP[:, hf, pr, cs], OTi[:, pr, :], pr == 0, pr == 3)
                for ct in range(4):
                    self.mm(self.bank(base + 1), coP[:, hf, ct, cs], vT[:, ct, :], ct == 0, ct == 3)
                self.act(gas[dt % 2][:, :], self.bank(base + 2), AF.Sigmoid, bias=self.bg[:, dt:dt + 1])
                self.act(gcs[dt % 2][:, :], self.bank(base + 3), AF.Sigmoid, bias=self.bg[:, 8 + dt:9 + dt])
                self.vop("dve", "tensor_tensor", t1[:, :], [self.bank(base), gas[dt % 2][:, :]], ALU.mult)
                self.vop("dve", "tensor_tensor", t2[:, :], [self.bank(base + 1), gcs[dt % 2][:, :]], ALU.mult)
                self.vop("pool", "tensor_tensor", mixT[:, dt, :], [t1[:, :], t2[:, :]], ALU.add)
            done(4)
            oP = [get_panel()[:, 0], get_panel()[:, 0]]
            tA = [tmp[0][:, :], tmp[1][:, :], fs[:, 0:2, :].rearrange("p a c -> p (a c)"), fs[:, 2:4, :].rearrange("p a c -> p (a c)")]
            blocks = []
            for tb in range(4):
                pb = (tb % 3) * 2
                ts_ = slice(tb * 128, (tb + 1) * 128)
                for hf in range(2):
                    for kc in range(8):
                        self.mm(self.bank(pb + hf), mixT[:, kc, ts_], oP[hf][:, kc, :], kc == 0, kc == 7)
                c = col()
                self.act(junk[:, :], self.bank(pb, 2), AF.Square, accum=ss[:, c:c + 1])
                self.vop("dve", "tensor_tensor", tA[tb], [self.bank(pb, 2), self.gpm[:, :]], ALU.mult)
                blocks.append((X[:, tb, :], [(tA[tb], 0, D)], ss[:, c:c + 1]))
                if tb == 1:
                    xb5 = chain(blocks)
                    blocks = []
            xb5 = xb5 + chain(blocks)
            done(2)
            if self.debug and i == 0:
                self.dma("sp", self.dbg_x1.ap().rearrange("(t p) d -> p t d", p=128), X[:, :, :], "dbgx1")
                self.dma("sp", self.dbg_mix.ap(), mixT[:, :, :], "dbgmix")
                self.dma("sp", self.dbg_vT.ap(), vT[:, :, :], "dbgvt")
            self.act(pbf[:, :, :], pst[:, :, :], AF.Identity)
            for tb in range(4):
                pbk = self.Pb[:, 1024 * 7:1024 * 8]
                for kc in range(2):
                    self.tr(pbk[:, kc * 128:(kc + 1) * 128], pbf[:, tb, kc * 128:(kc + 1) * 128], self.ident)
                self.vop("dve", "tensor_copy", pT[:, :, tb * 128:(tb + 1) * 128],
                         [pbk[:, 0:256].rearrange("p (k t) -> p k t", k=2)])
            for tb in range(4):
                post(xb5[tb], lambda tb=tb: hB[:, :, tb * 128:(tb + 1) * 128])
            for g in range(8):
                Wp = get_panel()[:, 0]
                for q in range(4):
                    ft = g * 4 + q
                    bk = self.bank(ft % 6)
                    for kc in range(8):
                        self.mm(bk, Wp[:, kc, q * 128:(q + 1) * 128], hB[:, kc, :], kc == 0, kc == 7)
                    self.act(rr[ft % 2][:, :], bk, AF.Relu)
                    self.vop("dve", "tensor_tensor", fT[:, ft, :], [bk, rr[ft % 2][:, :]], ALU.mult)
                done(1)
                if g < 4:
                    c1_pre(i + 1, g)
                if 1 <= g <= 4:
                    c1_post(i + 1, g - 1)
            blocks6 = []
            for hf in range(2):
                for g in range(4):
                    Wp = get_panel()[:, 0]
                    for tb in range(4):
                        for kc in range(8):
                            self.mm(self.bank(hf * 4 + tb), fT[:, g * 8 + kc, tb * 128:(tb + 1) * 128], Wp[:, kc, :],
                                    g == 0 and kc == 0, g == 3 and kc == 7)
                    done(1)
                for tb in range(4):
                    bk = self.bank(hf * 4 + tb)
                    if hf == 0:
                        self.act(junk[:, 0:512], bk, AF.Square, accum=ssb[:, tb:tb + 1])
                        self.vop("dve", "tensor_tensor", fs[:, tb, :], [bk, self.gpl[:, 0:512]], ALU.mult)
                    else:
                        c = col()
                        self.act(junk[:, 0:512], bk, AF.Square, accum=ss[:, c:c + 1])
                        self.vop("dve", "tensor_tensor", ss[:, c:c + 1], [ss[:, c:c + 1], ssb[:, tb:tb + 1]], ALU.add)
                        self.vop("dve", "tensor_tensor", G[tb][:, :], [bk, self.gpl[:, 512:1024]], ALU.mult)
                        blocks6.append((X[:, tb, :], [(fs[:, tb, :], 0, 512), (G[tb][:, :], 512, 512)], ss[:, c:c + 1]))
            xb6 = chain(blocks6)
            if self.debug and i == 0:
                self.dma("sp", self.dbg_x2.ap().rearrange("(t p) d -> p t d", p=128), X[:, :, :], "dbgx2")
            pgP = [get_panel()[:, 0], get_panel()[:, 0]]
            ppP = get_panel()
            k = 0
            for tb in range(4):
                ts_ = slice(tb * 128, (tb + 1) * 128)
                post(xb6[tb], lambda tb=tb: hB[:, :, tb * 128:(tb + 1) * 128])
                tp = tmp[tb % 2]
                for hf in range(2):
                    gb = self.bank(k % 4)
                    pb_ = self.bank(4 + k % 2)
                    for kc in range(8):
                        self.mm(gb, hB[:, kc, ts_], pgP[hf][:, kc, :], kc == 0, kc == 7)
                    for kc in range(2):
                        self.mm(pb_, pT[:, kc, ts_], ppP[:, hf, kc, :], kc == 0, kc == 1)
                    self.act(gsb[k % 2][:, :], gb, AF.Sigmoid)
                    self.vop("dve", "tensor_tensor", tp[:, hf * 512:(hf + 1) * 512], [pb_, gsb[k % 2][:, :]], ALU.mult)
                    k += 1
                self.vop("pool", "tensor_tensor", X[:, tb, :], [X[:, tb, :], tp[:, :]], ALU.add)
            done(3)
            self.dma("sp", self.out[tok, :].rearrange("(t p) d -> p t d", p=128), X[:, :, :], ("outC", i % 2))
        A.release(m)

    def build(self):
        self.declare()
        with contextlib.ExitStack() as stack:
            self.phase_const()
            first, rest = self.weight_pieces()
            self.phase_A(lambda: self.phase_W1(first))
            self.s.barrier()
            self.start_W2(rest)
            self.phase_B()
            self.step_W2(len(rest))
            self.dma("sp", self.ot_s.ap(), self.OT[:, :, :], "otspill")
            if self.debug:
                self.dma("sp", self.dbg_ot.ap(), self.OT[:, :, :], "dbg0")
                self.dma("sp", self.dbg_qt.ap(), self.QT[:, :, :], "dbg1")
            self.s.barrier()
            self.A.release(self.mAB)
            self.phase_C()
            self.s.barrier()
            self.s.emit(self.nc, stack)
        return self.nc


_CACHE = {}


def _get_nc(S, debug=False):
    key = (S, debug)
    if key not in _CACHE:
        _CACHE[key] = Builder(S, debug).build()
    return _CACHE[key]


def _host_consts():
    cm = np.zeros((128, 3 * 128 + 4), np.float32)
    cm[:, 0:128] = np.eye(128, dtype=np.float32)
    j = np.arange(128)[:, None]
    s = np.arange(128)[None, :]
    cm[:, 128:256] = np.where(j >= s, -1.0, 0.0)
    cm[:, 256:384] = 1.0
    cm[:, 384:386] = 1.0
    cm[1, 386] = 1.0
    cm[33, 386] = 1.0
    return cm


def _host_masks(h):
    m = np.arange(8)[None, :, None]
    s = np.arange(128)[:, None, None]
    q = np.arange(512)[None, None, :]
    mk = np.where(m * 128 + s >= h * 512 + q, NEG, 0.0).astype(np.float32)
    return np.ascontiguousarray(mk.reshape(128, 8 * 512))


def make_in_maps(inputs):
    x = np.asarray(inputs["x"], np.float32)
    p = np.asarray(inputs["p"], np.float32)[0]
    B, S, _ = x.shape
    NCH = S // 512
    NSLOT = NCH // 2
    f = lambda k: np.ascontiguousarray(np.asarray(inputs[k], np.float32)[0])
    gvec = np.concatenate([f(k).reshape(8, 128).T for k in ("g_pre_mix", "g_pre_mlp", "g_ple")], axis=1)
    gpost = np.stack([f("g_post_mix"), f("g_post_mlp")])
    bgate = f("b_gate").reshape(16, 128).T
    wconv = f("w_conv").reshape(3, 4, 128).transpose(2, 1, 0).reshape(128, 12)
    common = {
        "w_in": f("w_in"), "w_attn_out": f("w_attn_out"), "w_conv_out": f("w_conv_out"), "w_o": f("w_o"),
        "w_up": f("w_up"), "w_down": f("w_down"), "w_ple_gate": f("w_ple_gate"), "w_ple_proj": f("w_ple_proj"),
        "gvec": np.ascontiguousarray(gvec), "gpost": np.ascontiguousarray(gpost),
        "bgate": np.ascontiguousarray(bgate), "wconv": np.ascontiguousarray(wconv), "cmat": _host_consts(),
    }
    maps = []
    for core in range(2 * B):
        b, h = core // 2, core % 2
        xc = x[b].reshape(NCH, 512, D)
        halo = np.zeros((16, D), np.float32)
        for i in range(NSLOT):
            c = 2 * i + h
            if c > 0:
                halo[2 * i:2 * i + 2] = xc[c - 1, 510:512]
        mp = dict(common)
        mp["x_seq"] = np.ascontiguousarray(x[b])
        mp["x_own"] = np.ascontiguousarray(xc[h::2].reshape(NSLOT * 512, D))
        mp["x_halo"] = halo
        mp["p_own"] = np.ascontiguousarray(p[b].reshape(NCH, 512, PLE)[h::2].reshape(NSLOT * 512, PLE))
        mp["masks"] = _host_masks(h)
        maps.append(mp)
    return maps


def kernel(**inputs):
    x = np.asarray(inputs["x"])
    B, S, _ = x.shape
    nc = _get_nc(S)
    maps = make_in_maps(inputs)
    res = run_bass_kernel_spmd(nc, maps, core_ids=list(range(2 * B)))
    NCH = S // 512
    out = np.empty((B, NCH, 512, D), np.float32)
    for core in range(2 * B):
        b, h = core // 2, core % 2
        out[b, h::2] = np.asarray(res.results[core]["out"]).reshape(NCH // 2, 512, D)
    return out.reshape(B, S, D)
```
